# Optimizing a Trainium2 kernel written in Bass

```python
import functools
import jax, jax.numpy as jnp
from jax import lax
import numpy as np

D_MODEL = 2048
BATCH = 4
SEQ = 2048
DEPTH = 4

GRID_W = 64
CTX_LEN = 256
N_MIXERS = 2
RET_HEADS = D_MODEL // 256
RET_DK = 256
RET_DV = 2 * RET_DK
RET_CHUNK = 128
RET_QK = RET_HEADS * RET_DK
RET_VW = RET_HEADS * RET_DV
RET_IN = 2 * RET_QK + 2 * RET_VW
ROPE_BASE = 10000.0
DN_K_HEADS = D_MODEL // 128
DN_V_HEADS = 2 * DN_K_HEADS
DN_DK = 128
DN_DV = 128
DN_CHUNK = 64
DN_CONV = 5
DN_QK = DN_K_HEADS * DN_DK
DN_VW = DN_V_HEADS * DN_DV
DN_CONV_CH = 2 * DN_QK + DN_VW
DN_IN = DN_CONV_CH + DN_VW + 4 * DN_V_HEADS
FFN_HIDDEN = 5504
N_MOD = 9
ALPHA = (2 * DEPTH) ** 0.25
BETA = (8 * DEPTH) ** -0.25
LN_EPS = 1e-5
NORM_EPS = 1e-6

kernel_name = "hybrid_retention_gated_deltanet_dit"


def _standardize(t):
    tf = t.astype(jnp.float32)
    mu = tf.mean(-1, keepdims=True)
    var = jnp.mean(jnp.square(tf - mu), -1, keepdims=True)
    return (tf - mu) * lax.rsqrt(var + LN_EPS)


def layer_norm(x, g, b):
    return (_standardize(x) * g.astype(jnp.float32) + b.astype(jnp.float32)).astype(x.dtype)


def _rms(t):
    tf = t.astype(jnp.float32)
    return tf * lax.rsqrt(jnp.mean(jnp.square(tf), -1, keepdims=True) + NORM_EPS)


def _l2norm(t):
    tf = t.astype(jnp.float32)
    return tf * lax.rsqrt(jnp.sum(jnp.square(tf), -1, keepdims=True) + NORM_EPS)


def _heads(t, n_heads, d):
    b, l = t.shape[:2]
    return t.reshape(b, l, n_heads, d).transpose(0, 2, 1, 3)


def _merge_heads(t):
    b, h, l, d = t.shape
    return t.transpose(0, 2, 1, 3).reshape(b, l, h * d)


def ada_modulation(cond, w, b):
    m = jax.nn.silu(cond) @ w + b
    return m.reshape(cond.shape[:-1] + (N_MOD, D_MODEL))


def mod_terms(m, j):
    pick = lambda r: m[..., r, :][..., None, :]
    return pick(3 * j), pick(3 * j + 1), pick(3 * j + 2)


def swiglu(h, w_in, w_out):
    gate, up = jnp.split(h @ w_in, 2, axis=-1)
    return (jax.nn.silu(gate) * up) @ w_out


def ffn_step(h, m, j, w_in, w_out, g, b):
    shift, scale, gate = mod_terms(m, j)
    y = swiglu(h * (1 + scale) + shift, w_in, w_out)
    return layer_norm(ALPHA * h + 0.5 * gate * y, g, b)


def axial_rotary(length):
    rows = length // GRID_W
    pos_r = jnp.repeat(jnp.arange(rows), GRID_W).astype(jnp.float32)
    pos_c = jnp.tile(jnp.arange(GRID_W), rows).astype(jnp.float32)
    half = RET_DK // 2
    inv = ROPE_BASE ** (-jnp.arange(0, half, 2, dtype=jnp.float32) / half)
    ang = jnp.concatenate([pos_r[:, None] * inv, pos_c[:, None] * inv], -1)
    return jnp.cos(ang), jnp.sin(ang)


def apply_rotary(t, cos, sin):
    t1, t2 = t[..., 0::2], t[..., 1::2]
    return jnp.stack([t1 * cos - t2 * sin, t1 * sin + t2 * cos], -1).reshape(t.shape)


def short_conv(x, w):
    ch = x.shape[-1]
    return lax.conv_general_dilated(
        x, w[:, None, :].astype(x.dtype), window_strides=(1,),
        padding=[(DN_CONV // 2, DN_CONV // 2)],
        dimension_numbers=('NWC', 'WIO', 'NWC'), feature_group_count=ch)


def retention_scan(log_gamma, q, k, v, s0):
    b, h, l, _ = q.shape
    dv = v.shape[-1]
    n = l // RET_CHUNK
    idx = jnp.arange(RET_CHUNK, dtype=jnp.float32)
    lg = log_gamma.astype(jnp.float32)[:, None]
    rel = idx[:, None] - idx[None, :]
    intra = jnp.where(rel >= 0, jnp.exp(jnp.maximum(rel, 0.0) * lg[:, :, None]), 0.0)
    q_dec = jnp.exp((idx + 1.0) * lg)[..., None]
    k_dec = jnp.exp((RET_CHUNK - 1.0 - idx) * lg)[..., None]
    c_dec = jnp.exp(RET_CHUNK * lg)[:, :, None]
    to_chunks = lambda t: jnp.moveaxis(t.reshape(b, h, n, RET_CHUNK, t.shape[-1]), 2, 0)

    def step(s, inp):
        qc, kc, vc = inp
        scores = jnp.einsum('bhid,bhjd->bhij', qc, kc) * intra
        o = (jnp.einsum('bhij,bhje->bhie', scores, vc)
             + jnp.einsum('bhid,bhde->bhie', qc * q_dec, s))
        s = s * c_dec + jnp.einsum('bhjd,bhje->bhde', kc * k_dec, vc)
        return s, o

    s, o = lax.scan(step, s0, (to_chunks(q), to_chunks(k), to_chunks(v)))
    return jnp.moveaxis(o, 0, 2).reshape(b, h, l, dv), s


def gated_delta_scan(q, k, v, g, beta, s0):
    b, h, l, _ = q.shape
    dv = v.shape[-1]
    c = DN_CHUNK
    n = l // c
    ch = lambda t: t.reshape((b, h, n, c) + t.shape[3:])
    q, k, v, beta = ch(q), ch(k), ch(v), ch(beta)
    gc = jnp.cumsum(ch(g), axis=-1)
    idx = jnp.arange(c)
    lower = idx[:, None] >= idx[None, :]
    strict = idx[:, None] > idx[None, :]
    diff = gc[..., :, None] - gc[..., None, :]
    decay = jnp.where(lower, jnp.exp(jnp.where(lower, diff, 0.0)), 0.0)
    k_beta = k * beta[..., None]
    kk = jnp.einsum('bhnid,bhnjd->bhnij', k_beta, k) * decay
    eye = jnp.eye(c, dtype=jnp.float32)
    a = eye + jnp.where(strict, kk, 0.0)
    t_inv = lax.linalg.triangular_solve(a, jnp.broadcast_to(eye, a.shape), left_side=True,
                                        lower=True, unit_diagonal=True)
    u = jnp.einsum('bhnij,bhnjd->bhnid', t_inv, v * beta[..., None])
    w = jnp.einsum('bhnij,bhnjd->bhnid', t_inv, k_beta * jnp.exp(gc)[..., None])
    qk = jnp.where(lower, jnp.einsum('bhnid,bhnjd->bhnij', q, k) * decay, 0.0)
    q_in = q * jnp.exp(gc)[..., None]
    k_out = k * jnp.exp(gc[..., -1:] - gc)[..., None]
    g_last = jnp.exp(gc[..., -1])
    mv = lambda t: jnp.moveaxis(t, 2, 0)

    def step(s, inp):
        u_c, w_c, qk_c, qin_c, kout_c, gl_c = inp
        v_new = u_c - jnp.einsum('bhid,bhde->bhie', w_c, s)
        o = (jnp.einsum('bhid,bhde->bhie', qin_c, s)
             + jnp.einsum('bhij,bhje->bhie', qk_c, v_new))
        s = s * gl_c[..., None, None] + jnp.einsum('bhjd,bhje->bhde', kout_c, v_new)
        return s, o

    s, o = lax.scan(step, s0, tuple(mv(t) for t in (u, w, qk, q_in, k_out, g_last)))
    return jnp.moveaxis(o, 0, 2).reshape(b, h, l, dv), s


def bidir_with_prefix(scan_f, scan_b, lat_f, lat_b, ctx_f, ctx_b, s0):
    flip = lambda t: jnp.flip(t, axis=2)
    oc_f, sc_f = scan_f(*ctx_f, s0)
    oc_b, sc_b = scan_b(*[flip(t) for t in ctx_b], s0)
    ox_f, _ = scan_f(*lat_f, sc_f)
    ox_b, _ = scan_b(*[flip(t) for t in lat_b], sc_b)
    return ox_f + flip(ox_b), oc_f + flip(oc_b)


def retention_mixer(hx, hc, w_in, log_decay, w_out, cos, sin):
    def project(h, rotate):
        q, k, v, g = jnp.split(h @ w_in, [RET_QK, 2 * RET_QK, 2 * RET_QK + RET_VW], axis=-1)
        q = _heads(q, RET_HEADS, RET_DK).astype(jnp.float32)
        k = _heads(k, RET_HEADS, RET_DK).astype(jnp.float32) * RET_DK ** -0.5
        v = _heads(v, RET_HEADS, RET_DV).astype(jnp.float32)
        if rotate:
            q, k = apply_rotary(q, cos, sin), apply_rotary(k, cos, sin)
        return (q, k, v), g

    lat, gx = project(hx, True)
    ctx_in, gc = project(hc, False)
    scan_f = functools.partial(retention_scan, log_decay[0])
    scan_b = functools.partial(retention_scan, log_decay[1])
    s0 = jnp.zeros((hx.shape[0], RET_HEADS, RET_DK, RET_DV), jnp.float32)
    ox, oc = bidir_with_prefix(scan_f, scan_b, lat, lat, ctx_in, ctx_in, s0)

    def finish(o, g, dtype):
        y = _merge_heads(_standardize(o)) * jax.nn.silu(g.astype(jnp.float32))
        return y.astype(dtype) @ w_out

    return finish(ox, gx, hx.dtype), finish(oc, gc, hc.dtype)


def deltanet_mixer(hx, hc, w_in, conv_w, a_log, dt_bias, norm_w, w_out):
    rep = DN_V_HEADS // DN_K_HEADS

    def gates(b_raw, a_raw, d):
        beta = jax.nn.sigmoid(b_raw)
        g = (-jnp.exp(a_log[d].astype(jnp.float32))[:, None]
             * jax.nn.softplus(a_raw + dt_bias[d].astype(jnp.float32)[:, None]))
        return g, beta

    def project(h):
        qkv, z, ba = jnp.split(h @ w_in, [DN_CONV_CH, DN_CONV_CH + DN_VW], axis=-1)
        qkv = jax.nn.silu(short_conv(qkv, conv_w))
        q, k, v = jnp.split(qkv, [DN_QK, 2 * DN_QK], axis=-1)
        q = jnp.repeat(_l2norm(_heads(q, DN_K_HEADS, DN_DK)), rep, axis=1) * DN_DK ** -0.5
        k = jnp.repeat(_l2norm(_heads(k, DN_K_HEADS, DN_DK)), rep, axis=1)
        v = _heads(v, DN_V_HEADS, DN_DV).astype(jnp.float32)
        ba = ba.astype(jnp.float32).transpose(0, 2, 1)
        b_f, a_f, b_b, a_b = jnp.split(ba, 4, axis=1)
        g_f, be_f = gates(b_f, a_f, 0)
        g_b, be_b = gates(b_b, a_b, 1)
        return (q, k, v, g_f, be_f), (q, k, v, g_b, be_b), z

    lat_f, lat_b, zx = project(hx)
    ctx_f, ctx_b, zc = project(hc)
    s0 = jnp.zeros((hx.shape[0], DN_V_HEADS, DN_DK, DN_DV), jnp.float32)
    ox, oc = bidir_with_prefix(gated_delta_scan, gated_delta_scan, lat_f, lat_b, ctx_f, ctx_b, s0)

    def finish(o, z, dtype):
        y = _merge_heads(_rms(o) * norm_w.astype(jnp.float32)) * jax.nn.silu(z.astype(jnp.float32))
        return y.astype(dtype) @ w_out

    return finish(ox, zx, hx.dtype), finish(oc, zc, hc.dtype)


def setup_inputs(seed: int = 0) -> dict:
    key = jax.random.key(seed)
    ks = jax.random.split(key, 24)
    f32 = jnp.float32
    d = D_MODEL
    n_ret = (DEPTH + N_MIXERS - 1) // N_MIXERS
    n_dn = DEPTH // N_MIXERS
    nrm = lambda k, shape, scale: jax.random.normal(k, shape, f32) * scale
    ret_base = jnp.log(1.0 - 2.0 ** (-5.0 - jnp.arange(RET_HEADS, dtype=f32)))
    dt = jnp.exp(jax.random.uniform(ks[17], (n_dn, 2, DN_V_HEADS), f32,
                                    float(np.log(1e-3)), float(np.log(1e-1))))
    return {
        "x": nrm(ks[0], (BATCH, SEQ, d), 1.0),
        "c": nrm(ks[1], (BATCH, d), 1.0),
        "ctx": nrm(ks[2], (BATCH, CTX_LEN, d), 1.0),
        "c_ctx": nrm(ks[3], (d,), 1.0),
        "mod_w": nrm(ks[4], (DEPTH, d, N_MOD * d), d ** -0.5),
        "mod_b": nrm(ks[5], (DEPTH, N_MOD * d), 0.02),
        "ln_g": 1.0 + nrm(ks[6], (DEPTH, 3, d), 0.02),
        "ln_b": nrm(ks[7], (DEPTH, 3, d), 0.02),
        "ffn_w_in": nrm(ks[8], (DEPTH, 2, d, 2 * FFN_HIDDEN), d ** -0.5),
        "ffn_w_out": nrm(ks[9], (DEPTH, 2, FFN_HIDDEN, d), BETA * FFN_HIDDEN ** -0.5),
        "ret_w_in": nrm(ks[10], (n_ret, d, RET_IN), d ** -0.5),
        "ret_log_decay": ret_base * jnp.exp(nrm(ks[11], (n_ret, 2, RET_HEADS), 0.1)),
        "ret_w_out": nrm(ks[12], (n_ret, RET_VW, d), BETA * RET_VW ** -0.5),
        "dn_w_in": nrm(ks[13], (n_dn, d, DN_IN), d ** -0.5),
        "dn_conv_w": nrm(ks[14], (n_dn, DN_CONV, DN_CONV_CH), DN_CONV ** -0.5),
        "dn_a_log": jnp.log(jax.random.uniform(ks[15], (n_dn, 2, DN_V_HEADS), f32, 1.0, 16.0)),
        "dn_dt_bias": dt + jnp.log(-jnp.expm1(-dt)),
        "dn_norm_w": 1.0 + nrm(ks[16], (n_dn, DN_DV), 0.02),
        "dn_w_out": nrm(ks[18], (n_dn, DN_VW, d), BETA * DN_VW ** -0.5),
    }


def reference(x, c, ctx, c_ctx, mod_w, mod_b, ln_g, ln_b, ffn_w_in, ffn_w_out,
              ret_w_in, ret_log_decay, ret_w_out,
              dn_w_in, dn_conv_w, dn_a_log, dn_dt_bias, dn_norm_w, dn_w_out):
    cos, sin = axial_rotary(x.shape[1])
    for i in range(DEPTH):
        j = i // N_MIXERS
        m_x = ada_modulation(c, mod_w[i], mod_b[i])
        m_c = ada_modulation(c_ctx, mod_w[i], mod_b[i])
        x = ffn_step(x, m_x, 0, ffn_w_in[i, 0], ffn_w_out[i, 0], ln_g[i, 0], ln_b[i, 0])
        ctx = ffn_step(ctx, m_c, 0, ffn_w_in[i, 0], ffn_w_out[i, 0], ln_g[i, 0], ln_b[i, 0])
        sx, scx, gx = mod_terms(m_x, 1)
        sc, scc, gc = mod_terms(m_c, 1)
        hx = x * (1 + scx) + sx
        hc = ctx * (1 + scc) + sc
        if i % N_MIXERS == 0:
            ox, oc = retention_mixer(hx, hc, ret_w_in[j], ret_log_decay[j], ret_w_out[j], cos, sin)
        else:
            ox, oc = deltanet_mixer(hx, hc, dn_w_in[j], dn_conv_w[j], dn_a_log[j], dn_dt_bias[j],
                                    dn_norm_w[j], dn_w_out[j])
        x = layer_norm(ALPHA * x + gx * ox, ln_g[i, 1], ln_b[i, 1])
        x = ffn_step(x, m_x, 2, ffn_w_in[i, 1], ffn_w_out[i, 1], ln_g[i, 2], ln_b[i, 2])
        if i < DEPTH - 1:
            ctx = layer_norm(ALPHA * ctx + gc * oc, ln_g[i, 1], ln_b[i, 1])
            ctx = ffn_step(ctx, m_c, 2, ffn_w_in[i, 1], ffn_w_out[i, 1], ln_g[i, 2], ln_b[i, 2])
    return x
```

```python
import contextlib
import numpy as np
import concourse.bass as bass
import concourse.mybir as mybir
from concourse.bass_utils import run_bass_kernel_spmd

F32 = mybir.dt.float32
BF16 = mybir.dt.bfloat16
AF = mybir.ActivationFunctionType
ALU = mybir.AluOpType

D = 2048
KC = 16
SEQ = 2048
CTX = 256
NT = SEQ + CTX
DEPTH = 4
FH = 5504
FJ = 43
ALPHA = float((2 * DEPTH) ** 0.25)
LN_EPS = 1e-5
TT = [(0, 512), (512, 512), (1024, 512), (1536, 512), (2048, 256)]
RH = 8
RET_EPS = LN_EPS * 256.0
NH = 32
DN_EPS = 1e-6 * 128.0
L2_EPS = 1e-6


LAST_COUNTS = {}


class Buf:
    __slots__ = ("name", "w", "r", "dsem", "dcount")

    def __init__(self, name):
        self.name = name
        self.w = None
        self.r = {}
        self.dsem = None
        self.dcount = 0


class KB:
    def __init__(self, nc, es):
        self.nc = nc
        self.es = es
        self.eng = {"pe": nc.tensor, "dve": nc.vector, "act": nc.scalar, "pool": nc.gpsimd, "sp": nc.sync}
        self.sem = {}
        self.cnt = {}
        self.seen = {}
        for e in self.eng:
            self.sem[e] = es.enter_context(nc.semaphore("sem_" + e))
            self.cnt[e] = 0
            self.seen[e] = {}
        self.nsem = len(self.eng)
        self.uid = 0
        self.dbufs = []
        self.ges = es
        self.eps_tab = {}
        self.eps_t = es.enter_context(nc.sbuf_tensor("s_epsconst", [128, 8], F32))
        self.eps_b = Buf("epsconst")

    def sb(self, name, shape, dt):
        self.uid += 1
        return self.es.enter_context(self.nc.sbuf_tensor("s_%s_%d" % (name, self.uid), list(shape), dt))

    def ps(self, name, shape, dt=F32):
        return self.es.enter_context(self.nc.psum_tensor("p_" + name, list(shape), dt))

    def dram(self, name, shape, dt):
        return self.nc.dram_tensor("d_" + name, list(shape), dt, kind="Internal").ap()

    def buf(self, name="b"):
        self.uid += 1
        return Buf("%s_%d" % (name, self.uid))

    def _need(self, eng, reads, writes):
        need = {}

        def add(ev):
            if ev is None:
                return
            if ev[0] == "c":
                if ev[1] == eng and eng == "pe":
                    return
                key = ("c", ev[1])
                val = ev[2]
                sem = self.sem[ev[1]]
            else:
                b = ev[1]
                key = ("d", id(b))
                val = b.dcount
                sem = b.dsem
            if key not in need or need[key][1] < val:
                need[key] = (sem, val)

        for b in reads:
            add(b.w)
        for b in writes:
            add(b.w)
            for ev in b.r.values():
                add(ev)
        e = self.eng[eng]
        seen = self.seen[eng]
        for key, (sem, val) in need.items():
            if seen.get(key, 0) < val:
                e.wait_ge(sem, val)
                seen[key] = val

    def op(self, eng, fn, reads=(), writes=()):
        self._need(eng, reads, writes)
        ins = fn(self.eng[eng])
        self.cnt[eng] += 1
        ins.then_inc(self.sem[eng], 1)
        ev = ("c", eng, self.cnt[eng])
        for b in reads:
            b.r[eng] = ev
        for b in writes:
            b.w = ev
            b.r = {}
        return ins

    def dma(self, q, out, in_, reads, dst):
        self._need(q, reads, [dst])
        if dst.dsem is None:
            dst.dsem = self.ges.enter_context(self.nc.semaphore("ds_" + dst.name))
            self.dbufs.append(dst)
            self.nsem += 1
            assert self.nsem < 240, "too many semaphores"
        ins = self.eng[q].dma_start(out=out, in_=in_)
        dst.dcount += 16
        ins.then_inc(dst.dsem, 16)
        ev = ("d", dst)
        for b in reads:
            b.r[("d", id(dst))] = ev
        dst.w = ev
        dst.r = {}
        return ins

    def finish(self, eng, bufs):
        self._need(eng, bufs, [])

    def barrier(self):
        for e in self.eng:
            for e2 in self.eng:
                if e2 == e or self.cnt[e2] == 0:
                    continue
                key = ("c", e2)
                if self.seen[e].get(key, 0) < self.cnt[e2]:
                    self.eng[e].wait_ge(self.sem[e2], self.cnt[e2])
                    self.seen[e][key] = self.cnt[e2]
            for b in self.dbufs:
                key = ("d", id(b))
                if self.seen[e].get(key, 0) < b.dcount:
                    self.eng[e].wait_ge(b.dsem, b.dcount)
                    self.seen[e][key] = b.dcount

    @contextlib.contextmanager
    def phase(self):
        old = self.es
        with contextlib.ExitStack() as pes:
            self.es = pes
            try:
                yield
            finally:
                self.barrier()
                self.es = old

    def mm(self, out, lhsT, rhs, start, stop, reads, writes):
        return self.op("pe", lambda e: e.matmul(out, lhsT=lhsT, rhs=rhs, start=start, stop=stop), reads, writes)

    def tr(self, out, in_, ident, reads, writes):
        return self.op("pe", lambda e: e.transpose(out, in_, ident), reads, writes)

    def ts(self, eng, out, in0, s1, s2, op0, op1, reads, writes):
        if op1 is None:
            return self.op(eng, lambda e: e.tensor_scalar(out=out, in0=in0, scalar1=s1, scalar2=None, op0=op0), reads, writes)
        return self.op(eng, lambda e: e.tensor_scalar(out=out, in0=in0, scalar1=s1, scalar2=s2, op0=op0, op1=op1), reads, writes)

    def tt(self, eng, out, in0, in1, op, reads, writes):
        return self.op(eng, lambda e: e.tensor_tensor(out=out, in0=in0, in1=in1, op=op), reads, writes)

    def stt(self, out, in0, scalar, in1, op0, op1, reads, writes):
        return self.op("dve", lambda e: e.scalar_tensor_tensor(out=out, in0=in0, scalar=scalar, in1=in1, op0=op0, op1=op1), reads, writes)

    def act(self, out, in_, func, reads, writes, bias=None, scale=None):
        kw = {}
        if bias is not None:
            kw["bias"] = bias
        if scale is not None:
            kw["scale"] = scale
        return self.op("act", lambda e: e.activation(out=out, in_=in_, func=func, **kw), reads, writes)

    def rsqrt(self, out, in_, eps, reads, writes):
        b = self.eps_ap(eps)
        self.op("act", lambda e: e.activation(out=out, in_=in_, func=AF.Sqrt, bias=b[0:out.shape[0], :]), list(reads) + [self.eps_b], writes)
        self.op("dve", lambda e: e.reciprocal(out=out, in_=out), writes, writes)

    def eps_ap(self, eps):
        key = float(eps)
        if key not in self.eps_tab:
            i = len(self.eps_tab)
            self.memset("dve", self.eps_t[:, i:i + 1], key, [self.eps_b])
            self.eps_tab[key] = i
        i = self.eps_tab[key]
        return self.eps_t[:, i:i + 1]

    def copy(self, eng, out, in_, reads, writes):
        if eng == "act":
            return self.op("act", lambda e: e.activation(out=out, in_=in_, func=AF.Identity), reads, writes)
        return self.op(eng, lambda e: e.tensor_copy(out=out, in_=in_), reads, writes)

    def memset(self, eng, ap, val, writes):
        return self.op(eng, lambda e: e.memset(ap, val), [], writes)


class Rot:
    def __init__(self, k, name, n):
        self.bufs = [k.buf(name) for _ in range(n)]
        self.i = 0
        self.n = n

    def next(self):
        j = self.i % self.n
        self.i += 1
        return j, self.bufs[j]


def build_program(depth=DEPTH, mixers=("ret", "dn", "ret", "dn"), do_mixer=True):
    nc = bass.Bass("TRN2", target_bir_lowering=False)
    es = contextlib.ExitStack()
    with es:
        k = KB(nc, es)
        _emit(nc, k, depth, mixers, do_mixer)
        global LAST_COUNTS
        LAST_COUNTS = dict(k.cnt)
    return nc


def _emit(nc, k, depth, mixers, do_mixer):
    def din(name, shape, dt=F32):
        return nc.dram_tensor(name, list(shape), dt, kind="ExternalInput").ap()

    n_ret = sum(1 for m in mixers[:depth] if m == "ret")
    n_dn = sum(1 for m in mixers[:depth] if m == "dn")
    xT_in = din("xT", [D, NT])
    cc_in = din("cc", [128, KC, 2])
    modw = din("mod_w", [depth, 36, 128, KC, 512])
    modb = din("mod_b", [depth, 128, 144])
    lng = din("ln_g", [128, depth * 3 * KC])
    lnb = din("ln_b", [128, depth * 3 * KC])
    wi = din("ffn_w_in", [depth * 2, FJ, 128, 2, KC, 128])
    wo = din("ffn_w_out", [depth * 2, KC, 128, FJ, 128])
    ident_in = din("ident", [128, 128])
    outT = nc.dram_tensor("outT", [D, SEQ], F32, kind="ExternalOutput").ap()
    if n_ret and do_mixer:
        r_wqk = din("ret_wqk", [n_ret, 2, RH, 2, 128, KC, 128])
        r_wvg = din("ret_wvg", [n_ret, 2, RH, 128, KC, 512])
        r_wo = din("ret_wo", [n_ret, KC, 128, 32, 128])
        r_lg = din("ret_lg", [n_ret, 128, 2 * RH])
        cos_in = din("cosT", [128, SEQ])
        sin_in = din("sinT", [128, SEQ])
        tri_in = din("tri", [128, 2, 128])
        idx_in = din("idx", [128, 4])
    if n_dn and do_mixer:
        d_w = din("dn_w", [n_dn, 96, 128, KC, 128])
        d_wba = din("dn_wba", [n_dn, 4, 128, KC, 128])
        d_cw = din("dn_cw", [n_dn, 128, 64, 5])
        d_gp = din("dn_gp", [n_dn, 32, 2, 2])
        d_nw = din("dn_nw", [n_dn, 128, 128])
        d_wo = din("dn_wo", [n_dn, KC, 128, 32, 128])
        dn_sel = din("dn_sel", [32, 32, 128])
        dn_mneg = din("dn_mneg", [128, 4, 128])
        dn_strict = din("dn_strict", [128, 2, 128])
        dn_bd = din("dn_bd", [128, 128])
        dn_lm = din("dn_lm", [128, 2, 3, 128])
    xT = k.dram("xT_s", [D, NT], F32)
    xT_b = [k.buf("xT%d" % t) for t in range(len(TT))]
    ident = k.sb("ident", [128, 128], F32)
    identb = k.sb("identb", [128, 128], BF16)
    ones = k.sb("ones", [128, 128], F32)
    cst = k.buf("cst")
    k.dma("sp", ident[:], ident_in[:, :], [], cst)
    k.copy("dve", identb[:], ident[:], [cst], [cst])
    k.memset("dve", ones[:], 1.0, [cst])
    lng_t = k.sb("lng", [128, depth * 3 * KC], F32)
    lnb_t = k.sb("lnb", [128, depth * 3 * KC], F32)
    k.dma("sp", lng_t[:], lng[:, :], [], cst)
    k.dma("sp", lnb_t[:], lnb[:, :], [], cst)
    cc32 = k.sb("cc32", [128, KC, 2], F32)
    ccb = k.sb("ccb", [128, KC, 2], BF16)
    k.dma("sp", cc32[:], cc_in[:, :, :], [], cst)
    k.act(ccb[:], cc32[:], AF.Silu, [cst], [cst])

    TW = 512
    xt32 = k.sb("xt32", [128, KC, TW], F32)
    xt_b = k.buf("xt32")
    tmpA = k.sb("tmpA", [128, 2, TW], F32)
    tmpA_r = Rot(k, "tmpA", 2)
    tmpD = k.sb("tmpD", [128, 4, TW], F32)
    tmpD_r = Rot(k, "tmpD", 4)
    stat = k.sb("stat", [128, 4, TW], F32)
    stat_b = k.buf("stat")
    wslot = k.sb("wslot", [128, 3, 2 * KC * 128], BF16)
    wslot_r = Rot(k, "wslot", 3)
    modt = k.sb("modt", [128, 144, 2], F32)
    mod_b = k.buf("modt")
    modbias = k.sb("modbias", [128, 144], F32)
    psA = [k.ps("psA%d" % i, [128, 512]) for i in range(7)]
    psA_b = [k.buf("psA%d" % i) for i in range(7)]
    psB = k.ps("psB", [128, 1024], BF16)
    _pb = k.buf("psB")
    psB_b = [_pb, _pb]

    class PsRot:
        def __init__(self, idxs):
            self.idxs = idxs
            self.i = 0

        def next(self):
            j = self.idxs[self.i % len(self.idxs)]
            self.i += 1
            return psA[j], psA_b[j]

    ps_main = PsRot([0, 1, 2, 3])
    ps_y = PsRot([4, 5])
    ps_st = PsRot([6, 4, 5])

    for t, (t0, tw) in enumerate(TT):
        k.dma("sp", xT[:, t0:t0 + tw], xT_in[:, t0:t0 + tw], [], xT_b[t])

    def mod_col(r, kc, which):
        return modt[:, r * KC + kc, which:which + 1]

    def modulation(layer):
        k.dma("sp", modbias[:], modb[layer, :, :], [], mod_b)
        for nb in range(36):
            for half in range(2):
                s, sb_ = wslot_r.next()
                wv = wslot[:, s, :].rearrange("p (kc c) -> p kc c", kc=KC)
                k.dma("pool", wv, modw[layer, nb, :, :, half * 256:(half + 1) * 256], [], sb_)
                for cch in range(2):
                    j = nb * 4 + half * 2 + cch
                    pt, pb = ps_st.next()
                    for kc in range(KC):
                        k.mm(pt[:, 0:2], wv[:, kc, cch * 128:(cch + 1) * 128], ccb[:, kc, :], kc == 0, kc == KC - 1, [sb_, cst], [pb])
                    r = j // KC
                    if r in (1, 4, 7):
                        k.ts("dve", modt[:, j, :], pt[:, 0:2], modbias[:, j:j + 1], 1.0, ALU.add, ALU.add, [pb, mod_b], [mod_b])
                    elif r in (2, 8):
                        k.ts("dve", modt[:, j, :], pt[:, 0:2], modbias[:, j:j + 1], 0.5, ALU.add, ALU.mult, [pb, mod_b], [mod_b])
                    else:
                        k.ts("dve", modt[:, j, :], pt[:, 0:2], modbias[:, j:j + 1], None, ALU.add, None, [pb, mod_b], [mod_b])

    def load_x(t):
        t0, tw = TT[t]
        k.dma("sp", xt32[:, :, 0:tw], xT[:, t0:t0 + tw].rearrange("(kc p) t -> p kc t", p=128), [xT_b[t]], xt_b)

    def store_x(t, final=False):
        t0, tw = TT[t]
        if final:
            k.dma("sp", outT[:, t0:t0 + tw].rearrange("(kc p) t -> p kc t", p=128), xt32[:, :, 0:tw], [xt_b], out_b)
        else:
            k.dma("sp", xT[:, t0:t0 + tw].rearrange("(kc p) t -> p kc t", p=128), xt32[:, :, 0:tw], [xt_b], xT_b[t])

    def modulate(t, r_shift, which, dst, dst_b, col0=0):
        t0, tw = TT[t]
        for kc in range(KC):
            eng = "dve" if kc % 2 == 0 else "pool"
            k.ts(eng, dst[:, kc, col0:col0 + tw], xt32[:, kc, 0:tw], mod_col(r_shift + 1, kc, which), mod_col(r_shift, kc, which),
                 ALU.mult, ALU.add, [xt_b, mod_b], [dst_b])

    def layer_norm(t, lnidx):
        t0, tw = TT[t]
        pm, pmb = ps_st.next()
        pq, pqb = ps_st.next()
        for kc in range(KC):
            k.mm(pm[:, 0:tw], ones[:], xt32[:, kc, 0:tw], kc == 0, kc == KC - 1, [xt_b, cst], [pmb])
        for kc in range(KC):
            s, sb_ = tmpA_r.next()
            k.act(tmpA[:, s, 0:tw], xt32[:, kc, 0:tw], AF.Square, [xt_b], [sb_])
            k.mm(pq[:, 0:tw], ones[:], tmpA[:, s, 0:tw], kc == 0, kc == KC - 1, [sb_, cst], [pqb])
        mean = stat[:, 0, 0:tw]
        var = stat[:, 1, 0:tw]
        rstd = stat[:, 2, 0:tw]
        msq = stat[:, 3, 0:tw]
        k.ts("dve", mean, pm[:, 0:tw], 1.0 / D, None, ALU.mult, None, [pmb], [stat_b])
        k.tt("dve", msq, mean, mean, ALU.mult, [stat_b], [stat_b])
        k.stt(var, pq[:, 0:tw], 1.0 / D, msq, ALU.mult, ALU.subtract, [pqb, stat_b], [stat_b])
        k.rsqrt(rstd, var, LN_EPS, [stat_b], [stat_b])
        for kc in range(KC):
            col = lnidx * KC + kc
            e1 = "dve" if kc % 2 == 0 else "pool"
            k.tt(e1, xt32[:, kc, 0:tw], xt32[:, kc, 0:tw], mean, ALU.subtract, [stat_b, xt_b], [xt_b])
            k.tt(e1, xt32[:, kc, 0:tw], xt32[:, kc, 0:tw], rstd, ALU.mult, [stat_b, xt_b], [xt_b])
            k.act(xt32[:, kc, 0:tw], xt32[:, kc, 0:tw], AF.Identity, [xt_b, cst], [xt_b],
                  bias=lnb_t[:, col:col + 1], scale=lng_t[:, col:col + 1])

    def ffn_phase(layer, f, rbase, lnidx, tiles, final=False):
        with k.phase():
            hT = k.sb("hT", [128, KC, TW], BF16)
            hT_b = k.buf("hT")
            hid = k.sb("hid", [128, FJ, TW], BF16)
            hid_b = [k.buf("hid%d" % j) for j in range(FJ)]
            wo_slot = k.sb("woslot", [128, 2, FJ * 128], BF16)
            wo_r = Rot(k, "woslot", 2)
            wl = layer * 2 + f
            for t in tiles:
                which = 1 if t == 4 else 0
                t0, tw = TT[t]
                load_x(t)
                modulate(t, rbase, which, hT, hT_b)
                for j in range(FJ):
                    s_, sb_ = wslot_r.next()
                    wv = wslot[:, s_, :].rearrange("p (g kc c) -> p g kc c", g=2, kc=KC)
                    k.dma("pool", wv, wi[wl, j, :, :, :, :], [], sb_)
                    pg, pgb = ps_main.next()
                    pu, pub = ps_main.next()
                    for kc in range(KC):
                        k.mm(pg[:, 0:tw], wv[:, 0, kc, :], hT[:, kc, 0:tw], kc == 0, kc == KC - 1, [sb_, hT_b], [pgb])
                    for kc in range(KC):
                        k.mm(pu[:, 0:tw], wv[:, 1, kc, :], hT[:, kc, 0:tw], kc == 0, kc == KC - 1, [sb_, hT_b], [pub])
                    a_, ab = tmpA_r.next()
                    k.act(tmpA[:, a_, 0:tw], pg[:, 0:tw], AF.Silu, [pgb], [ab])
                    k.tt("dve", hid[:, j, 0:tw], tmpA[:, a_, 0:tw], pu[:, 0:tw], ALU.mult, [ab, pub], [hid_b[j]])
                for n in range(KC):
                    s_, sb_ = wo_r.next()
                    wv = wo_slot[:, s_, :].rearrange("p (j c) -> p j c", j=FJ)
                    k.dma("pool", wv, wo[wl, n, :, :, :], [], sb_)
                    py, pyb = ps_y.next()
                    for j in range(FJ):
                        k.mm(py[:, 0:tw], wv[:, j, :], hid[:, j, 0:tw], j == 0, j == FJ - 1, [sb_, hid_b[j]], [pyb])
                    d_, db = tmpD_r.next()
                    k.ts("dve", tmpD[:, d_, 0:tw], py[:, 0:tw], mod_col(rbase + 2, n, which), None, ALU.mult, None, [pyb, mod_b], [db])
                    k.stt(xt32[:, n, 0:tw], xt32[:, n, 0:tw], ALPHA, tmpD[:, d_, 0:tw], ALU.mult, ALU.add, [db, xt_b], [xt_b])
                layer_norm(t, lnidx)
                store_x(t, final)

    def out_proj(layer, w_ap, yT_s, yT_b, last):
        M, A_ = ALU.mult, ALU.add
        with k.phase():
            yTt = k.sb("yTt", [128, 32, TW], BF16)
            yTt_b = k.buf("yTt")
            for t in ([0, 1, 2, 3] if last else [0, 1, 2, 3, 4]):
                which = 1 if t == 4 else 0
                t0, tw = TT[t]
                load_x(t)
                k.dma("sp", yTt[:, :, 0:tw], yT_s[:, :, t0:t0 + tw].rearrange("e p t -> p e t"), [yT_b], yTt_b)
                for n in range(KC):
                    s_, sb_ = wslot_r.next()
                    wv = wslot[:, s_, :].rearrange("p (e c) -> p e c", e=32)
                    k.dma("pool", wv, w_ap[n, :, :, :], [], sb_)
                    py, pyb = ps_y.next()
                    for ec in range(32):
                        k.mm(py[:, 0:tw], wv[:, ec, :], yTt[:, ec, 0:tw], ec == 0, ec == 31, [sb_, yTt_b], [pyb])
                    d_, db = tmpD_r.next()
                    k.ts("dve", tmpD[:, d_, 0:tw], py[:, 0:tw], mod_col(5, n, which), None, M, None, [pyb, mod_b], [db])
                    k.stt(xt32[:, n, 0:tw], xt32[:, n, 0:tw], ALPHA, tmpD[:, d_, 0:tw], M, A_, [db, xt_b], [xt_b])
                layer_norm(t, layer * 3 + 1)
                store_x(t)

    def retention(layer, ri, last):
        NTL = 18
        M, A_, SB = ALU.mult, ALU.add, ALU.subtract
        qT_s = k.dram("r_qT%d" % ri, [RH, 2, 128, NT], BF16)
        kT_s = k.dram("r_kT%d" % ri, [RH, 2, 128, NT], BF16)
        v_s = k.dram("r_v%d" % ri, [RH, NTL, 128, 512], BF16)
        g_s = k.dram("r_g%d" % ri, [RH, NTL, 128, 512], BF16)
        yT_s = k.dram("r_yT%d" % ri, [32, 128, NT], BF16)
        hd_b = [k.buf("rhd%d_%d" % (ri, h)) for h in range(RH)]
        yT_b = k.buf("ryT%d" % ri)
        with k.phase():
            hTa = k.sb("hTa", [128, KC, NT], BF16)
            hTa_b = k.buf("hTa")
            cosT = k.sb("cosT", [128, SEQ], F32)
            sinT = k.sb("sinT", [128, SEQ], F32)
            tab_b = k.buf("tab")
            k.dma("sp", cosT[:], cos_in[:, :], [], tab_b)
            k.dma("sp", sinT[:], sin_in[:, :], [], tab_b)
            wbig = k.sb("wbig", [128, 2, KC * 512], BF16)
            wbig_r = Rot(k, "wbig", 2)
            stg = k.sb("stg", [128, 2, 1024], BF16)
            stg_r = Rot(k, "stg", 2)
            for t in range(5):
                load_x(t)
                modulate(t, 3, 1 if t == 4 else 0, hTa, hTa_b, col0=TT[t][0])
            for qk in range(2):
                dst = qT_s if qk == 0 else kT_s
                for h in range(RH):
                    s_, sb_ = wslot_r.next()
                    wv = wslot[:, s_, :].rearrange("p (dc kc c) -> p dc kc c", dc=2, kc=KC)
                    k.dma("pool", wv, r_wqk[ri, qk, h].rearrange("dc p kc c -> p dc kc c"), [], sb_)
                    for t in range(5):
                        t0, tw = TT[t]
                        p1, p1b = ps_main.next()
                        p2, p2b = ps_main.next()
                        for kc in range(KC):
                            k.mm(p1[:, 0:tw], wv[:, 0, kc, :], hTa[:, kc, t0:t0 + tw], kc == 0, kc == KC - 1, [sb_, hTa_b], [p1b])
                        for kc in range(KC):
                            k.mm(p2[:, 0:tw], wv[:, 1, kc, :], hTa[:, kc, t0:t0 + tw], kc == 0, kc == KC - 1, [sb_, hTa_b], [p2b])
                        g_, gb = stg_r.next()
                        o1 = stg[:, g_, 0:tw]
                        o2 = stg[:, g_, 512:512 + tw]
                        if t < 4:
                            cs = cosT[:, t0:t0 + tw]
                            sn = sinT[:, t0:t0 + tw]
                            a_, ab = tmpD_r.next()
                            b_, bb = tmpD_r.next()
                            k.tt("dve", tmpD[:, a_, 0:tw], p1[:, 0:tw], cs, M, [p1b, tab_b], [ab])
                            k.tt("dve", tmpD[:, b_, 0:tw], p2[:, 0:tw], sn, M, [p2b, tab_b], [bb])
                            k.tt("pool", o1, tmpD[:, a_, 0:tw], tmpD[:, b_, 0:tw], SB, [ab, bb], [gb])
                            c_, cb = tmpD_r.next()
                            d_, db = tmpD_r.next()
                            k.tt("dve", tmpD[:, c_, 0:tw], p1[:, 0:tw], sn, M, [p1b, tab_b], [cb])
                            k.tt("dve", tmpD[:, d_, 0:tw], p2[:, 0:tw], cs, M, [p2b, tab_b], [db])
                            k.tt("pool", o2, tmpD[:, c_, 0:tw], tmpD[:, d_, 0:tw], A_, [cb, db], [gb])
                        else:
                            k.copy("act", o1, p1[:, 0:tw], [p1b], [gb])
                            k.copy("act", o2, p2[:, 0:tw], [p2b], [gb])
                        k.dma("sp", dst[h, :, :, t0:t0 + tw].rearrange("dc p t -> p dc t"),
                              stg[:, g_, :].rearrange("p (dc t) -> p dc t", dc=2)[:, :, 0:tw], [gb], hd_b[h])
            for vg in range(2):
                dst = v_s if vg == 0 else g_s
                for h in range(RH):
                    s_, sb_ = wbig_r.next()
                    wv = wbig[:, s_, :].rearrange("p (kc c) -> p kc c", kc=KC)
                    k.dma("pool", wv, r_wvg[ri, vg, h, :, :, :], [], sb_)
                    for tt_ in range(NTL):
                        c0 = tt_ * 128
                        pp, ppb = ps_main.next()
                        for kc in range(KC):
                            k.mm(pp[:, :], hTa[:, kc, c0:c0 + 128], wv[:, kc, :], kc == 0, kc == KC - 1, [sb_, hTa_b], [ppb])
                        g_, gb = stg_r.next()
                        if vg == 0:
                            k.copy("dve" if tt_ % 2 else "act", stg[:, g_, 0:512], pp[:, :], [ppb], [gb])
                        else:
                            k.act(stg[:, g_, 0:512], pp[:, :], AF.Silu, [ppb], [gb])
                        k.dma("sp", dst[h, tt_, :, :], stg[:, g_, 0:512], [gb], hd_b[h])
        with k.phase():
            dec = k.sb("dec", [128, 5, 16], F32)
            lgt = k.sb("lgt", [128, 16], F32)
            idx = k.sb("idx", [128, 4], F32)
            tri = k.sb("tri", [128, 2, 128], F32)
            dec_b = k.buf("dec")
            k.dma("sp", lgt[:], r_lg[ri, :, :], [], dec_b)
            k.dma("sp", idx[:], idx_in[:, :], [], dec_b)
            k.dma("sp", tri[:], tri_in[:, :, :], [], dec_b)
            for a in range(4):
                k.ts("dve", dec[:, a, :], lgt[:], idx[:, a:a + 1], None, M, None, [dec_b], [dec_b])
            k.ts("dve", dec[:, 4, :], lgt[:], 128.0, None, M, None, [dec_b], [dec_b])
            k.act(dec[:], dec[:], AF.Exp, [dec_b], [dec_b])
            qkt = k.sb("qkt", [128, 2, 2, NT], BF16)
            vt = k.sb("vt", [128, NTL, 512], BF16)
            ld_b = k.buf("rld")
            ktf = k.sb("ktf", [128, NTL, 256], BF16)
            ktb = k.sb("ktb", [128, NTL, 256], BF16)
            kt_b = k.buf("kt")
            oacc = k.sb("oacc", [128, NTL, 512], F32)
            oacc_b = [k.buf("oacc") for _ in range(NTL)]
            msk = k.sb("msk", [128, 2, 128], F32)
            msk_b = k.buf("msk")
            S32 = k.sb("S32", [128, 2, 512], F32)
            Sbf = k.sb("Sbf", [128, 2, 512], BF16)
            S_b = k.buf("S")
            PT = k.sb("PT", [128, 2, 128], BF16)
            PT_r = Rot(k, "PT", 2)
            sgt = k.sb("sgt", [128, 2, 512], BF16)
            sg_r = Rot(k, "sg", 2)
            yt = k.sb("yt", [128, 2, 512], BF16)
            yt_r = Rot(k, "yt", 2)
            yTst = k.sb("yTst", [128, 2, 4, 128], BF16)
            yTst_r = Rot(k, "yTst", 2)
            bst = k.sb("bst", [128, 8], F32)
            bst_b = k.buf("bst")
            for h in range(RH):
                k.dma("sp", qkt[:, 0, :, :], qT_s[h].rearrange("dc p t -> p dc t"), [hd_b[h]], ld_b)
                k.dma("sp", qkt[:, 1, :, :], kT_s[h].rearrange("dc p t -> p dc t"), [hd_b[h]], ld_b)
                k.dma("sp", vt[:, :, :], v_s[h].rearrange("t p e -> p t e"), [hd_b[h]], ld_b)
                k.ts("dve", msk[:, 0, :], tri[:, 0, :], dec[:, 0, h:h + 1], None, M, None, [dec_b], [msk_b])
                k.ts("dve", msk[:, 1, :], tri[:, 1, :], dec[:, 2, 8 + h:9 + h], None, M, None, [dec_b], [msk_b])
                for tt_ in range(NTL):
                    c0 = tt_ * 128
                    pbuf = psB_b[tt_ % 2]
                    pv = psB[:, (tt_ % 2) * 512:(tt_ % 2) * 512 + 256]
                    for dc in range(2):
                        k.tr(pv[:, dc * 128:(dc + 1) * 128], qkt[:, 1, dc, c0:c0 + 128], identb[:], [ld_b, cst], [pbuf])
                    k.ts("dve", ktf[:, tt_, :], pv, dec[:, 0, h:h + 1], None, M, None, [pbuf, dec_b], [kt_b])
                    k.ts("dve", ktb[:, tt_, :], pv, dec[:, 2, 8 + h:9 + h], None, M, None, [pbuf, dec_b], [kt_b])
                for dr in range(2):
                    order = ([16, 17] + list(range(16))) if dr == 0 else ([17, 16] + list(range(15, -1, -1)))
                    kt = ktf if dr == 0 else ktb
                    qd = dec[:, 1, h:h + 1] if dr == 0 else dec[:, 3, 8 + h:9 + h]
                    cd = dec[:, 4, h:h + 1] if dr == 0 else dec[:, 4, 8 + h:9 + h]
                    first = True
                    for oi, tt_ in enumerate(order):
                        c0 = tt_ * 128
                        pS, pSb = ps_st.next()
                        for dc in range(2):
                            k.mm(pS[:, 0:128], qkt[:, 1, dc, c0:c0 + 128], qkt[:, 0, dc, c0:c0 + 128], dc == 0, dc == 1, [ld_b], [pSb])
                        p_, pb_ = PT_r.next()
                        k.tt("dve", PT[:, p_, :], pS[:, 0:128], msk[:, dr, :], M, [pSb, msk_b], [pb_])
                        po, pob = ps_main.next()
                        k.mm(po[:, :], PT[:, p_, :], vt[:, tt_, :], True, first, [pb_, ld_b], [pob])
                        if not first:
                            for dc in range(2):
                                k.mm(po[:, :], qkt[:, 0, dc, c0:c0 + 128], Sbf[:, dc, :], False, dc == 1, [ld_b, S_b], [pob])
                        if dr == 0:
                            k.ts("dve", oacc[:, tt_, :], po[:, :], qd, None, M, None, [pob, dec_b], [oacc_b[tt_]])
                        else:
                            k.stt(oacc[:, tt_, :], po[:, :], qd, oacc[:, tt_, :], M, A_, [pob, dec_b, oacc_b[tt_]], [oacc_b[tt_]])
                        if oi < NTL - 1:
                            for dc in range(2):
                                pd, pdb = ps_y.next()
                                k.mm(pd[:, :], kt[:, tt_, dc * 128:(dc + 1) * 128], vt[:, tt_, :], True, True, [kt_b, ld_b], [pdb])
                                if first:
                                    k.ts("dve", S32[:, dc, :], pd[:, :], cd, None, M, None, [pdb, dec_b], [S_b])
                                else:
                                    k.tt("dve", S32[:, dc, :], pd[:, :], S32[:, dc, :], A_, [pdb, S_b], [S_b])
                                    k.ts("pool", S32[:, dc, :], S32[:, dc, :], cd, None, M, None, [S_b, dec_b], [S_b])
                                k.copy("act", Sbf[:, dc, :], S32[:, dc, :], [S_b], [S_b])
                        first = False
                for tt_ in range(NTL):
                    c0 = tt_ * 128
                    k.op("dve", lambda e: e.bn_stats(out=bst[:, 0:6], in_=oacc[:, tt_, :]), [oacc_b[tt_]], [bst_b])
                    k.op("dve", lambda e: e.bn_aggr(out=bst[:, 6:8], in_=bst[:, 0:6]), [bst_b], [bst_b])
                    k.rsqrt(bst[:, 7:8], bst[:, 7:8], RET_EPS, [bst_b], [bst_b])
                    a_, ab = tmpD_r.next()
                    k.ts("dve", tmpD[:, a_, :], oacc[:, tt_, :], bst[:, 6:7], bst[:, 7:8], SB, M, [oacc_b[tt_], bst_b], [ab])
                    s_, sb2 = sg_r.next()
                    k.dma("sp", sgt[:, s_, :], g_s[h, tt_, :, :], [hd_b[h]], sb2)
                    y_, yb = yt_r.next()
                    k.tt("dve", yt[:, y_, :], tmpD[:, a_, :], sgt[:, s_, :], M, [ab, sb2], [yb])
                    pbuf = psB_b[tt_ % 2]
                    pv = psB[:, (tt_ % 2) * 512:(tt_ % 2 + 1) * 512]
                    for ec in range(4):
                        k.tr(pv[:, ec * 128:(ec + 1) * 128], yt[:, y_, ec * 128:(ec + 1) * 128], identb[:], [yb, cst], [pbuf])
                    z_, zb = yTst_r.next()
                    k.copy("act", yTst[:, z_, :, :], pv.rearrange("p (e c) -> p e c", e=4), [pbuf], [zb])
                    k.dma("sp", yT_s[h * 4:(h + 1) * 4, :, c0:c0 + 128].rearrange("e p t -> p e t"), yTst[:, z_, :, :], [zb], yT_b)
        out_proj(layer, r_wo[ri], yT_s, yT_b, last)

    def deltanet(layer, di, last):
        NTL = 18
        M, A_, SB = ALU.mult, ALU.add, ALU.subtract
        qk_s = k.dram("n_qk%d" % di, [32, 128, NT], BF16)
        v_s = k.dram("n_v%d" % di, [NH, NTL, 128, 128], BF16)
        z_s = k.dram("n_z%d" % di, [NH, NTL, 128, 128], BF16)
        yT_s = k.dram("n_yT%d" % di, [32, 128, NT], BF16)
        qk_b = k.buf("nqk%d" % di)
        vz_b = k.buf("nvz%d" % di)
        yT_b = k.buf("nyT%d" % di)
        LOFF, COFF, RAWW = 2, 2054, 2316
        with k.phase():
            cols = k.sb("cols", [128, NTL, 4, 32], F32)
            cols_b = k.buf("cols")
            gcT = k.sb("gcT", [32, 2, NT], F32)
            gcT_b = k.buf("gcT")
            mneg = k.sb("mneg", [128, 4, 128], F32)
            strict = k.sb("strict", [128, 2, 128], F32)
            nwt = k.sb("nwt", [128, 128], F32)
            ctab_b = k.buf("ctab")
            k.dma("sp", mneg[:], dn_mneg[:, :, :], [], ctab_b)
            k.dma("sp", strict[:], dn_strict[:, :, :], [], ctab_b)
            k.dma("sp", nwt[:], d_nw[di, :, :], [], ctab_b)
            with k.phase():
                hTa = k.sb("hTa", [128, KC, NT], BF16)
                hTa_b = k.buf("hTa")
                for t in range(5):
                    load_x(t)
                    modulate(t, 3, 1 if t == 4 else 0, hTa, hTa_b, col0=TT[t][0])
                def project_fm(wsrc, dst_fn, nrows=128):
                    s_, sb_ = wslot_r.next()
                    wv = wslot[:, s_, 0:KC * 128].rearrange("p (kc c) -> p kc c", kc=KC)
                    k.dma("pool", wv, wsrc, [], sb_)
                    for t in range(5):
                        t0, tw = TT[t]
                        pp, ppb = ps_main.next()
                        for kc in range(KC):
                            k.mm(pp[0:nrows, 0:tw], wv[:, kc, 0:nrows], hTa[:, kc, t0:t0 + tw], kc == 0, kc == KC - 1, [sb_, hTa_b], [ppb])
                        dst_fn(t, pp, ppb)

                with k.phase():
                    rawp = k.sb("rawp", [128, RAWW], F32)
                    raw_b = k.buf("rawp")
                    cv = k.sb("cv", [128, NT], F32)
                    cv_b = k.buf("cv")
                    slb = k.sb("slb", [128, NT], BF16)
                    slb_b = k.buf("slb")
                    cw = k.sb("cw", [128, 5], F32)
                    cw_b = k.buf("cw")
                    stg = k.sb("stg", [128, 2, 512], BF16)
                    stg_r = Rot(k, "stg", 2)
                    k.memset("dve", rawp[:], 0.0, [raw_b])

                    def to_raw(t, pp, ppb):
                        t0, tw = TT[t]
                        off = (LOFF + t0) if t < 4 else COFF
                        k.copy("act", rawp[:, off:off + tw], pp[:, 0:tw], [ppb], [raw_b])

                    def conv_silu(blk, out_ap, out_b):
                        k.dma("sp", cw[:], d_cw[di, :, blk, :], [], cw_b)
                        for (o0, r0, L) in ((0, 0, SEQ), (SEQ, COFF - 2, CTX)):
                            k.ts("dve", cv[:, o0:o0 + L], rawp[:, r0:r0 + L], cw[:, 0:1], None, M, None, [raw_b, cw_b], [cv_b])
                            for tap in range(1, 5):
                                k.stt(cv[:, o0:o0 + L], rawp[:, r0 + tap:r0 + tap + L], cw[:, tap:tap + 1], cv[:, o0:o0 + L], M, A_,
                                      [raw_b, cw_b, cv_b], [cv_b])
                        k.act(out_ap, cv[:], AF.Silu, [cv_b], [out_b])

                    def to_tokmajor(src, src_b, dst, h):
                        for g0 in range(0, NTL, 4):
                            ng = min(4, NTL - g0)
                            for i_ in range(ng):
                                c0 = (g0 + i_) * 128
                                k.tr(psB[:, i_ * 128:(i_ + 1) * 128], src[:, c0:c0 + 128], identb[:], [src_b, cst], [psB_b[0]])
                            g_, gb = stg_r.next()
                            k.copy("dve", stg[:, g_, 0:ng * 128], psB[:, 0:ng * 128], [psB_b[0]], [gb])
                            k.dma("sp", dst[h, g0:g0 + ng, :, :].rearrange("t p e -> p t e"),
                                  stg[:, g_, 0:ng * 128].rearrange("p (t e) -> p t e", t=ng), [gb], vz_b)

                    for blk in range(32):
                        project_fm(d_w[di, blk, :, :, :], to_raw)
                        conv_silu(blk, cv[:], cv_b)
                        for t in range(5):
                            t0, tw = TT[t]
                            a_, ab = tmpA_r.next()
                            k.act(tmpA[:, a_, 0:tw], cv[:, t0:t0 + tw], AF.Square, [cv_b], [ab])
                            pq, pqb = ps_st.next()
                            k.mm(pq[:, 0:tw], ones[:], tmpA[:, a_, 0:tw], True, True, [ab, cst], [pqb])
                            d_, db = tmpD_r.next()
                            k.rsqrt(tmpD[:, d_, 0:tw], pq[:, 0:tw], L2_EPS, [pqb], [db])
                            k.tt("dve", slb[:, t0:t0 + tw], cv[:, t0:t0 + tw], tmpD[:, d_, 0:tw], M, [cv_b, db], [slb_b])
                        k.dma("sp", qk_s[blk, :, :], slb[:], [slb_b], qk_b)
                    for h in range(NH):
                        project_fm(d_w[di, 32 + h, :, :, :], to_raw)
                        conv_silu(32 + h, slb[:], slb_b)
                        to_tokmajor(slb, slb_b, v_s, h)
                    for h in range(NH):
                        def z_evac(t, pp, ppb):
                            t0, tw = TT[t]
                            k.act(slb[:, t0:t0 + tw], pp[:, 0:tw], AF.Silu, [ppb], [slb_b])
                        project_fm(d_w[di, 64 + h, :, :, :], z_evac)
                        to_tokmajor(slb, slb_b, z_s, h)
                with k.phase():
                    gt = k.sb("gt", [32, 2, NT], F32)
                    gt_b = k.buf("gt")
                    cs = k.sb("cs", [32, 128], F32)
                    cs_b = k.buf("cs")
                    one_r = k.sb("one_r", [32, 128], F32)
                    pr = k.sb("pr", [32, 4, 2], F32)
                    pr_b = k.buf("pr")
                    k.memset("dve", one_r[:], 1.0, [pr_b])
                    k.dma("sp", pr[:, 0:2, :], d_gp[di, :, :, :], [], pr_b)
                    k.act(pr[:, 2, :], pr[:, 0, :], AF.Exp, [pr_b], [pr_b])
                    k.ts("dve", pr[:, 2, :], pr[:, 2, :], -1.0, None, M, None, [pr_b], [pr_b])
                    for dr in range(2):
                        for q2 in range(2):
                            def g_evac(t, pp, ppb, q2=q2):
                                t0, tw = TT[t]
                                k.copy("act", gt[:, q2, t0:t0 + tw], pp[0:32, 0:tw], [ppb], [gt_b])
                            project_fm(d_wba[di, 2 * dr + q2, :, :, :], g_evac, nrows=32)
                        bsl = gt[:, 0, :]
                        asl = gt[:, 1, :]
                        k.act(bsl, bsl, AF.Exp, [gt_b], [gt_b], scale=-1.0)
                        k.ts("dve", bsl, bsl, 1.0, None, A_, None, [gt_b], [gt_b])
                        k.op("dve", lambda e: e.reciprocal(out=bsl, in_=bsl), [gt_b], [gt_b])
                        k.act(asl, asl, AF.Exp, [gt_b, pr_b], [gt_b], bias=pr[:, 1, dr:dr + 1])
                        k.act(asl, asl, AF.Ln, [gt_b], [gt_b], bias=1.0)
                        k.ts("dve", asl, asl, pr[:, 2, dr:dr + 1], None, M, None, [gt_b, pr_b], [gt_b])
                        for tt_ in range(NTL):
                            c0 = tt_ * 128
                            k.op("dve", lambda e: e.tensor_tensor_scan(out=cs[:, :], data0=one_r[:], data1=asl[:, c0:c0 + 128],
                                                                       initial=0.0, op0=M, op1=A_), [gt_b, pr_b], [cs_b])
                            if dr == 0:
                                k.copy("dve", gcT[:, 0, c0:c0 + 128], cs[:, :], [cs_b], [gcT_b])
                            else:
                                k.tt("dve", gcT[:, 1, c0:c0 + 128], asl[:, c0:c0 + 128], cs[:, :], SB, [gt_b, cs_b], [gcT_b])
                                k.ts("dve", gcT[:, 1, c0:c0 + 128], gcT[:, 1, c0:c0 + 128], cs[:, 127:128], None, A_, None,
                                     [gcT_b, cs_b], [gcT_b])
                        for tt_ in range(NTL):
                            c0 = tt_ * 128
                            pt_, ptb = ps_st.next()
                            k.tr(pt_[:, 0:32], bsl[:, c0:c0 + 128], ident[0:32, 0:32], [gt_b, cst], [ptb])
                            k.tr(pt_[:, 32:64], gcT[:, dr, c0:c0 + 128], ident[0:32, 0:32], [gcT_b, cst], [ptb])
                            k.copy("dve", cols[:, tt_, 2 * dr:2 * dr + 2, :], pt_[:, 0:64].rearrange("p (a h) -> p a h", a=2), [ptb], [cols_b])
            with k.phase():
                sel = k.sb("sel", [32, 32, 128], F32)
                k.dma("sp", sel[:], dn_sel[:, :, :], [], ctab_b)
                qT = k.sb("qT", [128, NT], BF16)
                kT = k.sb("kT", [128, NT], BF16)
                vt = k.sb("vt", [128, NTL, 128], BF16)
                ld_b = k.buf("nld")
                oacc = k.sb("oacc", [128, NTL, 128], F32)
                oacc_b = [k.buf("noacc") for _ in range(NTL)]
                mats = k.sb("mats", [128, 17, 128], F32)
                mb_ = [k.buf("mat%d" % i) for i in range(17)]
                bd16 = k.sb("bd16", [128, 128], F32)
                lmk = k.sb("lmk", [128, 2, 3, 128], F32)
                k.dma("sp", bd16[:], dn_bd[:, :], [], ctab_b)
                k.dma("sp", lmk[:], dn_lm[:, :, :, :], [], ctab_b)
                matb = k.sb("matb", [128, 8, 128], BF16)
                bb_ = [k.buf("matb%d" % i) for i in range(8)]
                S32 = k.sb("S32", [128, 128], F32)
                Sbf = k.sb("Sbf", [128, 128], BF16)
                S_b = k.buf("S")
                sc4 = k.sb("sc4", [128, 8], F32)
                sc_b = k.buf("sc4")
                zt = k.sb("zt", [128, 2, 128], BF16)
                zt_r = Rot(k, "zt", 2)
                yst = k.sb("yst", [128, 2, 128], BF16)
                yst_r = Rot(k, "yst", 2)
                ytk = k.sb("ytk", [128, 128], BF16)
                ytk_b = k.buf("ytk")
                D_, DT_, DN_, N0, M0, R_, NA, MA, NB, MB, U_, EG, T_, P1_, L0, L1, L2 = range(17)
                VB, KBG, WT, VN, QG, PT_, KO, KTK = range(8)

                def mat(i):
                    return mats[:, i, :]

                for h in range(NH):
                    kh = h // 2
                    if h % 2 == 0:
                        k.dma("sp", qT[:], qk_s[kh, :, :], [qk_b], ld_b)
                        k.dma("sp", kT[:], qk_s[16 + kh, :, :], [qk_b], ld_b)
                    k.dma("sp", vt[:, :, :], v_s[h].rearrange("t p e -> p t e"), [vz_b], ld_b)
                    for dr in range(2):
                        order = ([16, 17] + list(range(16))) if dr == 0 else ([17, 16] + list(range(15, -1, -1)))
                        first = True
                        for oi, tt_ in enumerate(order):
                            c0 = tt_ * 128
                            bcol = cols[:, tt_, 2 * dr, h:h + 1]
                            gcol = cols[:, tt_, 2 * dr + 1, h:h + 1]
                            lastpos = 127 if dr == 0 else 0
                            pbc, pbcb = ps_st.next()
                            k.mm(pbc[:, 0:128], sel[:, h, :], gcT[:, dr, c0:c0 + 128], True, True, [ctab_b, gcT_b], [pbcb])
                            k.ts("dve", mat(D_), pbc[:, 0:128], -1.0, gcol, M, A_, [pbcb, cols_b], [mb_[D_]])
                            k.tt("dve", mat(D_), mat(D_), mneg[:, dr, :], A_, [mb_[D_], ctab_b], [mb_[D_]])
                            k.act(mat(D_), mat(D_), AF.Exp, [mb_[D_]], [mb_[D_]])
                            k.tt("pool", mat(DN_), mat(D_), strict[:, dr, :], M, [mb_[D_], ctab_b], [mb_[DN_]])
                            k.ts("dve", mat(DT_), pbc[:, 0:128], gcol, None, SB, None, [pbcb, cols_b], [mb_[DT_]])
                            k.tt("dve", mat(DT_), mat(DT_), mneg[:, 2 + dr, :], A_, [mb_[DT_], ctab_b], [mb_[DT_]])
                            k.act(mat(DT_), mat(DT_), AF.Exp, [mb_[DT_]], [mb_[DT_]])
                            k.act(mat(EG), pbc[:, 0:128], AF.Exp, [pbcb], [mb_[EG]])
                            k.copy("dve", sc4[:, 0:1], pbc[:, lastpos:lastpos + 1], [pbcb], [sc_b])
                            k.act(sc4[:, 1:2], sc4[:, 0:1], AF.Exp, [sc_b], [sc_b])
                            k.act(sc4[:, 2:3], gcol, AF.Exp, [cols_b, sc_b], [sc_b], scale=-1.0, bias=sc4[:, 0:1])
                            k.act(sc4[:, 3:4], gcol, AF.Exp, [cols_b], [sc_b])
                            k.tt("dve", sc4[:, 4:5], sc4[:, 3:4], bcol, M, [sc_b, cols_b], [sc_b])
                            pkk, pkkb = ps_st.next()
                            k.mm(pkk[:, 0:128], kT[:, c0:c0 + 128], kT[:, c0:c0 + 128], True, True, [ld_b], [pkkb])
                            k.stt(mat(N0), pkk[:, 0:128], bcol, mat(DN_), M, M, [pkkb, cols_b, mb_[DN_]], [mb_[N0]])
                            ptr, ptrb = ps_st.next()
                            k.tr(ptr[:, 0:128], mat(N0), ident[:], [mb_[N0], cst], [ptrb])
                            k.copy("act", mat(M0), ptr[:, 0:128], [ptrb], [mb_[M0]])
                            k.tt("pool", mat(NA), mat(N0), bd16[:], M, [mb_[N0], ctab_b], [mb_[NA]])
                            k.tt("pool", mat(MA), mat(M0), bd16[:], M, [mb_[M0], ctab_b], [mb_[MA]])
                            for lv in range(3):
                                k.tt("pool", mat(L0 + lv), mat(N0), lmk[:, dr, lv, :], M, [mb_[N0], ctab_b], [mb_[L0 + lv]])
                            k.tt("dve", mat(R_), ident[:], mat(MA), SB, [cst, mb_[MA]], [mb_[R_]])
                            cn, cm = NA, MA
                            for sq in range(3):
                                nn, nm = (NB, MB) if sq % 2 == 0 else (NA, MA)
                                pn, pnb = ps_main.next()
                                k.mm(pn[:, 0:128], mat(cm), mat(cn), True, True, [mb_[cm], mb_[cn]], [pnb])
                                if sq < 2:
                                    pm_, pmb_ = ps_main.next()
                                    k.mm(pm_[:, 0:128], mat(cn), mat(cm), True, True, [mb_[cm], mb_[cn]], [pmb_])
                                k.copy("act", mat(nn), pn[:, 0:128], [pnb], [mb_[nn]])
                                if sq < 2:
                                    k.copy("dve", mat(nm), pm_[:, 0:128], [pmb_], [mb_[nm]])
                                pr_, prb_ = ps_y.next()
                                k.mm(pr_[:, 0:128], mat(nn), mat(R_), True, True, [mb_[nn], mb_[R_]], [prb_])
                                k.tt("dve", mat(R_), mat(R_), pr_[:, 0:128], A_, [mb_[R_], prb_], [mb_[R_]])
                                cn, cm = nn, nm
                            for lv in range(3):
                                pt2, pt2b = ps_st.next()
                                k.tr(pt2[:, 0:128], mat(R_), ident[:], [mb_[R_], cst], [pt2b])
                                k.copy("act", mat(T_), pt2[:, 0:128], [pt2b], [mb_[T_]])
                                p1, p1b = ps_main.next()
                                k.mm(p1[:, 0:128], mat(L0 + lv), mat(R_), True, True, [mb_[L0 + lv], mb_[R_]], [p1b])
                                k.copy("dve", mat(P1_), p1[:, 0:128], [p1b], [mb_[P1_]])
                                p2, p2b = ps_y.next()
                                k.mm(p2[:, 0:128], mat(T_), mat(P1_), True, True, [mb_[T_], mb_[P1_]], [p2b])
                                k.tt("dve", mat(R_), mat(R_), p2[:, 0:128], SB, [mb_[R_], p2b], [mb_[R_]])
                            k.ts("dve", mat(MA), vt[:, tt_, :], bcol, None, M, None, [ld_b, cols_b], [mb_[MA]])
                            pu_, pub_ = ps_main.next()
                            k.mm(pu_[:, 0:128], mat(R_), mat(MA), True, True, [mb_[R_], mb_[MA]], [pub_])
                            k.copy("act", mat(U_), pu_[:, 0:128], [pub_], [mb_[U_]])
                            k.tr(psB[:, 0:128], kT[:, c0:c0 + 128], identb[:], [ld_b, cst], [psB_b[0]])
                            k.ts("dve", mat(MB), psB[:, 0:128], sc4[:, 4:5], None, M, None, [psB_b[0], sc_b], [mb_[MB]])
                            k.ts("dve", matb[:, KO, :], psB[:, 0:128], sc4[:, 2:3], None, M, None, [psB_b[0], sc_b], [bb_[KO]])
                            pw_, pwb_ = ps_main.next()
                            k.mm(pw_[:, 0:128], mat(MB), mat(R_), True, True, [mb_[MB], mb_[R_]], [pwb_])
                            k.copy("act", matb[:, WT, :], pw_[:, 0:128], [pwb_], [bb_[WT]])
                            if first:
                                k.copy("dve", matb[:, VN, :], mat(U_), [mb_[U_]], [bb_[VN]])
                            else:
                                pv_, pvb_ = ps_main.next()
                                k.mm(pv_[:, 0:128], matb[:, WT, :], Sbf[:], True, True, [bb_[WT], S_b], [pvb_])
                                k.tt("dve", matb[:, VN, :], mat(U_), pv_[:, 0:128], SB, [mb_[U_], pvb_], [bb_[VN]])
                            pqk, pqkb = ps_st.next()
                            k.mm(pqk[:, 0:128], kT[:, c0:c0 + 128], qT[:, c0:c0 + 128], True, True, [ld_b], [pqkb])
                            k.tt("dve", matb[:, PT_, :], pqk[:, 0:128], mat(DT_), M, [pqkb, mb_[DT_]], [bb_[PT_]])
                            po, pob = ps_y.next()
                            k.mm(po[:, 0:128], matb[:, PT_, :], matb[:, VN, :], True, first, [bb_[PT_], bb_[VN]], [pob])
                            if not first:
                                k.tt("pool", matb[:, QG, :], qT[:, c0:c0 + 128], mat(EG), M, [ld_b, mb_[EG]], [bb_[QG]])
                                k.mm(po[:, 0:128], matb[:, QG, :], Sbf[:], False, True, [bb_[QG], S_b], [pob])
                            if dr == 0:
                                k.copy("act", oacc[:, tt_, :], po[:, 0:128], [pob], [oacc_b[tt_]])
                            else:
                                k.tt("dve", oacc[:, tt_, :], oacc[:, tt_, :], po[:, 0:128], A_, [pob, oacc_b[tt_]], [oacc_b[tt_]])
                            if oi < NTL - 1:
                                pd, pdb = ps_y.next()
                                k.mm(pd[:, 0:128], matb[:, KO, :], matb[:, VN, :], True, True, [bb_[KO], bb_[VN]], [pdb])
                                if first:
                                    k.copy("dve", S32[:], pd[:, 0:128], [pdb], [S_b])
                                else:
                                    k.stt(S32[:], S32[:], sc4[:, 1:2], pd[:, 0:128], M, A_, [pdb, sc_b, S_b], [S_b])
                                k.copy("act", Sbf[:], S32[:], [S_b], [S_b])
                            first = False
                    for tt_ in range(NTL):
                        c0 = tt_ * 128
                        a_, ab = tmpA_r.next()
                        k.act(tmpA[:, a_, 0:128], oacc[:, tt_, :], AF.Square, [oacc_b[tt_]], [ab])
                        k.op("dve", lambda e: e.tensor_reduce(out=sc4[:, 5:6], in_=tmpA[:, a_, 0:128], axis=mybir.AxisListType.X, op=A_), [ab], [sc_b])
                        k.ts("dve", sc4[:, 6:7], sc4[:, 5:6], 1.0 / 128.0, None, M, None, [sc_b], [sc_b])
                        k.rsqrt(sc4[:, 6:7], sc4[:, 6:7], DN_EPS, [sc_b], [sc_b])
                        d_, db = tmpD_r.next()
                        k.stt(tmpD[:, d_, 0:128], oacc[:, tt_, :], sc4[:, 6:7], nwt[:], M, M, [oacc_b[tt_], sc_b, ctab_b], [db])
                        z_, zb = zt_r.next()
                        k.dma("sp", zt[:, z_, :], z_s[h, tt_, :, :], [vz_b], zb)
                        k.tt("dve", ytk[:], tmpD[:, d_, 0:128], zt[:, z_, :], M, [db, zb], [ytk_b])
                        k.tr(psB[:, 0:128], ytk[:], identb[:], [ytk_b, cst], [psB_b[0]])
                        y_, yb = yst_r.next()
                        k.copy("act", yst[:, y_, :], psB[:, 0:128], [psB_b[0]], [yb])
                        k.dma("sp", yT_s[h, :, c0:c0 + 128], yst[:, y_, :], [yb], yT_b)
        out_proj(layer, d_wo[di], yT_s, yT_b, last)

    out_b = k.buf("out")

    ri = 0
    di = 0
    for layer in range(depth):
        last = layer == depth - 1
        modulation(layer)
        ffn_phase(layer, 0, 0, layer * 3 + 0, [0, 1, 2, 3, 4])
        if do_mixer:
            if mixers[layer] == "ret":
                retention(layer, ri, last)
                ri += 1
            else:
                deltanet(layer, di, last)
                di += 1
        ffn_phase(layer, 1, 6, layer * 3 + 2, [0, 1, 2, 3] if last else [0, 1, 2, 3, 4], final=last)
    k.finish("sp", [out_b])


def _fm(v):
    return np.ascontiguousarray(v.reshape(KC, 128).T)


def prep_shared(inputs, depth=DEPTH):
    f = np.float32
    sh = {}
    mw = inputs["mod_w"][:depth]
    sh["mod_w"] = np.ascontiguousarray(mw.reshape(depth, KC, 128, 36, 512).transpose(0, 3, 2, 1, 4))
    sh["mod_b"] = np.ascontiguousarray(inputs["mod_b"][:depth].reshape(depth, 144, 128).transpose(0, 2, 1))
    sh["ln_g"] = np.ascontiguousarray(inputs["ln_g"][:depth].reshape(depth * 3 * KC, 128).T)
    sh["ln_b"] = np.ascontiguousarray(inputs["ln_b"][:depth].reshape(depth * 3 * KC, 128).T)
    w_in = inputs["ffn_w_in"][:depth].reshape(depth * 2, KC, 128, 2, FJ, 128)
    sh["ffn_w_in"] = np.ascontiguousarray(w_in.transpose(0, 4, 2, 3, 1, 5))
    w_out = inputs["ffn_w_out"][:depth].reshape(depth * 2, FJ, 128, KC, 128)
    sh["ffn_w_out"] = np.ascontiguousarray(w_out.transpose(0, 3, 2, 1, 4))
    sh["ident"] = np.eye(128, dtype=f)
    return sh


def prep_ret(inputs, n_ret):
    f = np.float32
    sh = {}
    W = inputs["ret_w_in"][:n_ret]
    qk = W[:, :, 0:4096].reshape(n_ret, KC, 128, 2, RH, 128, 2)
    sh["ret_wqk"] = np.ascontiguousarray(qk.transpose(0, 3, 4, 6, 2, 1, 5))
    vg = W[:, :, 4096:12288].reshape(n_ret, KC, 128, 2, RH, 512)
    sh["ret_wvg"] = np.ascontiguousarray(vg.transpose(0, 3, 4, 2, 1, 5))
    wo_ = inputs["ret_w_out"][:n_ret].reshape(n_ret, 32, 128, KC, 128)
    sh["ret_wo"] = np.ascontiguousarray(wo_.transpose(0, 3, 2, 1, 4))
    lg = inputs["ret_log_decay"][:n_ret].reshape(n_ret, 1, 2 * RH)
    sh["ret_lg"] = np.ascontiguousarray(np.broadcast_to(lg, (n_ret, 128, 2 * RH))).astype(f)
    tok = np.arange(SEQ)
    pos_r = (tok // 64).astype(f)
    pos_c = (tok % 64).astype(f)
    inv = (10000.0 ** (-np.arange(0, 128, 2, dtype=f) / 128.0)).astype(f)
    ang = np.concatenate([pos_r[:, None] * inv, pos_c[:, None] * inv], -1).astype(f)
    sh["cosT"] = np.ascontiguousarray(np.cos(ang).T.astype(f))
    sh["sinT"] = np.ascontiguousarray(np.sin(ang).T.astype(f))
    j = np.arange(128)[:, None]
    i = np.arange(128)[None, :]
    sh["tri"] = np.ascontiguousarray(np.stack([(j <= i), (j >= i)], axis=1).astype(f))
    p = np.arange(128, dtype=f)
    sh["idx"] = np.ascontiguousarray(np.stack([-(p + 1), p + 1, p - 128, 128 - p], axis=1).astype(f))
    return sh


def prep_dn(inputs, n_dn):
    f = np.float32
    sh = {}
    W = inputs["dn_w_in"][:n_dn]
    blk = W[:, :, 0:12288].reshape(n_dn, KC, 128, 96, 128)
    sh["dn_w"] = np.ascontiguousarray(blk.transpose(0, 3, 2, 1, 4))
    ba = W[:, :, 12288:12416].reshape(n_dn, KC, 128, 4, 32)
    bap = np.zeros((n_dn, 4, 128, KC, 128), f)
    bap[:, :, :, :, 0:32] = ba.transpose(0, 3, 2, 1, 4)
    sh["dn_wba"] = bap
    cwv = inputs["dn_conv_w"][:n_dn].reshape(n_dn, 5, 64, 128)
    sh["dn_cw"] = np.ascontiguousarray(cwv.transpose(0, 3, 2, 1))
    gp = np.stack([inputs["dn_a_log"][:n_dn], inputs["dn_dt_bias"][:n_dn]], axis=1)
    sh["dn_gp"] = np.ascontiguousarray(gp.transpose(0, 3, 1, 2)).astype(f)
    nw = inputs["dn_norm_w"][:n_dn].reshape(n_dn, 1, 128)
    sh["dn_nw"] = np.ascontiguousarray(np.broadcast_to(nw, (n_dn, 128, 128))).astype(f)
    wo_ = inputs["dn_w_out"][:n_dn].reshape(n_dn, 32, 128, KC, 128)
    sh["dn_wo"] = np.ascontiguousarray(wo_.transpose(0, 3, 2, 1, 4))
    sel = np.zeros((32, 32, 128), f)
    for h in range(32):
        sel[h, h, :] = 1.0
    sh["dn_sel"] = sel
    i = np.arange(128)[:, None]
    j = np.arange(128)[None, :]
    NEG = -30000.0
    low = np.where(i >= j, 0.0, NEG)
    up = np.where(i <= j, 0.0, NEG)
    sh["dn_mneg"] = np.ascontiguousarray(np.stack([low, up, up, low], axis=1).astype(f))
    sh["dn_strict"] = np.ascontiguousarray(np.stack([(i > j), (i < j)], axis=1).astype(f))
    sh["dn_bd"] = np.ascontiguousarray((i // 16 == j // 16).astype(f))
    lm = np.zeros((128, 2, 3, 128), f)
    for lv, s_ in enumerate((16, 32, 64)):
        same = (i // (2 * s_)) == (j // (2 * s_))
        lo = same & ((i % (2 * s_)) >= s_) & ((j % (2 * s_)) < s_)
        lm[:, 0, lv, :] = lo
        lm[:, 1, lv, :] = lo.T
    sh["dn_lm"] = lm
    return sh


def prep_core(inputs, b):
    xT = np.concatenate([inputs["x"][b].T, inputs["ctx"][b].T], axis=1)
    cc = np.stack([_fm(inputs["c"][b]), _fm(inputs["c_ctx"])], axis=-1)
    return {"xT": np.ascontiguousarray(xT, dtype=np.float32), "cc": np.ascontiguousarray(cc, dtype=np.float32)}


def kernel(**inputs):
    inputs = {k_: np.asarray(v) for k_, v in inputs.items()}
    nc = build_program()
    sh = prep_shared(inputs)
    sh.update(prep_ret(inputs, 2))
    sh.update(prep_dn(inputs, 2))
    B = inputs["x"].shape[0]
    in_maps = []
    for b in range(B):
        m = dict(sh)
        m.update(prep_core(inputs, b))
        in_maps.append(m)
    res = run_bass_kernel_spmd(nc, in_maps, core_ids=list(range(B)))
    out = np.stack([np.ascontiguousarray(res.results[b]["outT"].T) for b in range(B)], axis=0)
    return out.astype(np.float32)
```

```python
import contextlib
import numpy as np
import concourse.bass as bass
import concourse.mybir as mybir
from concourse.bass_utils import run_bass_kernel_spmd

F32 = mybir.dt.float32
BF16 = mybir.dt.bfloat16
AF = mybir.ActivationFunctionType
ALU = mybir.AluOpType

D = 2048
KC = 16
SEQ = 2048
CTX = 256
NT = SEQ + CTX
DEPTH = 4
FH = 5504
FJ = 43
ALPHA = float((2 * DEPTH) ** 0.25)
LN_EPS = 1e-5
TT = [(0, 512), (512, 512), (1024, 512), (1536, 512), (2048, 256)]
RH = 8
RET_EPS = LN_EPS * 256.0
NH = 32
DN_EPS = 1e-6 * 128.0
L2_EPS = 1e-6


LAST_COUNTS = {}


_BUF_UID = [0]


class Buf:
    __slots__ = ("name", "w", "r", "dsem", "dcount", "uid")

    def __init__(self, name):
        _BUF_UID[0] += 1
        self.uid = _BUF_UID[0]
        self.name = name
        self.w = None
        self.r = {}
        self.dsem = None
        self.dcount = 0


class KB:
    def __init__(self, nc, es):
        self.nc = nc
        self.es = es
        self.eng = {"pe": nc.tensor, "dve": nc.vector, "act": nc.scalar, "pool": nc.gpsimd, "sp": nc.sync}
        self.sem = {}
        self.cnt = {}
        self.seen = {}
        for e in self.eng:
            self.sem[e] = es.enter_context(nc.semaphore("sem_" + e))
            self.cnt[e] = 0
            self.seen[e] = {}
        self.nsem = len(self.eng)
        self.uid = 0
        self.dbufs = []
        self.ges = es
        self.eps_tab = {}
        self.eps_t = es.enter_context(nc.sbuf_tensor("s_epsconst", [128, 8], F32))
        self.eps_b = Buf("epsconst")

    def sb(self, name, shape, dt):
        self.uid += 1
        return self.es.enter_context(self.nc.sbuf_tensor("s_%s_%d" % (name, self.uid), list(shape), dt))

    def ps(self, name, shape, dt=F32):
        return self.es.enter_context(self.nc.psum_tensor("p_" + name, list(shape), dt))

    def dram(self, name, shape, dt):
        return self.nc.dram_tensor("d_" + name, list(shape), dt, kind="Internal").ap()

    def buf(self, name="b"):
        self.uid += 1
        return Buf("%s_%d" % (name, self.uid))

    def _need(self, eng, reads, writes, skip_dma_waw=None):
        need = {}

        def add(ev):
            if ev is None:
                return
            if ev[0] == "c":
                if ev[1] == eng and eng == "pe":
                    return
                key = ("c", ev[1])
                val = ev[2]
                sem = self.sem[ev[1]]
            else:
                b = ev[1]
                key = ("d", b.uid)
                val = b.dcount
                sem = b.dsem
            if key not in need or need[key][1] < val:
                need[key] = (sem, val)

        for b in reads:
            add(b.w)
        for b in writes:
            if not (b is skip_dma_waw and b.w is not None and b.w[0] == "d" and b.w[1] is b):
                add(b.w)
            for ev in b.r.values():
                add(ev)
        e = self.eng[eng]
        seen = self.seen[eng]
        for key, (sem, val) in need.items():
            if seen.get(key, 0) < val:
                e.wait_ge(sem, val)
                seen[key] = val

    def op(self, eng, fn, reads=(), writes=()):
        self._need(eng, reads, writes)
        ins = fn(self.eng[eng])
        self.cnt[eng] += 1
        ins.then_inc(self.sem[eng], 1)
        ev = ("c", eng, self.cnt[eng])
        for b in reads:
            b.r[eng] = ev
        for b in writes:
            b.w = ev
            b.r = {}
        return ins

    def dma(self, q, out, in_, reads, dst):
        self._need(q, reads, [dst], skip_dma_waw=dst)
        if dst.dsem is None:
            dst.dsem = self.ges.enter_context(self.nc.semaphore("ds_" + dst.name))
            self.dbufs.append(dst)
            self.nsem += 1
            assert self.nsem < 240, "too many semaphores"
        ins = self.eng[q].dma_start(out=out, in_=in_)
        dst.dcount += 16
        ins.then_inc(dst.dsem, 16)
        ev = ("d", dst)
        for b in reads:
            b.r[("d", dst.uid)] = ev
        dst.w = ev
        dst.r = {}
        return ins

    def finish(self, eng, bufs):
        self._need(eng, bufs, [])

    def barrier(self):
        for e in self.eng:
            for e2 in self.eng:
                if e2 == e or self.cnt[e2] == 0:
                    continue
                key = ("c", e2)
                if self.seen[e].get(key, 0) < self.cnt[e2]:
                    self.eng[e].wait_ge(self.sem[e2], self.cnt[e2])
                    self.seen[e][key] = self.cnt[e2]
            for b in self.dbufs:
                key = ("d", b.uid)
                if self.seen[e].get(key, 0) < b.dcount:
                    self.eng[e].wait_ge(b.dsem, b.dcount)
                    self.seen[e][key] = b.dcount

    @contextlib.contextmanager
    def phase(self):
        old = self.es
        with contextlib.ExitStack() as pes:
            self.es = pes
            try:
                yield
            finally:
                self.barrier()
                self.es = old

    def mm(self, out, lhsT, rhs, start, stop, reads, writes):
        return self.op("pe", lambda e: e.matmul(out, lhsT=lhsT, rhs=rhs, start=start, stop=stop), reads, writes)

    def tr(self, out, in_, ident, reads, writes):
        return self.op("pe", lambda e: e.transpose(out, in_, ident), reads, writes)

    def ts(self, eng, out, in0, s1, s2, op0, op1, reads, writes):
        if op1 is None:
            return self.op(eng, lambda e: e.tensor_scalar(out=out, in0=in0, scalar1=s1, scalar2=None, op0=op0), reads, writes)
        return self.op(eng, lambda e: e.tensor_scalar(out=out, in0=in0, scalar1=s1, scalar2=s2, op0=op0, op1=op1), reads, writes)

    def tt(self, eng, out, in0, in1, op, reads, writes):
        return self.op(eng, lambda e: e.tensor_tensor(out=out, in0=in0, in1=in1, op=op), reads, writes)

    def stt(self, out, in0, scalar, in1, op0, op1, reads, writes):
        return self.op("dve", lambda e: e.scalar_tensor_tensor(out=out, in0=in0, scalar=scalar, in1=in1, op0=op0, op1=op1), reads, writes)

    def act(self, out, in_, func, reads, writes, bias=None, scale=None):
        kw = {}
        if bias is not None:
            kw["bias"] = bias
        if scale is not None:
            kw["scale"] = scale
        return self.op("act", lambda e: e.activation(out=out, in_=in_, func=func, **kw), reads, writes)

    def rsqrt(self, out, in_, eps, reads, writes):
        b = self.eps_ap(eps)
        self.op("act", lambda e: e.activation(out=out, in_=in_, func=AF.Sqrt, bias=b[0:out.shape[0], :]), list(reads) + [self.eps_b], writes)
        self.op("dve", lambda e: e.reciprocal(out=out, in_=out), writes, writes)

    def eps_ap(self, eps):
        key = float(eps)
        if key not in self.eps_tab:
            i = len(self.eps_tab)
            self.memset("dve", self.eps_t[:, i:i + 1], key, [self.eps_b])
            self.eps_tab[key] = i
        i = self.eps_tab[key]
        return self.eps_t[:, i:i + 1]

    def copy(self, eng, out, in_, reads, writes):
        if eng == "act":
            return self.op("act", lambda e: e.activation(out=out, in_=in_, func=AF.Identity), reads, writes)
        return self.op(eng, lambda e: e.tensor_copy(out=out, in_=in_), reads, writes)

    def memset(self, eng, ap, val, writes):
        return self.op(eng, lambda e: e.memset(ap, val), [], writes)


class Rot:
    def __init__(self, k, name, n):
        self.bufs = [k.buf(name) for _ in range(n)]
        self.i = 0
        self.n = n

    def next(self):
        j = self.i % self.n
        self.i += 1
        return j, self.bufs[j]


def build_program(depth=DEPTH, mixers=("ret", "dn", "ret", "dn"), do_mixer=True):
    nc = bass.Bass("TRN2", target_bir_lowering=False)
    es = contextlib.ExitStack()
    with es:
        k = KB(nc, es)
        _emit(nc, k, depth, mixers, do_mixer)
        global LAST_COUNTS
        LAST_COUNTS = dict(k.cnt)
    return nc


def _emit(nc, k, depth, mixers, do_mixer):
    def din(name, shape, dt=F32):
        return nc.dram_tensor(name, list(shape), dt, kind="ExternalInput").ap()

    n_ret = sum(1 for m in mixers[:depth] if m == "ret")
    n_dn = sum(1 for m in mixers[:depth] if m == "dn")
    xT_in = din("xT", [D, NT])
    cc_in = din("cc", [128, KC, 2])
    modw = din("mod_w", [depth, 36, 128, KC, 512])
    modb = din("mod_b", [depth, 128, 144])
    lng = din("ln_g", [128, depth * 3 * KC])
    lnb = din("ln_b", [128, depth * 3 * KC])
    wi = din("ffn_w_in", [depth * 2, FJ, 128, 2, KC, 128])
    wo = din("ffn_w_out", [depth * 2, KC, 128, FJ, 128])
    ident_in = din("ident", [128, 128])
    outT = nc.dram_tensor("outT", [D, SEQ], F32, kind="ExternalOutput").ap()
    if n_ret and do_mixer:
        r_wqk = din("ret_wqk", [n_ret, 2, RH, 2, 128, KC, 128])
        r_wvg = din("ret_wvg", [n_ret, 2, RH, 128, KC, 512])
        r_wo = din("ret_wo", [n_ret, KC, 128, 32, 128])
        r_lg = din("ret_lg", [n_ret, 128, 2 * RH])
        cos_in = din("cosT", [128, SEQ])
        sin_in = din("sinT", [128, SEQ])
        tri_in = din("tri", [128, 2, 128])
        idx_in = din("idx", [128, 4])
    if n_dn and do_mixer:
        d_w = din("dn_w", [n_dn, 96, 128, KC, 128])
        d_wba = din("dn_wba", [n_dn, 4, 128, KC, 128])
        d_cw = din("dn_cw", [n_dn, 128, 64, 5])
        d_gp = din("dn_gp", [n_dn, 32, 2, 2])
        d_nw = din("dn_nw", [n_dn, 128, 128])
        d_wo = din("dn_wo", [n_dn, KC, 128, 32, 128])
        dn_sel = din("dn_sel", [32, 32, 128])
        dn_mneg = din("dn_mneg", [128, 4, 128])
        dn_strict = din("dn_strict", [128, 2, 128])
        dn_bd = din("dn_bd", [128, 128])
        dn_lm = din("dn_lm", [128, 2, 3, 128])
    xT = k.dram("xT_s", [D, NT], F32)
    xT_b = [k.buf("xT%d" % t) for t in range(len(TT))]
    ident = k.sb("ident", [128, 128], F32)
    identb = k.sb("identb", [128, 128], BF16)
    ones = k.sb("ones", [128, 128], F32)
    cst = k.buf("cst")
    k.dma("sp", ident[:], ident_in[:, :], [], cst)
    k.copy("dve", identb[:], ident[:], [cst], [cst])
    k.memset("dve", ones[:], 1.0, [cst])
    lng_t = k.sb("lng", [128, depth * 3 * KC], F32)
    lnb_t = k.sb("lnb", [128, depth * 3 * KC], F32)
    k.dma("sp", lng_t[:], lng[:, :], [], cst)
    k.dma("sp", lnb_t[:], lnb[:, :], [], cst)
    cc32 = k.sb("cc32", [128, KC, 2], F32)
    ccb = k.sb("ccb", [128, KC, 2], BF16)
    k.dma("sp", cc32[:], cc_in[:, :, :], [], cst)
    k.act(ccb[:], cc32[:], AF.Silu, [cst], [cst])

    TW = 512
    xt32 = k.sb("xt32", [128, KC, TW], F32)
    xt_b = k.buf("xt32")
    tmpA = k.sb("tmpA", [128, 2, TW], F32)
    tmpA_r = Rot(k, "tmpA", 2)
    tmpD = k.sb("tmpD", [128, 4, TW], F32)
    tmpD_r = Rot(k, "tmpD", 4)
    stat = k.sb("stat", [128, 4, TW], F32)
    stat_b = k.buf("stat")
    wslot = k.sb("wslot", [128, 3, 2 * KC * 128], BF16)
    wslot_r = Rot(k, "wslot", 3)
    modt = k.sb("modt", [128, 144, 2], F32)
    mod_b = k.buf("modt")
    modbias = k.sb("modbias", [128, 144], F32)
    psA = [k.ps("psA%d" % i, [128, 512]) for i in range(7)]
    psA_b = [k.buf("psA%d" % i) for i in range(7)]
    psB = k.ps("psB", [128, 1024], BF16)
    _pb = k.buf("psB")
    psB_b = [_pb, _pb]

    class PsRot:
        def __init__(self, idxs):
            self.idxs = idxs
            self.i = 0

        def next(self):
            j = self.idxs[self.i % len(self.idxs)]
            self.i += 1
            return psA[j], psA_b[j]

    ps_main = PsRot([0, 1, 2, 3])
    ps_y = PsRot([4, 5])
    ps_st = PsRot([6, 4, 5])

    for t, (t0, tw) in enumerate(TT):
        k.dma("sp", xT[:, t0:t0 + tw], xT_in[:, t0:t0 + tw], [], xT_b[t])

    def mod_col(r, kc, which):
        return modt[:, r * KC + kc, which:which + 1]

    def modulation(layer):
        k.dma("sp", modbias[:], modb[layer, :, :], [], mod_b)
        for nb in range(36):
            for half in range(2):
                s, sb_ = wslot_r.next()
                wv = wslot[:, s, :].rearrange("p (kc c) -> p kc c", kc=KC)
                k.dma("pool", wv, modw[layer, nb, :, :, half * 256:(half + 1) * 256], [], sb_)
                for cch in range(2):
                    j = nb * 4 + half * 2 + cch
                    pt, pb = ps_st.next()
                    for kc in range(KC):
                        k.mm(pt[:, 0:2], wv[:, kc, cch * 128:(cch + 1) * 128], ccb[:, kc, :], kc == 0, kc == KC - 1, [sb_, cst], [pb])
                    r = j // KC
                    if r in (1, 4, 7):
                        k.ts("dve", modt[:, j, :], pt[:, 0:2], modbias[:, j:j + 1], 1.0, ALU.add, ALU.add, [pb, mod_b], [mod_b])
                    elif r in (2, 8):
                        k.ts("dve", modt[:, j, :], pt[:, 0:2], modbias[:, j:j + 1], 0.5, ALU.add, ALU.mult, [pb, mod_b], [mod_b])
                    else:
                        k.ts("dve", modt[:, j, :], pt[:, 0:2], modbias[:, j:j + 1], None, ALU.add, None, [pb, mod_b], [mod_b])

    def load_x(t):
        t0, tw = TT[t]
        k.dma("sp", xt32[:, :, 0:tw], xT[:, t0:t0 + tw].rearrange("(kc p) t -> p kc t", p=128), [xT_b[t]], xt_b)

    def store_x(t, final=False):
        t0, tw = TT[t]
        if final:
            k.dma("sp", outT[:, t0:t0 + tw].rearrange("(kc p) t -> p kc t", p=128), xt32[:, :, 0:tw], [xt_b], out_b)
        else:
            k.dma("sp", xT[:, t0:t0 + tw].rearrange("(kc p) t -> p kc t", p=128), xt32[:, :, 0:tw], [xt_b], xT_b[t])

    def modulate(t, r_shift, which, dst, dst_b, col0=0):
        t0, tw = TT[t]
        for kc in range(KC):
            eng = "dve" if kc % 2 == 0 else "pool"
            k.ts(eng, dst[:, kc, col0:col0 + tw], xt32[:, kc, 0:tw], mod_col(r_shift + 1, kc, which), mod_col(r_shift, kc, which),
                 ALU.mult, ALU.add, [xt_b, mod_b], [dst_b])

    def layer_norm(t, lnidx):
        t0, tw = TT[t]
        pm, pmb = ps_st.next()
        pq, pqb = ps_st.next()
        for kc in range(KC):
            k.mm(pm[:, 0:tw], ones[:], xt32[:, kc, 0:tw], kc == 0, kc == KC - 1, [xt_b, cst], [pmb])
        for kc in range(KC):
            s, sb_ = tmpA_r.next()
            k.act(tmpA[:, s, 0:tw], xt32[:, kc, 0:tw], AF.Square, [xt_b], [sb_])
            k.mm(pq[:, 0:tw], ones[:], tmpA[:, s, 0:tw], kc == 0, kc == KC - 1, [sb_, cst], [pqb])
        mean = stat[:, 0, 0:tw]
        var = stat[:, 1, 0:tw]
        rstd = stat[:, 2, 0:tw]
        msq = stat[:, 3, 0:tw]
        k.ts("dve", mean, pm[:, 0:tw], 1.0 / D, None, ALU.mult, None, [pmb], [stat_b])
        k.tt("dve", msq, mean, mean, ALU.mult, [stat_b], [stat_b])
        k.stt(var, pq[:, 0:tw], 1.0 / D, msq, ALU.mult, ALU.subtract, [pqb, stat_b], [stat_b])
        k.rsqrt(rstd, var, LN_EPS, [stat_b], [stat_b])
        for kc in range(KC):
            col = lnidx * KC + kc
            e1 = "dve" if kc % 2 == 0 else "pool"
            k.tt(e1, xt32[:, kc, 0:tw], xt32[:, kc, 0:tw], mean, ALU.subtract, [stat_b, xt_b], [xt_b])
            k.tt(e1, xt32[:, kc, 0:tw], xt32[:, kc, 0:tw], rstd, ALU.mult, [stat_b, xt_b], [xt_b])
            k.act(xt32[:, kc, 0:tw], xt32[:, kc, 0:tw], AF.Identity, [xt_b, cst], [xt_b],
                  bias=lnb_t[:, col:col + 1], scale=lng_t[:, col:col + 1])

    def ffn_phase(layer, f, rbase, lnidx, tiles, final=False):
        with k.phase():
            hT = k.sb("hT", [128, KC, TW], BF16)
            hT_b = k.buf("hT")
            hid = k.sb("hid", [128, FJ, TW], BF16)
            hid_b = [k.buf("hid%d" % j) for j in range(FJ)]
            wo_slot = k.sb("woslot", [128, 2, FJ * 128], BF16)
            wo_r = Rot(k, "woslot", 2)
            wl = layer * 2 + f
            for t in tiles:
                which = 1 if t == 4 else 0
                t0, tw = TT[t]
                load_x(t)
                modulate(t, rbase, which, hT, hT_b)
                for j in range(FJ):
                    s_, sb_ = wslot_r.next()
                    wv = wslot[:, s_, :].rearrange("p (g kc c) -> p g kc c", g=2, kc=KC)
                    k.dma("pool", wv, wi[wl, j, :, :, :, :], [], sb_)
                    pg, pgb = ps_main.next()
                    pu, pub = ps_main.next()
                    for kc in range(KC):
                        k.mm(pg[:, 0:tw], wv[:, 0, kc, :], hT[:, kc, 0:tw], kc == 0, kc == KC - 1, [sb_, hT_b], [pgb])
                    for kc in range(KC):
                        k.mm(pu[:, 0:tw], wv[:, 1, kc, :], hT[:, kc, 0:tw], kc == 0, kc == KC - 1, [sb_, hT_b], [pub])
                    a_, ab = tmpA_r.next()
                    k.act(tmpA[:, a_, 0:tw], pg[:, 0:tw], AF.Silu, [pgb], [ab])
                    k.tt("dve", hid[:, j, 0:tw], tmpA[:, a_, 0:tw], pu[:, 0:tw], ALU.mult, [ab, pub], [hid_b[j]])
                for n in range(KC):
                    s_, sb_ = wo_r.next()
                    wv = wo_slot[:, s_, :].rearrange("p (j c) -> p j c", j=FJ)
                    k.dma("pool", wv, wo[wl, n, :, :, :], [], sb_)
                    py, pyb = ps_y.next()
                    for j in range(FJ):
                        k.mm(py[:, 0:tw], wv[:, j, :], hid[:, j, 0:tw], j == 0, j == FJ - 1, [sb_, hid_b[j]], [pyb])
                    d_, db = tmpD_r.next()
                    k.ts("dve", tmpD[:, d_, 0:tw], py[:, 0:tw], mod_col(rbase + 2, n, which), None, ALU.mult, None, [pyb, mod_b], [db])
                    k.stt(xt32[:, n, 0:tw], xt32[:, n, 0:tw], ALPHA, tmpD[:, d_, 0:tw], ALU.mult, ALU.add, [db, xt_b], [xt_b])
                layer_norm(t, lnidx)
                store_x(t, final)

    def out_proj(layer, w_ap, yT_s, yT_b, last):
        M, A_ = ALU.mult, ALU.add
        with k.phase():
            yTt = k.sb("yTt", [128, 32, TW], BF16)
            yTt_b = k.buf("yTt")
            for t in ([0, 1, 2, 3] if last else [0, 1, 2, 3, 4]):
                which = 1 if t == 4 else 0
                t0, tw = TT[t]
                load_x(t)
                k.dma("sp", yTt[:, :, 0:tw], yT_s[:, :, t0:t0 + tw].rearrange("e p t -> p e t"), [yT_b], yTt_b)
                for n in range(KC):
                    s_, sb_ = wslot_r.next()
                    wv = wslot[:, s_, :].rearrange("p (e c) -> p e c", e=32)
                    k.dma("pool", wv, w_ap[n, :, :, :], [], sb_)
                    py, pyb = ps_y.next()
                    for ec in range(32):
                        k.mm(py[:, 0:tw], wv[:, ec, :], yTt[:, ec, 0:tw], ec == 0, ec == 31, [sb_, yTt_b], [pyb])
                    d_, db = tmpD_r.next()
                    k.ts("dve", tmpD[:, d_, 0:tw], py[:, 0:tw], mod_col(5, n, which), None, M, None, [pyb, mod_b], [db])
                    k.stt(xt32[:, n, 0:tw], xt32[:, n, 0:tw], ALPHA, tmpD[:, d_, 0:tw], M, A_, [db, xt_b], [xt_b])
                layer_norm(t, layer * 3 + 1)
                store_x(t)

    def retention(layer, ri, last):
        NTL = 18
        M, A_, SB = ALU.mult, ALU.add, ALU.subtract
        qT_s = k.dram("r_qT%d" % ri, [RH, 2, 128, NT], BF16)
        kT_s = k.dram("r_kT%d" % ri, [RH, 2, 128, NT], BF16)
        v_s = k.dram("r_v%d" % ri, [RH, NTL, 128, 512], BF16)
        g_s = k.dram("r_g%d" % ri, [RH, NTL, 128, 512], BF16)
        yT_s = k.dram("r_yT%d" % ri, [32, 128, NT], BF16)
        hd_b = [k.buf("rhd%d_%d" % (ri, h)) for h in range(RH)]
        yT_b = k.buf("ryT%d" % ri)
        with k.phase():
            hTa = k.sb("hTa", [128, KC, NT], BF16)
            hTa_b = k.buf("hTa")
            cosT = k.sb("cosT", [128, SEQ], F32)
            sinT = k.sb("sinT", [128, SEQ], F32)
            tab_b = k.buf("tab")
            k.dma("sp", cosT[:], cos_in[:, :], [], tab_b)
            k.dma("sp", sinT[:], sin_in[:, :], [], tab_b)
            wbig = k.sb("wbig", [128, 2, KC * 512], BF16)
            wbig_r = Rot(k, "wbig", 2)
            stg = k.sb("stg", [128, 2, 1024], BF16)
            stg_r = Rot(k, "stg", 2)
            for t in range(5):
                load_x(t)
                modulate(t, 3, 1 if t == 4 else 0, hTa, hTa_b, col0=TT[t][0])
            for qk in range(2):
                dst = qT_s if qk == 0 else kT_s
                for h in range(RH):
                    s_, sb_ = wslot_r.next()
                    wv = wslot[:, s_, :].rearrange("p (dc kc c) -> p dc kc c", dc=2, kc=KC)
                    k.dma("pool", wv, r_wqk[ri, qk, h].rearrange("dc p kc c -> p dc kc c"), [], sb_)
                    for t in range(5):
                        t0, tw = TT[t]
                        p1, p1b = ps_main.next()
                        p2, p2b = ps_main.next()
                        for kc in range(KC):
                            k.mm(p1[:, 0:tw], wv[:, 0, kc, :], hTa[:, kc, t0:t0 + tw], kc == 0, kc == KC - 1, [sb_, hTa_b], [p1b])
                        for kc in range(KC):
                            k.mm(p2[:, 0:tw], wv[:, 1, kc, :], hTa[:, kc, t0:t0 + tw], kc == 0, kc == KC - 1, [sb_, hTa_b], [p2b])
                        g_, gb = stg_r.next()
                        o1 = stg[:, g_, 0:tw]
                        o2 = stg[:, g_, 512:512 + tw]
                        if t < 4:
                            cs = cosT[:, t0:t0 + tw]
                            sn = sinT[:, t0:t0 + tw]
                            a_, ab = tmpD_r.next()
                            b_, bb = tmpD_r.next()
                            k.tt("dve", tmpD[:, a_, 0:tw], p1[:, 0:tw], cs, M, [p1b, tab_b], [ab])
                            k.tt("dve", tmpD[:, b_, 0:tw], p2[:, 0:tw], sn, M, [p2b, tab_b], [bb])
                            k.tt("pool", o1, tmpD[:, a_, 0:tw], tmpD[:, b_, 0:tw], SB, [ab, bb], [gb])
                            c_, cb = tmpD_r.next()
                            d_, db = tmpD_r.next()
                            k.tt("dve", tmpD[:, c_, 0:tw], p1[:, 0:tw], sn, M, [p1b, tab_b], [cb])
                            k.tt("dve", tmpD[:, d_, 0:tw], p2[:, 0:tw], cs, M, [p2b, tab_b], [db])
                            k.tt("pool", o2, tmpD[:, c_, 0:tw], tmpD[:, d_, 0:tw], A_, [cb, db], [gb])
                        else:
                            k.copy("act", o1, p1[:, 0:tw], [p1b], [gb])
                            k.copy("act", o2, p2[:, 0:tw], [p2b], [gb])
                        k.dma("sp", dst[h, :, :, t0:t0 + tw].rearrange("dc p t -> p dc t"),
                              stg[:, g_, :].rearrange("p (dc t) -> p dc t", dc=2)[:, :, 0:tw], [gb], hd_b[h])
            for vg in range(2):
                dst = v_s if vg == 0 else g_s
                for h in range(RH):
                    s_, sb_ = wbig_r.next()
                    wv = wbig[:, s_, :].rearrange("p (kc c) -> p kc c", kc=KC)
                    k.dma("pool", wv, r_wvg[ri, vg, h, :, :, :], [], sb_)
                    for tt_ in range(NTL):
                        c0 = tt_ * 128
                        pp, ppb = ps_main.next()
                        for kc in range(KC):
                            k.mm(pp[:, :], hTa[:, kc, c0:c0 + 128], wv[:, kc, :], kc == 0, kc == KC - 1, [sb_, hTa_b], [ppb])
                        g_, gb = stg_r.next()
                        if vg == 0:
                            k.copy("dve" if tt_ % 2 else "act", stg[:, g_, 0:512], pp[:, :], [ppb], [gb])
                        else:
                            k.act(stg[:, g_, 0:512], pp[:, :], AF.Silu, [ppb], [gb])
                        k.dma("sp", dst[h, tt_, :, :], stg[:, g_, 0:512], [gb], hd_b[h])
        with k.phase():
            dec = k.sb("dec", [128, 5, 16], F32)
            lgt = k.sb("lgt", [128, 16], F32)
            idx = k.sb("idx", [128, 4], F32)
            tri = k.sb("tri", [128, 2, 128], F32)
            dec_b = k.buf("dec")
            k.dma("sp", lgt[:], r_lg[ri, :, :], [], dec_b)
            k.dma("sp", idx[:], idx_in[:, :], [], dec_b)
            k.dma("sp", tri[:], tri_in[:, :, :], [], dec_b)
            for a in range(4):
                k.ts("dve", dec[:, a, :], lgt[:], idx[:, a:a + 1], None, M, None, [dec_b], [dec_b])
            k.ts("dve", dec[:, 4, :], lgt[:], 128.0, None, M, None, [dec_b], [dec_b])
            k.act(dec[:], dec[:], AF.Exp, [dec_b], [dec_b])
            qkt = k.sb("qkt", [128, 2, 2, NT], BF16)
            vt = k.sb("vt", [128, NTL, 512], BF16)
            ld_b = k.buf("rld")
            ktf = k.sb("ktf", [128, NTL, 256], BF16)
            ktb = k.sb("ktb", [128, NTL, 256], BF16)
            kt_b = k.buf("kt")
            oacc = k.sb("oacc", [128, NTL, 512], F32)
            oacc_b = [k.buf("oacc") for _ in range(NTL)]
            msk = k.sb("msk", [128, 2, 128], F32)
            msk_b = k.buf("msk")
            S32 = k.sb("S32", [128, 2, 512], F32)
            Sbf = k.sb("Sbf", [128, 2, 512], BF16)
            S_b = k.buf("S")
            PT = k.sb("PT", [128, 2, 128], BF16)
            PT_r = Rot(k, "PT", 2)
            sgt = k.sb("sgt", [128, 2, 512], BF16)
            sg_r = Rot(k, "sg", 2)
            yt = k.sb("yt", [128, 2, 512], BF16)
            yt_r = Rot(k, "yt", 2)
            yTst = k.sb("yTst", [128, 2, 4, 128], BF16)
            yTst_r = Rot(k, "yTst", 2)
            bst = k.sb("bst", [128, 8], F32)
            bst_b = k.buf("bst")
            for h in range(RH):
                k.dma("sp", qkt[:, 0, :, :], qT_s[h].rearrange("dc p t -> p dc t"), [hd_b[h]], ld_b)
                k.dma("sp", qkt[:, 1, :, :], kT_s[h].rearrange("dc p t -> p dc t"), [hd_b[h]], ld_b)
                k.dma("sp", vt[:, :, :], v_s[h].rearrange("t p e -> p t e"), [hd_b[h]], ld_b)
                k.ts("dve", msk[:, 0, :], tri[:, 0, :], dec[:, 0, h:h + 1], None, M, None, [dec_b], [msk_b])
                k.ts("dve", msk[:, 1, :], tri[:, 1, :], dec[:, 2, 8 + h:9 + h], None, M, None, [dec_b], [msk_b])
                for tt_ in range(NTL):
                    c0 = tt_ * 128
                    pbuf = psB_b[tt_ % 2]
                    pv = psB[:, (tt_ % 2) * 512:(tt_ % 2) * 512 + 256]
                    for dc in range(2):
                        k.tr(pv[:, dc * 128:(dc + 1) * 128], qkt[:, 1, dc, c0:c0 + 128], identb[:], [ld_b, cst], [pbuf])
                    k.ts("dve", ktf[:, tt_, :], pv, dec[:, 0, h:h + 1], None, M, None, [pbuf, dec_b], [kt_b])
                    k.ts("dve", ktb[:, tt_, :], pv, dec[:, 2, 8 + h:9 + h], None, M, None, [pbuf, dec_b], [kt_b])
                for dr in range(2):
                    order = ([16, 17] + list(range(16))) if dr == 0 else ([17, 16] + list(range(15, -1, -1)))
                    kt = ktf if dr == 0 else ktb
                    qd = dec[:, 1, h:h + 1] if dr == 0 else dec[:, 3, 8 + h:9 + h]
                    cd = dec[:, 4, h:h + 1] if dr == 0 else dec[:, 4, 8 + h:9 + h]
                    first = True
                    for oi, tt_ in enumerate(order):
                        c0 = tt_ * 128
                        pS, pSb = ps_st.next()
                        for dc in range(2):
                            k.mm(pS[:, 0:128], qkt[:, 1, dc, c0:c0 + 128], qkt[:, 0, dc, c0:c0 + 128], dc == 0, dc == 1, [ld_b], [pSb])
                        p_, pb_ = PT_r.next()
                        k.tt("dve", PT[:, p_, :], pS[:, 0:128], msk[:, dr, :], M, [pSb, msk_b], [pb_])
                        po, pob = ps_main.next()
                        k.mm(po[:, :], PT[:, p_, :], vt[:, tt_, :], True, first, [pb_, ld_b], [pob])
                        if not first:
                            for dc in range(2):
                                k.mm(po[:, :], qkt[:, 0, dc, c0:c0 + 128], Sbf[:, dc, :], False, dc == 1, [ld_b, S_b], [pob])
                        if dr == 0:
                            k.ts("dve", oacc[:, tt_, :], po[:, :], qd, None, M, None, [pob, dec_b], [oacc_b[tt_]])
                        else:
                            k.stt(oacc[:, tt_, :], po[:, :], qd, oacc[:, tt_, :], M, A_, [pob, dec_b, oacc_b[tt_]], [oacc_b[tt_]])
                        if oi < NTL - 1:
                            for dc in range(2):
                                pd, pdb = ps_y.next()
                                k.mm(pd[:, :], kt[:, tt_, dc * 128:(dc + 1) * 128], vt[:, tt_, :], True, True, [kt_b, ld_b], [pdb])
                                if first:
                                    k.ts("dve", S32[:, dc, :], pd[:, :], cd, None, M, None, [pdb, dec_b], [S_b])
                                    k.copy("act", Sbf[:, dc, :], S32[:, dc, :], [S_b], [S_b])
                                else:
                                    k.tt("dve", S32[:, dc, :], pd[:, :], S32[:, dc, :], A_, [pdb, S_b], [S_b])
                                    k.act(Sbf[:, dc, :], S32[:, dc, :], AF.Identity, [S_b, dec_b], [S_b], scale=cd)
                                    k.ts("dve", S32[:, dc, :], S32[:, dc, :], cd, None, M, None, [S_b, dec_b], [S_b])
                        first = False
                for tt_ in range(NTL):
                    c0 = tt_ * 128
                    k.op("dve", lambda e: e.bn_stats(out=bst[:, 0:6], in_=oacc[:, tt_, :]), [oacc_b[tt_]], [bst_b])
                    k.op("dve", lambda e: e.bn_aggr(out=bst[:, 6:8], in_=bst[:, 0:6]), [bst_b], [bst_b])
                    k.rsqrt(bst[:, 7:8], bst[:, 7:8], RET_EPS, [bst_b], [bst_b])
                    a_, ab = tmpD_r.next()
                    k.ts("dve", tmpD[:, a_, :], oacc[:, tt_, :], bst[:, 6:7], bst[:, 7:8], SB, M, [oacc_b[tt_], bst_b], [ab])
                    s_, sb2 = sg_r.next()
                    k.dma("sp", sgt[:, s_, :], g_s[h, tt_, :, :], [hd_b[h]], sb2)
                    y_, yb = yt_r.next()
                    k.tt("dve", yt[:, y_, :], tmpD[:, a_, :], sgt[:, s_, :], M, [ab, sb2], [yb])
                    pbuf = psB_b[tt_ % 2]
                    pv = psB[:, (tt_ % 2) * 512:(tt_ % 2 + 1) * 512]
                    for ec in range(4):
                        k.tr(pv[:, ec * 128:(ec + 1) * 128], yt[:, y_, ec * 128:(ec + 1) * 128], identb[:], [yb, cst], [pbuf])
                    z_, zb = yTst_r.next()
                    k.copy("act", yTst[:, z_, :, :], pv.rearrange("p (e c) -> p e c", e=4), [pbuf], [zb])
                    k.dma("sp", yT_s[h * 4:(h + 1) * 4, :, c0:c0 + 128].rearrange("e p t -> p e t"), yTst[:, z_, :, :], [zb], yT_b)
        out_proj(layer, r_wo[ri], yT_s, yT_b, last)

    def deltanet(layer, di, last):
        NTL = 18
        M, A_, SB = ALU.mult, ALU.add, ALU.subtract
        qk_s = k.dram("n_qk%d" % di, [32, 128, NT], BF16)
        v_s = k.dram("n_v%d" % di, [NH, NTL, 128, 128], BF16)
        z_s = k.dram("n_z%d" % di, [NH, NTL, 128, 128], BF16)
        yT_s = k.dram("n_yT%d" % di, [32, 128, NT], BF16)
        qk_b = k.buf("nqk%d" % di)
        vz_b = k.buf("nvz%d" % di)
        yT_b = k.buf("nyT%d" % di)
        LOFF, COFF, RAWW = 2, 2054, 2316
        with k.phase():
            cols = k.sb("cols", [128, NTL, 4, 32], F32)
            cols_b = k.buf("cols")
            gcT = k.sb("gcT", [32, 2, NT], F32)
            gcT_b = k.buf("gcT")
            mneg = k.sb("mneg", [128, 4, 128], F32)
            strict = k.sb("strict", [128, 2, 128], F32)
            nwt = k.sb("nwt", [128, 128], F32)
            ctab_b = k.buf("ctab")
            k.dma("sp", mneg[:], dn_mneg[:, :, :], [], ctab_b)
            k.dma("sp", strict[:], dn_strict[:, :, :], [], ctab_b)
            k.dma("sp", nwt[:], d_nw[di, :, :], [], ctab_b)
            with k.phase():
                hTa = k.sb("hTa", [128, KC, NT], BF16)
                hTa_b = k.buf("hTa")
                for t in range(5):
                    load_x(t)
                    modulate(t, 3, 1 if t == 4 else 0, hTa, hTa_b, col0=TT[t][0])
                def project_fm(wsrc, dst_fn, nrows=128):
                    s_, sb_ = wslot_r.next()
                    wv = wslot[:, s_, 0:KC * 128].rearrange("p (kc c) -> p kc c", kc=KC)
                    k.dma("pool", wv, wsrc, [], sb_)
                    for t in range(5):
                        t0, tw = TT[t]
                        pp, ppb = ps_main.next()
                        for kc in range(KC):
                            k.mm(pp[0:nrows, 0:tw], wv[:, kc, 0:nrows], hTa[:, kc, t0:t0 + tw], kc == 0, kc == KC - 1, [sb_, hTa_b], [ppb])
                        dst_fn(t, pp, ppb)

                with k.phase():
                    rawp = k.sb("rawp", [128, RAWW], F32)
                    raw_b = k.buf("rawp")
                    cv = k.sb("cv", [128, NT], F32)
                    cv_b = k.buf("cv")
                    slb = k.sb("slb", [128, NT], BF16)
                    slb_b = k.buf("slb")
                    cw = k.sb("cw", [128, 5], F32)
                    cw_b = k.buf("cw")
                    stg = k.sb("stg", [128, 2, 512], BF16)
                    stg_r = Rot(k, "stg", 2)
                    k.memset("dve", rawp[:], 0.0, [raw_b])

                    def to_raw(t, pp, ppb):
                        t0, tw = TT[t]
                        off = (LOFF + t0) if t < 4 else COFF
                        k.copy("act", rawp[:, off:off + tw], pp[:, 0:tw], [ppb], [raw_b])

                    def conv_silu(blk, out_ap, out_b):
                        k.dma("sp", cw[:], d_cw[di, :, blk, :], [], cw_b)
                        for (o0, r0, L) in ((0, 0, SEQ), (SEQ, COFF - 2, CTX)):
                            k.ts("dve", cv[:, o0:o0 + L], rawp[:, r0:r0 + L], cw[:, 0:1], None, M, None, [raw_b, cw_b], [cv_b])
                            for tap in range(1, 5):
                                k.stt(cv[:, o0:o0 + L], rawp[:, r0 + tap:r0 + tap + L], cw[:, tap:tap + 1], cv[:, o0:o0 + L], M, A_,
                                      [raw_b, cw_b, cv_b], [cv_b])
                        k.act(out_ap, cv[:], AF.Silu, [cv_b], [out_b])

                    def to_tokmajor(src, src_b, dst, h):
                        for g0 in range(0, NTL, 4):
                            ng = min(4, NTL - g0)
                            for i_ in range(ng):
                                c0 = (g0 + i_) * 128
                                k.tr(psB[:, i_ * 128:(i_ + 1) * 128], src[:, c0:c0 + 128], identb[:], [src_b, cst], [psB_b[0]])
                            g_, gb = stg_r.next()
                            k.copy("dve", stg[:, g_, 0:ng * 128], psB[:, 0:ng * 128], [psB_b[0]], [gb])
                            k.dma("sp", dst[h, g0:g0 + ng, :, :].rearrange("t p e -> p t e"),
                                  stg[:, g_, 0:ng * 128].rearrange("p (t e) -> p t e", t=ng), [gb], vz_b)

                    for blk in range(32):
                        project_fm(d_w[di, blk, :, :, :], to_raw)
                        conv_silu(blk, cv[:], cv_b)
                        for t in range(5):
                            t0, tw = TT[t]
                            a_, ab = tmpA_r.next()
                            k.act(tmpA[:, a_, 0:tw], cv[:, t0:t0 + tw], AF.Square, [cv_b], [ab])
                            pq, pqb = ps_st.next()
                            k.mm(pq[:, 0:tw], ones[:], tmpA[:, a_, 0:tw], True, True, [ab, cst], [pqb])
                            d_, db = tmpD_r.next()
                            k.rsqrt(tmpD[:, d_, 0:tw], pq[:, 0:tw], L2_EPS, [pqb], [db])
                            k.tt("dve", slb[:, t0:t0 + tw], cv[:, t0:t0 + tw], tmpD[:, d_, 0:tw], M, [cv_b, db], [slb_b])
                        k.dma("sp", qk_s[blk, :, :], slb[:], [slb_b], qk_b)
                    for h in range(NH):
                        project_fm(d_w[di, 32 + h, :, :, :], to_raw)
                        conv_silu(32 + h, slb[:], slb_b)
                        to_tokmajor(slb, slb_b, v_s, h)
                    for h in range(NH):
                        def z_evac(t, pp, ppb):
                            t0, tw = TT[t]
                            k.act(slb[:, t0:t0 + tw], pp[:, 0:tw], AF.Silu, [ppb], [slb_b])
                        project_fm(d_w[di, 64 + h, :, :, :], z_evac)
                        to_tokmajor(slb, slb_b, z_s, h)
                with k.phase():
                    gt = k.sb("gt", [32, 2, NT], F32)
                    gt_b = k.buf("gt")
                    cs = k.sb("cs", [32, 128], F32)
                    cs_b = k.buf("cs")
                    one_r = k.sb("one_r", [32, 128], F32)
                    pr = k.sb("pr", [32, 4, 2], F32)
                    pr_b = k.buf("pr")
                    k.memset("dve", one_r[:], 1.0, [pr_b])
                    k.dma("sp", pr[:, 0:2, :], d_gp[di, :, :, :], [], pr_b)
                    k.act(pr[:, 2, :], pr[:, 0, :], AF.Exp, [pr_b], [pr_b])
                    k.ts("dve", pr[:, 2, :], pr[:, 2, :], -1.0, None, M, None, [pr_b], [pr_b])
                    for dr in range(2):
                        for q2 in range(2):
                            def g_evac(t, pp, ppb, q2=q2):
                                t0, tw = TT[t]
                                k.copy("act", gt[:, q2, t0:t0 + tw], pp[0:32, 0:tw], [ppb], [gt_b])
                            project_fm(d_wba[di, 2 * dr + q2, :, :, :], g_evac, nrows=32)
                        bsl = gt[:, 0, :]
                        asl = gt[:, 1, :]
                        k.act(bsl, bsl, AF.Exp, [gt_b], [gt_b], scale=-1.0)
                        k.ts("dve", bsl, bsl, 1.0, None, A_, None, [gt_b], [gt_b])
                        k.op("dve", lambda e: e.reciprocal(out=bsl, in_=bsl), [gt_b], [gt_b])
                        k.act(asl, asl, AF.Exp, [gt_b, pr_b], [gt_b], bias=pr[:, 1, dr:dr + 1])
                        k.act(asl, asl, AF.Ln, [gt_b], [gt_b], bias=1.0)
                        k.ts("dve", asl, asl, pr[:, 2, dr:dr + 1], None, M, None, [gt_b, pr_b], [gt_b])
                        for tt_ in range(NTL):
                            c0 = tt_ * 128
                            k.op("dve", lambda e: e.tensor_tensor_scan(out=cs[:, :], data0=one_r[:], data1=asl[:, c0:c0 + 128],
                                                                       initial=0.0, op0=M, op1=A_), [gt_b, pr_b], [cs_b])
                            if dr == 0:
                                k.copy("dve", gcT[:, 0, c0:c0 + 128], cs[:, :], [cs_b], [gcT_b])
                            else:
                                k.tt("dve", gcT[:, 1, c0:c0 + 128], asl[:, c0:c0 + 128], cs[:, :], SB, [gt_b, cs_b], [gcT_b])
                                k.ts("dve", gcT[:, 1, c0:c0 + 128], gcT[:, 1, c0:c0 + 128], cs[:, 127:128], None, A_, None,
                                     [gcT_b, cs_b], [gcT_b])
                        for tt_ in range(NTL):
                            c0 = tt_ * 128
                            pt_, ptb = ps_st.next()
                            k.tr(pt_[:, 0:32], bsl[:, c0:c0 + 128], ident[0:32, 0:32], [gt_b, cst], [ptb])
                            k.tr(pt_[:, 32:64], gcT[:, dr, c0:c0 + 128], ident[0:32, 0:32], [gcT_b, cst], [ptb])
                            k.copy("dve", cols[:, tt_, 2 * dr:2 * dr + 2, :], pt_[:, 0:64].rearrange("p (a h) -> p a h", a=2), [ptb], [cols_b])
            with k.phase():
                sel = k.sb("sel", [32, 32, 128], F32)
                k.dma("sp", sel[:], dn_sel[:, :, :], [], ctab_b)
                qT = k.sb("qT", [128, NT], BF16)
                kT = k.sb("kT", [128, NT], BF16)
                vt = k.sb("vt", [128, NTL, 128], BF16)
                ld_b = k.buf("nld")
                oacc = k.sb("oacc", [128, NTL, 128], F32)
                oacc_b = [k.buf("noacc") for _ in range(NTL)]
                mats = k.sb("mats", [128, 17, 128], F32)
                mb_ = [k.buf("mat%d" % i) for i in range(17)]
                bd16 = k.sb("bd16", [128, 128], F32)
                lmk = k.sb("lmk", [128, 2, 3, 128], F32)
                k.dma("sp", bd16[:], dn_bd[:, :], [], ctab_b)
                k.dma("sp", lmk[:], dn_lm[:, :, :, :], [], ctab_b)
                matb = k.sb("matb", [128, 8, 128], BF16)
                bb_ = [k.buf("matb%d" % i) for i in range(8)]
                S32 = k.sb("S32", [128, 128], F32)
                Sbf = k.sb("Sbf", [128, 128], BF16)
                S_b = k.buf("S")
                sc4 = k.sb("sc4", [128, 8], F32)
                sc_b = k.buf("sc4")
                zt = k.sb("zt", [128, 2, 128], BF16)
                zt_r = Rot(k, "zt", 2)
                yst = k.sb("yst", [128, 2, 128], BF16)
                yst_r = Rot(k, "yst", 2)
                ytk = k.sb("ytk", [128, 128], BF16)
                ytk_b = k.buf("ytk")
                D_, DT_, DN_, N0, M0, R_, NA, MA, NB, MB, U_, EG, T_, P1_, L0, L1, L2 = range(17)
                VB, KBG, WT, VN, QG, PT_, KO, KTK = range(8)

                def mat(i):
                    return mats[:, i, :]

                for h in range(NH):
                    kh = h // 2
                    if h % 2 == 0:
                        k.dma("sp", qT[:], qk_s[kh, :, :], [qk_b], ld_b)
                        k.dma("sp", kT[:], qk_s[16 + kh, :, :], [qk_b], ld_b)
                    k.dma("sp", vt[:, :, :], v_s[h].rearrange("t p e -> p t e"), [vz_b], ld_b)
                    for dr in range(2):
                        order = ([16, 17] + list(range(16))) if dr == 0 else ([17, 16] + list(range(15, -1, -1)))
                        first = True
                        for oi, tt_ in enumerate(order):
                            c0 = tt_ * 128
                            bcol = cols[:, tt_, 2 * dr, h:h + 1]
                            gcol = cols[:, tt_, 2 * dr + 1, h:h + 1]
                            lastpos = 127 if dr == 0 else 0
                            pbc, pbcb = ps_st.next()
                            k.mm(pbc[:, 0:128], sel[:, h, :], gcT[:, dr, c0:c0 + 128], True, True, [ctab_b, gcT_b], [pbcb])
                            k.ts("dve", mat(D_), pbc[:, 0:128], -1.0, gcol, M, A_, [pbcb, cols_b], [mb_[D_]])
                            k.tt("dve", mat(D_), mat(D_), mneg[:, dr, :], A_, [mb_[D_], ctab_b], [mb_[D_]])
                            k.act(mat(D_), mat(D_), AF.Exp, [mb_[D_]], [mb_[D_]])
                            k.tt("pool", mat(DN_), mat(D_), strict[:, dr, :], M, [mb_[D_], ctab_b], [mb_[DN_]])
                            k.ts("dve", mat(DT_), pbc[:, 0:128], gcol, None, SB, None, [pbcb, cols_b], [mb_[DT_]])
                            k.tt("dve", mat(DT_), mat(DT_), mneg[:, 2 + dr, :], A_, [mb_[DT_], ctab_b], [mb_[DT_]])
                            k.act(mat(DT_), mat(DT_), AF.Exp, [mb_[DT_]], [mb_[DT_]])
                            k.act(mat(EG), pbc[:, 0:128], AF.Exp, [pbcb], [mb_[EG]])
                            k.copy("dve", sc4[:, 0:1], pbc[:, lastpos:lastpos + 1], [pbcb], [sc_b])
                            k.act(sc4[:, 1:2], sc4[:, 0:1], AF.Exp, [sc_b], [sc_b])
                            k.act(sc4[:, 2:3], gcol, AF.Exp, [cols_b, sc_b], [sc_b], scale=-1.0, bias=sc4[:, 0:1])
                            k.act(sc4[:, 3:4], gcol, AF.Exp, [cols_b], [sc_b])
                            k.tt("dve", sc4[:, 4:5], sc4[:, 3:4], bcol, M, [sc_b, cols_b], [sc_b])
                            pkk, pkkb = ps_st.next()
                            k.mm(pkk[:, 0:128], kT[:, c0:c0 + 128], kT[:, c0:c0 + 128], True, True, [ld_b], [pkkb])
                            k.stt(mat(N0), pkk[:, 0:128], bcol, mat(DN_), M, M, [pkkb, cols_b, mb_[DN_]], [mb_[N0]])
                            ptr, ptrb = ps_st.next()
                            k.tr(ptr[:, 0:128], mat(N0), ident[:], [mb_[N0], cst], [ptrb])
                            k.copy("act", mat(M0), ptr[:, 0:128], [ptrb], [mb_[M0]])
                            k.tt("pool", mat(NA), mat(N0), bd16[:], M, [mb_[N0], ctab_b], [mb_[NA]])
                            k.tt("pool", mat(MA), mat(M0), bd16[:], M, [mb_[M0], ctab_b], [mb_[MA]])
                            for lv in range(3):
                                k.tt("pool", mat(L0 + lv), mat(N0), lmk[:, dr, lv, :], M, [mb_[N0], ctab_b], [mb_[L0 + lv]])
                            k.tt("dve", mat(R_), ident[:], mat(MA), SB, [cst, mb_[MA]], [mb_[R_]])
                            cn, cm = NA, MA
                            for sq in range(3):
                                nn, nm = (NB, MB) if sq % 2 == 0 else (NA, MA)
                                pn, pnb = ps_main.next()
                                k.mm(pn[:, 0:128], mat(cm), mat(cn), True, True, [mb_[cm], mb_[cn]], [pnb])
                                if sq < 2:
                                    pm_, pmb_ = ps_main.next()
                                    k.mm(pm_[:, 0:128], mat(cn), mat(cm), True, True, [mb_[cm], mb_[cn]], [pmb_])
                                k.copy("act", mat(nn), pn[:, 0:128], [pnb], [mb_[nn]])
                                if sq < 2:
                                    k.copy("dve", mat(nm), pm_[:, 0:128], [pmb_], [mb_[nm]])
                                pr_, prb_ = ps_y.next()
                                k.mm(pr_[:, 0:128], mat(nn), mat(R_), True, True, [mb_[nn], mb_[R_]], [prb_])
                                k.tt("dve", mat(R_), mat(R_), pr_[:, 0:128], A_, [mb_[R_], prb_], [mb_[R_]])
                                cn, cm = nn, nm
                            for lv in range(3):
                                pt2, pt2b = ps_st.next()
                                k.tr(pt2[:, 0:128], mat(R_), ident[:], [mb_[R_], cst], [pt2b])
                                k.copy("act", mat(T_), pt2[:, 0:128], [pt2b], [mb_[T_]])
                                p1, p1b = ps_main.next()
                                k.mm(p1[:, 0:128], mat(L0 + lv), mat(R_), True, True, [mb_[L0 + lv], mb_[R_]], [p1b])
                                k.copy("dve", mat(P1_), p1[:, 0:128], [p1b], [mb_[P1_]])
                                p2, p2b = ps_y.next()
                                k.mm(p2[:, 0:128], mat(T_), mat(P1_), True, True, [mb_[T_], mb_[P1_]], [p2b])
                                k.tt("dve", mat(R_), mat(R_), p2[:, 0:128], SB, [mb_[R_], p2b], [mb_[R_]])
                            k.ts("dve", mat(MA), vt[:, tt_, :], bcol, None, M, None, [ld_b, cols_b], [mb_[MA]])
                            pu_, pub_ = ps_main.next()
                            k.mm(pu_[:, 0:128], mat(R_), mat(MA), True, True, [mb_[R_], mb_[MA]], [pub_])
                            k.copy("act", mat(U_), pu_[:, 0:128], [pub_], [mb_[U_]])
                            k.tr(psB[:, 0:128], kT[:, c0:c0 + 128], identb[:], [ld_b, cst], [psB_b[0]])
                            k.ts("dve", mat(MB), psB[:, 0:128], sc4[:, 4:5], None, M, None, [psB_b[0], sc_b], [mb_[MB]])
                            k.ts("dve", matb[:, KO, :], psB[:, 0:128], sc4[:, 2:3], None, M, None, [psB_b[0], sc_b], [bb_[KO]])
                            pw_, pwb_ = ps_main.next()
                            k.mm(pw_[:, 0:128], mat(MB), mat(R_), True, True, [mb_[MB], mb_[R_]], [pwb_])
                            k.copy("act", matb[:, WT, :], pw_[:, 0:128], [pwb_], [bb_[WT]])
                            if first:
                                k.copy("dve", matb[:, VN, :], mat(U_), [mb_[U_]], [bb_[VN]])
                            else:
                                pv_, pvb_ = ps_main.next()
                                k.mm(pv_[:, 0:128], matb[:, WT, :], Sbf[:], True, True, [bb_[WT], S_b], [pvb_])
                                k.tt("dve", matb[:, VN, :], mat(U_), pv_[:, 0:128], SB, [mb_[U_], pvb_], [bb_[VN]])
                            pqk, pqkb = ps_st.next()
                            k.mm(pqk[:, 0:128], kT[:, c0:c0 + 128], qT[:, c0:c0 + 128], True, True, [ld_b], [pqkb])
                            k.tt("dve", matb[:, PT_, :], pqk[:, 0:128], mat(DT_), M, [pqkb, mb_[DT_]], [bb_[PT_]])
                            po, pob = ps_y.next()
                            k.mm(po[:, 0:128], matb[:, PT_, :], matb[:, VN, :], True, first, [bb_[PT_], bb_[VN]], [pob])
                            if not first:
                                k.tt("pool", matb[:, QG, :], qT[:, c0:c0 + 128], mat(EG), M, [ld_b, mb_[EG]], [bb_[QG]])
                                k.mm(po[:, 0:128], matb[:, QG, :], Sbf[:], False, True, [bb_[QG], S_b], [pob])
                            if dr == 0:
                                k.copy("act", oacc[:, tt_, :], po[:, 0:128], [pob], [oacc_b[tt_]])
                            else:
                                k.tt("dve", oacc[:, tt_, :], oacc[:, tt_, :], po[:, 0:128], A_, [pob, oacc_b[tt_]], [oacc_b[tt_]])
                            if oi < NTL - 1:
                                pd, pdb = ps_y.next()
                                k.mm(pd[:, 0:128], matb[:, KO, :], matb[:, VN, :], True, True, [bb_[KO], bb_[VN]], [pdb])
                                if first:
                                    k.copy("dve", S32[:], pd[:, 0:128], [pdb], [S_b])
                                else:
                                    k.stt(S32[:], S32[:], sc4[:, 1:2], pd[:, 0:128], M, A_, [pdb, sc_b, S_b], [S_b])
                                k.copy("act", Sbf[:], S32[:], [S_b], [S_b])
                            first = False
                    for tt_ in range(NTL):
                        c0 = tt_ * 128
                        a_, ab = tmpA_r.next()
                        k.act(tmpA[:, a_, 0:128], oacc[:, tt_, :], AF.Square, [oacc_b[tt_]], [ab])
                        k.op("dve", lambda e: e.tensor_reduce(out=sc4[:, 5:6], in_=tmpA[:, a_, 0:128], axis=mybir.AxisListType.X, op=A_), [ab], [sc_b])
                        k.ts("dve", sc4[:, 6:7], sc4[:, 5:6], 1.0 / 128.0, None, M, None, [sc_b], [sc_b])
                        k.rsqrt(sc4[:, 6:7], sc4[:, 6:7], DN_EPS, [sc_b], [sc_b])
                        d_, db = tmpD_r.next()
                        k.stt(tmpD[:, d_, 0:128], oacc[:, tt_, :], sc4[:, 6:7], nwt[:], M, M, [oacc_b[tt_], sc_b, ctab_b], [db])
                        z_, zb = zt_r.next()
                        k.dma("sp", zt[:, z_, :], z_s[h, tt_, :, :], [vz_b], zb)
                        k.tt("dve", ytk[:], tmpD[:, d_, 0:128], zt[:, z_, :], M, [db, zb], [ytk_b])
                        k.tr(psB[:, 0:128], ytk[:], identb[:], [ytk_b, cst], [psB_b[0]])
                        y_, yb = yst_r.next()
                        k.copy("act", yst[:, y_, :], psB[:, 0:128], [psB_b[0]], [yb])
                        k.dma("sp", yT_s[h, :, c0:c0 + 128], yst[:, y_, :], [yb], yT_b)
        out_proj(layer, d_wo[di], yT_s, yT_b, last)

    out_b = k.buf("out")

    ri = 0
    di = 0
    for layer in range(depth):
        last = layer == depth - 1
        modulation(layer)
        ffn_phase(layer, 0, 0, layer * 3 + 0, [0, 1, 2, 3, 4])
        if do_mixer:
            if mixers[layer] == "ret":
                retention(layer, ri, last)
                ri += 1
            else:
                deltanet(layer, di, last)
                di += 1
        ffn_phase(layer, 1, 6, layer * 3 + 2, [0, 1, 2, 3] if last else [0, 1, 2, 3, 4], final=last)
    k.finish("sp", [out_b])


def _fm(v):
    return np.ascontiguousarray(v.reshape(KC, 128).T)


def prep_shared(inputs, depth=DEPTH):
    f = np.float32
    sh = {}
    mw = inputs["mod_w"][:depth]
    sh["mod_w"] = np.ascontiguousarray(mw.reshape(depth, KC, 128, 36, 512).transpose(0, 3, 2, 1, 4))
    sh["mod_b"] = np.ascontiguousarray(inputs["mod_b"][:depth].reshape(depth, 144, 128).transpose(0, 2, 1))
    sh["ln_g"] = np.ascontiguousarray(inputs["ln_g"][:depth].reshape(depth * 3 * KC, 128).T)
    sh["ln_b"] = np.ascontiguousarray(inputs["ln_b"][:depth].reshape(depth * 3 * KC, 128).T)
    w_in = inputs["ffn_w_in"][:depth].reshape(depth * 2, KC, 128, 2, FJ, 128)
    sh["ffn_w_in"] = np.ascontiguousarray(w_in.transpose(0, 4, 2, 3, 1, 5))
    w_out = inputs["ffn_w_out"][:depth].reshape(depth * 2, FJ, 128, KC, 128)
    sh["ffn_w_out"] = np.ascontiguousarray(w_out.transpose(0, 3, 2, 1, 4))
    sh["ident"] = np.eye(128, dtype=f)
    return sh


def prep_ret(inputs, n_ret):
    f = np.float32
    sh = {}
    W = inputs["ret_w_in"][:n_ret]
    qk = W[:, :, 0:4096].reshape(n_ret, KC, 128, 2, RH, 128, 2)
    sh["ret_wqk"] = np.ascontiguousarray(qk.transpose(0, 3, 4, 6, 2, 1, 5))
    vg = W[:, :, 4096:12288].reshape(n_ret, KC, 128, 2, RH, 512)
    sh["ret_wvg"] = np.ascontiguousarray(vg.transpose(0, 3, 4, 2, 1, 5))
    wo_ = inputs["ret_w_out"][:n_ret].reshape(n_ret, 32, 128, KC, 128)
    sh["ret_wo"] = np.ascontiguousarray(wo_.transpose(0, 3, 2, 1, 4))
    lg = inputs["ret_log_decay"][:n_ret].reshape(n_ret, 1, 2 * RH)
    sh["ret_lg"] = np.ascontiguousarray(np.broadcast_to(lg, (n_ret, 128, 2 * RH))).astype(f)
    tok = np.arange(SEQ)
    pos_r = (tok // 64).astype(f)
    pos_c = (tok % 64).astype(f)
    inv = (10000.0 ** (-np.arange(0, 128, 2, dtype=f) / 128.0)).astype(f)
    ang = np.concatenate([pos_r[:, None] * inv, pos_c[:, None] * inv], -1).astype(f)
    sh["cosT"] = np.ascontiguousarray(np.cos(ang).T.astype(f))
    sh["sinT"] = np.ascontiguousarray(np.sin(ang).T.astype(f))
    j = np.arange(128)[:, None]
    i = np.arange(128)[None, :]
    sh["tri"] = np.ascontiguousarray(np.stack([(j <= i), (j >= i)], axis=1).astype(f))
    p = np.arange(128, dtype=f)
    sh["idx"] = np.ascontiguousarray(np.stack([-(p + 1), p + 1, p - 128, 128 - p], axis=1).astype(f))
    return sh


def prep_dn(inputs, n_dn):
    f = np.float32
    sh = {}
    W = inputs["dn_w_in"][:n_dn]
    blk = W[:, :, 0:12288].reshape(n_dn, KC, 128, 96, 128)
    sh["dn_w"] = np.ascontiguousarray(blk.transpose(0, 3, 2, 1, 4))
    ba = W[:, :, 12288:12416].reshape(n_dn, KC, 128, 4, 32)
    bap = np.zeros((n_dn, 4, 128, KC, 128), f)
    bap[:, :, :, :, 0:32] = ba.transpose(0, 3, 2, 1, 4)
    sh["dn_wba"] = bap
    cwv = inputs["dn_conv_w"][:n_dn].reshape(n_dn, 5, 64, 128)
    sh["dn_cw"] = np.ascontiguousarray(cwv.transpose(0, 3, 2, 1))
    gp = np.stack([inputs["dn_a_log"][:n_dn], inputs["dn_dt_bias"][:n_dn]], axis=1)
    sh["dn_gp"] = np.ascontiguousarray(gp.transpose(0, 3, 1, 2)).astype(f)
    nw = inputs["dn_norm_w"][:n_dn].reshape(n_dn, 1, 128)
    sh["dn_nw"] = np.ascontiguousarray(np.broadcast_to(nw, (n_dn, 128, 128))).astype(f)
    wo_ = inputs["dn_w_out"][:n_dn].reshape(n_dn, 32, 128, KC, 128)
    sh["dn_wo"] = np.ascontiguousarray(wo_.transpose(0, 3, 2, 1, 4))
    sel = np.zeros((32, 32, 128), f)
    for h in range(32):
        sel[h, h, :] = 1.0
    sh["dn_sel"] = sel
    i = np.arange(128)[:, None]
    j = np.arange(128)[None, :]
    NEG = -30000.0
    low = np.where(i >= j, 0.0, NEG)
    up = np.where(i <= j, 0.0, NEG)
    sh["dn_mneg"] = np.ascontiguousarray(np.stack([low, up, up, low], axis=1).astype(f))
    sh["dn_strict"] = np.ascontiguousarray(np.stack([(i > j), (i < j)], axis=1).astype(f))
    sh["dn_bd"] = np.ascontiguousarray((i // 16 == j // 16).astype(f))
    lm = np.zeros((128, 2, 3, 128), f)
    for lv, s_ in enumerate((16, 32, 64)):
        same = (i // (2 * s_)) == (j // (2 * s_))
        lo = same & ((i % (2 * s_)) >= s_) & ((j % (2 * s_)) < s_)
        lm[:, 0, lv, :] = lo
        lm[:, 1, lv, :] = lo.T
    sh["dn_lm"] = lm
    return sh


def prep_core(inputs, b):
    xT = np.concatenate([inputs["x"][b].T, inputs["ctx"][b].T], axis=1)
    cc = np.stack([_fm(inputs["c"][b]), _fm(inputs["c_ctx"])], axis=-1)
    return {"xT": np.ascontiguousarray(xT, dtype=np.float32), "cc": np.ascontiguousarray(cc, dtype=np.float32)}


def kernel(**inputs):
    inputs = {k_: np.asarray(v) for k_, v in inputs.items()}
    nc = build_program()
    sh = prep_shared(inputs)
    sh.update(prep_ret(inputs, 2))
    sh.update(prep_dn(inputs, 2))
    B = inputs["x"].shape[0]
    in_maps = []
    for b in range(B):
        m = dict(sh)
        m.update(prep_core(inputs, b))
        in_maps.append(m)
    res = run_bass_kernel_spmd(nc, in_maps, core_ids=list(range(B)))
    out = np.stack([np.ascontiguousarray(res.results[b]["outT"].T) for b in range(B)], axis=0)
    return out.astype(np.float32)
```

```python
import contextlib
import numpy as np
import concourse.bass as bass
import concourse.mybir as mybir
from concourse.bass_utils import run_bass_kernel_spmd

F32 = mybir.dt.float32
BF16 = mybir.dt.bfloat16
AF = mybir.ActivationFunctionType
ALU = mybir.AluOpType

D = 2048
KC = 16
SEQ = 2048
CTX = 256
NT = SEQ + CTX
DEPTH = 4
FH = 5504
FJ = 43
ALPHA = float((2 * DEPTH) ** 0.25)
LN_EPS = 1e-5
TT = [(0, 512), (512, 512), (1024, 512), (1536, 512), (2048, 256)]
RH = 8
RET_EPS = LN_EPS * 256.0
NH = 32
DN_EPS = 1e-6 * 128.0
L2_EPS = 1e-6


LAST_COUNTS = {}


_BUF_UID = [0]


class Buf:
    __slots__ = ("name", "w", "r", "dsem", "dcount", "uid")

    def __init__(self, name):
        _BUF_UID[0] += 1
        self.uid = _BUF_UID[0]
        self.name = name
        self.w = None
        self.r = {}
        self.dsem = None
        self.dcount = 0


class KB:
    def __init__(self, nc, es):
        self.nc = nc
        self.es = es
        self.eng = {"pe": nc.tensor, "dve": nc.vector, "act": nc.scalar, "pool": nc.gpsimd, "sp": nc.sync}
        self.sem = {}
        self.cnt = {}
        self.seen = {}
        for e in self.eng:
            self.sem[e] = es.enter_context(nc.semaphore("sem_" + e))
            self.cnt[e] = 0
            self.seen[e] = {}
        self.nsem = len(self.eng)
        self.uid = 0
        self.dbufs = []
        self.ges = es
        self.eps_tab = {}
        self.eps_t = es.enter_context(nc.sbuf_tensor("s_epsconst", [128, 8], F32))
        self.eps_b = Buf("epsconst")

    def sb(self, name, shape, dt):
        self.uid += 1
        return self.es.enter_context(self.nc.sbuf_tensor("s_%s_%d" % (name, self.uid), list(shape), dt))

    def ps(self, name, shape, dt=F32):
        return self.es.enter_context(self.nc.psum_tensor("p_" + name, list(shape), dt))

    def dram(self, name, shape, dt):
        return self.nc.dram_tensor("d_" + name, list(shape), dt, kind="Internal").ap()

    def buf(self, name="b"):
        self.uid += 1
        return Buf("%s_%d" % (name, self.uid))

    def _need(self, eng, reads, writes, skip_dma_waw=None):
        need = {}

        def add(ev):
            if ev is None:
                return
            if ev[0] == "c":
                if ev[1] == eng and eng == "pe":
                    return
                key = ("c", ev[1])
                val = ev[2]
                sem = self.sem[ev[1]]
            else:
                b = ev[1]
                key = ("d", b.uid)
                val = b.dcount
                sem = b.dsem
            if key not in need or need[key][1] < val:
                need[key] = (sem, val)

        for b in reads:
            add(b.w)
        for b in writes:
            if not (b is skip_dma_waw and b.w is not None and b.w[0] == "d" and b.w[1] is b):
                add(b.w)
            for ev in b.r.values():
                add(ev)
        e = self.eng[eng]
        seen = self.seen[eng]
        for key, (sem, val) in need.items():
            if seen.get(key, 0) < val:
                e.wait_ge(sem, val)
                seen[key] = val

    def op(self, eng, fn, reads=(), writes=()):
        self._need(eng, reads, writes)
        ins = fn(self.eng[eng])
        self.cnt[eng] += 1
        ins.then_inc(self.sem[eng], 1)
        ev = ("c", eng, self.cnt[eng])
        for b in reads:
            b.r[eng] = ev
        for b in writes:
            b.w = ev
            b.r = {}
        return ins

    def dma(self, q, out, in_, reads, dst):
        self._need(q, reads, [dst], skip_dma_waw=dst)
        if dst.dsem is None:
            dst.dsem = self.ges.enter_context(self.nc.semaphore("ds_" + dst.name))
            self.dbufs.append(dst)
            self.nsem += 1
            assert self.nsem < 240, "too many semaphores"
        ins = self.eng[q].dma_start(out=out, in_=in_)
        dst.dcount += 16
        ins.then_inc(dst.dsem, 16)
        ev = ("d", dst)
        for b in reads:
            b.r[("d", dst.uid)] = ev
        dst.w = ev
        dst.r = {}
        return ins

    def finish(self, eng, bufs):
        self._need(eng, bufs, [])

    def barrier(self):
        for e in self.eng:
            for e2 in self.eng:
                if e2 == e or self.cnt[e2] == 0:
                    continue
                key = ("c", e2)
                if self.seen[e].get(key, 0) < self.cnt[e2]:
                    self.eng[e].wait_ge(self.sem[e2], self.cnt[e2])
                    self.seen[e][key] = self.cnt[e2]
            for b in self.dbufs:
                key = ("d", b.uid)
                if self.seen[e].get(key, 0) < b.dcount:
                    self.eng[e].wait_ge(b.dsem, b.dcount)
                    self.seen[e][key] = b.dcount

    @contextlib.contextmanager
    def phase(self):
        old = self.es
        with contextlib.ExitStack() as pes:
            self.es = pes
            try:
                yield
            finally:
                self.barrier()
                self.es = old

    def mm(self, out, lhsT, rhs, start, stop, reads, writes):
        return self.op("pe", lambda e: e.matmul(out, lhsT=lhsT, rhs=rhs, start=start, stop=stop), reads, writes)

    def tr(self, out, in_, ident, reads, writes):
        return self.op("pe", lambda e: e.transpose(out, in_, ident), reads, writes)

    def ts(self, eng, out, in0, s1, s2, op0, op1, reads, writes):
        if op1 is None:
            return self.op(eng, lambda e: e.tensor_scalar(out=out, in0=in0, scalar1=s1, scalar2=None, op0=op0), reads, writes)
        return self.op(eng, lambda e: e.tensor_scalar(out=out, in0=in0, scalar1=s1, scalar2=s2, op0=op0, op1=op1), reads, writes)

    def tt(self, eng, out, in0, in1, op, reads, writes):
        return self.op(eng, lambda e: e.tensor_tensor(out=out, in0=in0, in1=in1, op=op), reads, writes)

    def stt(self, out, in0, scalar, in1, op0, op1, reads, writes):
        return self.op("dve", lambda e: e.scalar_tensor_tensor(out=out, in0=in0, scalar=scalar, in1=in1, op0=op0, op1=op1), reads, writes)

    def act(self, out, in_, func, reads, writes, bias=None, scale=None):
        kw = {}
        if bias is not None:
            kw["bias"] = bias
        if scale is not None:
            kw["scale"] = scale
        return self.op("act", lambda e: e.activation(out=out, in_=in_, func=func, **kw), reads, writes)

    def rsqrt(self, out, in_, eps, reads, writes):
        b = self.eps_ap(eps)
        self.op("act", lambda e: e.activation(out=out, in_=in_, func=AF.Sqrt, bias=b[0:out.shape[0], :]), list(reads) + [self.eps_b], writes)
        self.op("dve", lambda e: e.reciprocal(out=out, in_=out), writes, writes)

    def eps_ap(self, eps):
        key = float(eps)
        if key not in self.eps_tab:
            i = len(self.eps_tab)
            self.memset("dve", self.eps_t[:, i:i + 1], key, [self.eps_b])
            self.eps_tab[key] = i
        i = self.eps_tab[key]
        return self.eps_t[:, i:i + 1]

    def copy(self, eng, out, in_, reads, writes):
        if eng == "act":
            return self.op("act", lambda e: e.activation(out=out, in_=in_, func=AF.Identity), reads, writes)
        return self.op(eng, lambda e: e.tensor_copy(out=out, in_=in_), reads, writes)

    def memset(self, eng, ap, val, writes):
        return self.op(eng, lambda e: e.memset(ap, val), [], writes)


class Rot:
    def __init__(self, k, name, n):
        self.bufs = [k.buf(name) for _ in range(n)]
        self.i = 0
        self.n = n

    def next(self):
        j = self.i % self.n
        self.i += 1
        return j, self.bufs[j]


def build_program(depth=DEPTH, mixers=("ret", "dn", "ret", "dn"), do_mixer=True):
    nc = bass.Bass("TRN2", target_bir_lowering=False)
    es = contextlib.ExitStack()
    with es:
        k = KB(nc, es)
        _emit(nc, k, depth, mixers, do_mixer)
        global LAST_COUNTS
        LAST_COUNTS = dict(k.cnt)
    return nc


def _emit(nc, k, depth, mixers, do_mixer):
    def din(name, shape, dt=F32):
        return nc.dram_tensor(name, list(shape), dt, kind="ExternalInput").ap()

    n_ret = sum(1 for m in mixers[:depth] if m == "ret")
    n_dn = sum(1 for m in mixers[:depth] if m == "dn")
    xT_in = din("xT", [D, NT])
    cc_in = din("cc", [128, KC, 2])
    modw = din("mod_w", [depth, 36, 128, KC, 512])
    modb = din("mod_b", [depth, 128, 144])
    lng = din("ln_g", [128, depth * 3 * KC])
    lnb = din("ln_b", [128, depth * 3 * KC])
    wi = din("ffn_w_in", [depth * 2, FJ, 128, 2, KC, 128])
    wo = din("ffn_w_out", [depth * 2, KC, 128, FJ, 128])
    ident_in = din("ident", [128, 128])
    outT = nc.dram_tensor("outT", [D, SEQ], F32, kind="ExternalOutput").ap()
    if n_ret and do_mixer:
        r_wqk = din("ret_wqk", [n_ret, 2, RH, 2, 128, KC, 128])
        r_wvg = din("ret_wvg", [n_ret, 2, RH, 128, KC, 512])
        r_wo = din("ret_wo", [n_ret, KC, 128, 32, 128])
        r_lg = din("ret_lg", [n_ret, 128, 2 * RH])
        cos_in = din("cosT", [128, SEQ])
        sin_in = din("sinT", [128, SEQ])
        tri_in = din("tri", [128, 2, 128])
        idx_in = din("idx", [128, 4])
    if n_dn and do_mixer:
        d_w = din("dn_w", [n_dn, 96, 128, KC, 128])
        d_wba = din("dn_wba", [n_dn, 4, 128, KC, 128])
        d_cw = din("dn_cw", [n_dn, 128, 64, 5])
        d_gp = din("dn_gp", [n_dn, 32, 2, 2])
        d_nw = din("dn_nw", [n_dn, 128, 128])
        d_wo = din("dn_wo", [n_dn, KC, 128, 32, 128])
        dn_sel = din("dn_sel", [32, 32, 128])
        dn_mneg = din("dn_mneg", [128, 4, 128])
        dn_strict = din("dn_strict", [128, 2, 128])
        dn_bd = din("dn_bd", [128, 128])
        dn_lm = din("dn_lm", [128, 2, 3, 128])
    xT = k.dram("xT_s", [D, NT], F32)
    xT_b = [k.buf("xT%d" % t) for t in range(len(TT))]
    ident = k.sb("ident", [128, 128], F32)
    identb = k.sb("identb", [128, 128], BF16)
    ones = k.sb("ones", [128, 128], F32)
    cst = k.buf("cst")
    k.dma("sp", ident[:], ident_in[:, :], [], cst)
    k.copy("dve", identb[:], ident[:], [cst], [cst])
    k.memset("dve", ones[:], 1.0, [cst])
    lng_t = k.sb("lng", [128, depth * 3 * KC], F32)
    lnb_t = k.sb("lnb", [128, depth * 3 * KC], F32)
    k.dma("sp", lng_t[:], lng[:, :], [], cst)
    k.dma("sp", lnb_t[:], lnb[:, :], [], cst)
    cc32 = k.sb("cc32", [128, KC, 2], F32)
    ccb = k.sb("ccb", [128, KC, 2], BF16)
    k.dma("sp", cc32[:], cc_in[:, :, :], [], cst)
    k.act(ccb[:], cc32[:], AF.Silu, [cst], [cst])

    TW = 512
    xt32 = k.sb("xt32", [128, KC, TW], F32)
    xt_b = k.buf("xt32")
    tmpA = k.sb("tmpA", [128, 2, TW], F32)
    tmpA_r = Rot(k, "tmpA", 2)
    tmpD = k.sb("tmpD", [128, 4, TW], F32)
    tmpD_r = Rot(k, "tmpD", 4)
    stat = k.sb("stat", [128, 4, TW], F32)
    stat_b = k.buf("stat")
    wslot = k.sb("wslot", [128, 3, 2 * KC * 128], BF16)
    wslot_r = Rot(k, "wslot", 3)
    modt = k.sb("modt", [128, 144, 2], F32)
    mod_b = k.buf("modt")
    modbias = k.sb("modbias", [128, 144], F32)
    psA = [k.ps("psA%d" % i, [128, 512]) for i in range(7)]
    psA_b = [k.buf("psA%d" % i) for i in range(7)]
    psB = k.ps("psB", [128, 1024], BF16)
    _pb = k.buf("psB")
    psB_b = [_pb, _pb]

    class PsRot:
        def __init__(self, idxs):
            self.idxs = idxs
            self.i = 0

        def next(self):
            j = self.idxs[self.i % len(self.idxs)]
            self.i += 1
            return psA[j], psA_b[j]

    ps_main = PsRot([0, 1, 2, 3])
    ps_y = PsRot([4, 5])
    ps_st = PsRot([6, 4, 5])

    for t, (t0, tw) in enumerate(TT):
        k.dma("sp", xT[:, t0:t0 + tw], xT_in[:, t0:t0 + tw], [], xT_b[t])

    def mod_col(r, kc, which):
        return modt[:, r * KC + kc, which:which + 1]

    def modulation(layer):
        k.dma("sp", modbias[:], modb[layer, :, :], [], mod_b)
        for nb in range(36):
            for half in range(2):
                s, sb_ = wslot_r.next()
                wv = wslot[:, s, :].rearrange("p (kc c) -> p kc c", kc=KC)
                k.dma("pool", wv, modw[layer, nb, :, :, half * 256:(half + 1) * 256], [], sb_)
                for cch in range(2):
                    j = nb * 4 + half * 2 + cch
                    pt, pb = ps_st.next()
                    for kc in range(KC):
                        k.mm(pt[:, 0:2], wv[:, kc, cch * 128:(cch + 1) * 128], ccb[:, kc, :], kc == 0, kc == KC - 1, [sb_, cst], [pb])
                    r = j // KC
                    if r in (1, 4, 7):
                        k.ts("dve", modt[:, j, :], pt[:, 0:2], modbias[:, j:j + 1], 1.0, ALU.add, ALU.add, [pb, mod_b], [mod_b])
                    elif r in (2, 8):
                        k.ts("dve", modt[:, j, :], pt[:, 0:2], modbias[:, j:j + 1], 0.5, ALU.add, ALU.mult, [pb, mod_b], [mod_b])
                    else:
                        k.ts("dve", modt[:, j, :], pt[:, 0:2], modbias[:, j:j + 1], None, ALU.add, None, [pb, mod_b], [mod_b])

    def load_x(t):
        t0, tw = TT[t]
        k.dma("sp", xt32[:, :, 0:tw], xT[:, t0:t0 + tw].rearrange("(kc p) t -> p kc t", p=128), [xT_b[t]], xt_b)

    def store_x(t, final=False):
        t0, tw = TT[t]
        if final:
            k.dma("sp", outT[:, t0:t0 + tw].rearrange("(kc p) t -> p kc t", p=128), xt32[:, :, 0:tw], [xt_b], out_b)
        else:
            k.dma("sp", xT[:, t0:t0 + tw].rearrange("(kc p) t -> p kc t", p=128), xt32[:, :, 0:tw], [xt_b], xT_b[t])

    def modulate(t, r_shift, which, dst, dst_b, col0=0):
        t0, tw = TT[t]
        for kc in range(KC):
            if kc % 2 == 0:
                k.ts("dve", dst[:, kc, col0:col0 + tw], xt32[:, kc, 0:tw], mod_col(r_shift + 1, kc, which), mod_col(r_shift, kc, which),
                     ALU.mult, ALU.add, [xt_b, mod_b], [dst_b])
            else:
                k.act(dst[:, kc, col0:col0 + tw], xt32[:, kc, 0:tw], AF.Identity, [xt_b, mod_b], [dst_b],
                      bias=mod_col(r_shift, kc, which), scale=mod_col(r_shift + 1, kc, which))

    def layer_norm(t, lnidx):
        t0, tw = TT[t]
        pm, pmb = ps_st.next()
        pq, pqb = ps_st.next()
        for kc in range(KC):
            k.mm(pm[:, 0:tw], ones[:], xt32[:, kc, 0:tw], kc == 0, kc == KC - 1, [xt_b, cst], [pmb])
        for kc in range(KC):
            s, sb_ = tmpA_r.next()
            k.act(tmpA[:, s, 0:tw], xt32[:, kc, 0:tw], AF.Square, [xt_b], [sb_])
            k.mm(pq[:, 0:tw], ones[:], tmpA[:, s, 0:tw], kc == 0, kc == KC - 1, [sb_, cst], [pqb])
        mean = stat[:, 0, 0:tw]
        var = stat[:, 1, 0:tw]
        rstd = stat[:, 2, 0:tw]
        msq = stat[:, 3, 0:tw]
        k.ts("dve", mean, pm[:, 0:tw], 1.0 / D, None, ALU.mult, None, [pmb], [stat_b])
        k.tt("dve", msq, mean, mean, ALU.mult, [stat_b], [stat_b])
        k.stt(var, pq[:, 0:tw], 1.0 / D, msq, ALU.mult, ALU.subtract, [pqb, stat_b], [stat_b])
        k.rsqrt(rstd, var, LN_EPS, [stat_b], [stat_b])
        for kc in range(KC):
            col = lnidx * KC + kc
            e1 = "dve"
            k.tt(e1, xt32[:, kc, 0:tw], xt32[:, kc, 0:tw], mean, ALU.subtract, [stat_b, xt_b], [xt_b])
            k.tt(e1, xt32[:, kc, 0:tw], xt32[:, kc, 0:tw], rstd, ALU.mult, [stat_b, xt_b], [xt_b])
            k.act(xt32[:, kc, 0:tw], xt32[:, kc, 0:tw], AF.Identity, [xt_b, cst], [xt_b],
                  bias=lnb_t[:, col:col + 1], scale=lng_t[:, col:col + 1])

    def ffn_phase(layer, f, rbase, lnidx, tiles, final=False):
        with k.phase():
            hT = k.sb("hT", [128, KC, TW], BF16)
            hT_b = k.buf("hT")
            hid = k.sb("hid", [128, FJ, TW], BF16)
            hid_b = [k.buf("hid%d" % j) for j in range(FJ)]
            wo_slot = k.sb("woslot", [128, 2, FJ * 128], BF16)
            wo_r = Rot(k, "woslot", 2)
            wl = layer * 2 + f
            for t in tiles:
                which = 1 if t == 4 else 0
                t0, tw = TT[t]
                load_x(t)
                modulate(t, rbase, which, hT, hT_b)
                for j in range(FJ):
                    s_, sb_ = wslot_r.next()
                    wv = wslot[:, s_, :].rearrange("p (g kc c) -> p g kc c", g=2, kc=KC)
                    k.dma("pool", wv, wi[wl, j, :, :, :, :], [], sb_)
                    pg, pgb = ps_main.next()
                    pu, pub = ps_main.next()
                    for kc in range(KC):
                        k.mm(pg[:, 0:tw], wv[:, 0, kc, :], hT[:, kc, 0:tw], kc == 0, kc == KC - 1, [sb_, hT_b], [pgb])
                    for kc in range(KC):
                        k.mm(pu[:, 0:tw], wv[:, 1, kc, :], hT[:, kc, 0:tw], kc == 0, kc == KC - 1, [sb_, hT_b], [pub])
                    a_, ab = tmpA_r.next()
                    k.act(tmpA[:, a_, 0:tw], pg[:, 0:tw], AF.Silu, [pgb], [ab])
                    k.tt("dve", hid[:, j, 0:tw], tmpA[:, a_, 0:tw], pu[:, 0:tw], ALU.mult, [ab, pub], [hid_b[j]])
                for n in range(KC):
                    s_, sb_ = wo_r.next()
                    wv = wo_slot[:, s_, :].rearrange("p (j c) -> p j c", j=FJ)
                    k.dma("pool", wv, wo[wl, n, :, :, :], [], sb_)
                    py, pyb = ps_y.next()
                    for j in range(FJ):
                        k.mm(py[:, 0:tw], wv[:, j, :], hid[:, j, 0:tw], j == 0, j == FJ - 1, [sb_, hid_b[j]], [pyb])
                    d_, db = tmpD_r.next()
                    k.ts("dve", tmpD[:, d_, 0:tw], py[:, 0:tw], mod_col(rbase + 2, n, which), None, ALU.mult, None, [pyb, mod_b], [db])
                    k.stt(xt32[:, n, 0:tw], xt32[:, n, 0:tw], ALPHA, tmpD[:, d_, 0:tw], ALU.mult, ALU.add, [db, xt_b], [xt_b])
                layer_norm(t, lnidx)
                store_x(t, final)

    def out_proj(layer, w_ap, yT_s, yT_b, last):
        M, A_ = ALU.mult, ALU.add
        with k.phase():
            yTt = k.sb("yTt", [128, 32, TW], BF16)
            yTt_b = k.buf("yTt")
            for t in ([0, 1, 2, 3] if last else [0, 1, 2, 3, 4]):
                which = 1 if t == 4 else 0
                t0, tw = TT[t]
                load_x(t)
                k.dma("sp", yTt[:, :, 0:tw], yT_s[:, :, t0:t0 + tw].rearrange("e p t -> p e t"), [yT_b], yTt_b)
                for n in range(KC):
                    s_, sb_ = wslot_r.next()
                    wv = wslot[:, s_, :].rearrange("p (e c) -> p e c", e=32)
                    k.dma("pool", wv, w_ap[n, :, :, :], [], sb_)
                    py, pyb = ps_y.next()
                    for ec in range(32):
                        k.mm(py[:, 0:tw], wv[:, ec, :], yTt[:, ec, 0:tw], ec == 0, ec == 31, [sb_, yTt_b], [pyb])
                    d_, db = tmpD_r.next()
                    k.ts("dve", tmpD[:, d_, 0:tw], py[:, 0:tw], mod_col(5, n, which), None, M, None, [pyb, mod_b], [db])
                    k.stt(xt32[:, n, 0:tw], xt32[:, n, 0:tw], ALPHA, tmpD[:, d_, 0:tw], M, A_, [db, xt_b], [xt_b])
                layer_norm(t, layer * 3 + 1)
                store_x(t)

    def retention(layer, ri, last):
        NTL = 18
        M, A_, SB = ALU.mult, ALU.add, ALU.subtract
        qT_s = k.dram("r_qT%d" % ri, [RH, 2, 128, NT], BF16)
        kT_s = k.dram("r_kT%d" % ri, [RH, 2, 128, NT], BF16)
        v_s = k.dram("r_v%d" % ri, [RH, NTL, 128, 512], BF16)
        g_s = k.dram("r_g%d" % ri, [RH, NTL, 128, 512], BF16)
        yT_s = k.dram("r_yT%d" % ri, [32, 128, NT], BF16)
        hd_b = [k.buf("rhd%d_%d" % (ri, h)) for h in range(RH)]
        yT_b = k.buf("ryT%d" % ri)
        with k.phase():
            hTa = k.sb("hTa", [128, KC, NT], BF16)
            hTa_b = k.buf("hTa")
            cosT = k.sb("cosT", [128, SEQ], F32)
            sinT = k.sb("sinT", [128, SEQ], F32)
            tab_b = k.buf("tab")
            k.dma("sp", cosT[:], cos_in[:, :], [], tab_b)
            k.dma("sp", sinT[:], sin_in[:, :], [], tab_b)
            wbig = k.sb("wbig", [128, 2, KC * 512], BF16)
            wbig_r = Rot(k, "wbig", 2)
            stg = k.sb("stg", [128, 2, 1024], BF16)
            stg_r = Rot(k, "stg", 2)
            for t in range(5):
                load_x(t)
                modulate(t, 3, 1 if t == 4 else 0, hTa, hTa_b, col0=TT[t][0])
            for qk in range(2):
                dst = qT_s if qk == 0 else kT_s
                for h in range(RH):
                    s_, sb_ = wslot_r.next()
                    wv = wslot[:, s_, :].rearrange("p (dc kc c) -> p dc kc c", dc=2, kc=KC)
                    k.dma("pool", wv, r_wqk[ri, qk, h].rearrange("dc p kc c -> p dc kc c"), [], sb_)
                    for t in range(5):
                        t0, tw = TT[t]
                        p1, p1b = ps_main.next()
                        p2, p2b = ps_main.next()
                        for kc in range(KC):
                            k.mm(p1[:, 0:tw], wv[:, 0, kc, :], hTa[:, kc, t0:t0 + tw], kc == 0, kc == KC - 1, [sb_, hTa_b], [p1b])
                        for kc in range(KC):
                            k.mm(p2[:, 0:tw], wv[:, 1, kc, :], hTa[:, kc, t0:t0 + tw], kc == 0, kc == KC - 1, [sb_, hTa_b], [p2b])
                        g_, gb = stg_r.next()
                        o1 = stg[:, g_, 0:tw]
                        o2 = stg[:, g_, 512:512 + tw]
                        if t < 4:
                            cs = cosT[:, t0:t0 + tw]
                            sn = sinT[:, t0:t0 + tw]
                            a_, ab = tmpD_r.next()
                            b_, bb = tmpD_r.next()
                            k.tt("dve", tmpD[:, a_, 0:tw], p1[:, 0:tw], cs, M, [p1b, tab_b], [ab])
                            k.tt("dve", tmpD[:, b_, 0:tw], p2[:, 0:tw], sn, M, [p2b, tab_b], [bb])
                            k.tt("pool", o1, tmpD[:, a_, 0:tw], tmpD[:, b_, 0:tw], SB, [ab, bb], [gb])
                            c_, cb = tmpD_r.next()
                            d_, db = tmpD_r.next()
                            k.tt("dve", tmpD[:, c_, 0:tw], p1[:, 0:tw], sn, M, [p1b, tab_b], [cb])
                            k.tt("dve", tmpD[:, d_, 0:tw], p2[:, 0:tw], cs, M, [p2b, tab_b], [db])
                            k.tt("pool", o2, tmpD[:, c_, 0:tw], tmpD[:, d_, 0:tw], A_, [cb, db], [gb])
                        else:
                            k.copy("act", o1, p1[:, 0:tw], [p1b], [gb])
                            k.copy("act", o2, p2[:, 0:tw], [p2b], [gb])
                        k.dma("sp", dst[h, :, :, t0:t0 + tw].rearrange("dc p t -> p dc t"),
                              stg[:, g_, :].rearrange("p (dc t) -> p dc t", dc=2)[:, :, 0:tw], [gb], hd_b[h])
            for vg in range(2):
                dst = v_s if vg == 0 else g_s
                for h in range(RH):
                    s_, sb_ = wbig_r.next()
                    wv = wbig[:, s_, :].rearrange("p (kc c) -> p kc c", kc=KC)
                    k.dma("pool", wv, r_wvg[ri, vg, h, :, :, :], [], sb_)
                    for tt_ in range(NTL):
                        c0 = tt_ * 128
                        pp, ppb = ps_main.next()
                        for kc in range(KC):
                            k.mm(pp[:, :], hTa[:, kc, c0:c0 + 128], wv[:, kc, :], kc == 0, kc == KC - 1, [sb_, hTa_b], [ppb])
                        g_, gb = stg_r.next()
                        if vg == 0:
                            k.copy("dve" if tt_ % 2 else "act", stg[:, g_, 0:512], pp[:, :], [ppb], [gb])
                        else:
                            k.act(stg[:, g_, 0:512], pp[:, :], AF.Silu, [ppb], [gb])
                        k.dma("sp", dst[h, tt_, :, :], stg[:, g_, 0:512], [gb], hd_b[h])
        with k.phase():
            dec = k.sb("dec", [128, 5, 16], F32)
            lgt = k.sb("lgt", [128, 16], F32)
            idx = k.sb("idx", [128, 4], F32)
            tri = k.sb("tri", [128, 2, 128], F32)
            dec_b = k.buf("dec")
            k.dma("sp", lgt[:], r_lg[ri, :, :], [], dec_b)
            k.dma("sp", idx[:], idx_in[:, :], [], dec_b)
            k.dma("sp", tri[:], tri_in[:, :, :], [], dec_b)
            for a in range(4):
                k.ts("dve", dec[:, a, :], lgt[:], idx[:, a:a + 1], None, M, None, [dec_b], [dec_b])
            k.ts("dve", dec[:, 4, :], lgt[:], 128.0, None, M, None, [dec_b], [dec_b])
            k.act(dec[:], dec[:], AF.Exp, [dec_b], [dec_b])
            qkt = k.sb("qkt", [128, 2, 2, NT], BF16)
            vt = k.sb("vt", [128, NTL, 512], BF16)
            ld_b = k.buf("rld")
            ktf = k.sb("ktf", [128, NTL, 256], BF16)
            ktb = k.sb("ktb", [128, NTL, 256], BF16)
            kt_b = k.buf("kt")
            oacc = k.sb("oacc", [128, NTL, 512], F32)
            oacc_b = [k.buf("oacc") for _ in range(NTL)]
            msk = k.sb("msk", [128, 2, 128], F32)
            msk_b = k.buf("msk")
            S32 = k.sb("S32", [128, 2, 512], F32)
            Sbf = k.sb("Sbf", [128, 2, 512], BF16)
            S_b = k.buf("S")
            PT = k.sb("PT", [128, 2, 128], BF16)
            PT_r = Rot(k, "PT", 2)
            sgt = k.sb("sgt", [128, 2, 512], BF16)
            sg_r = Rot(k, "sg", 2)
            yt = k.sb("yt", [128, 2, 512], BF16)
            yt_r = Rot(k, "yt", 2)
            yTst = k.sb("yTst", [128, 2, 4, 128], BF16)
            yTst_r = Rot(k, "yTst", 2)
            bst = k.sb("bst", [128, 8], F32)
            bst_b = k.buf("bst")
            for h in range(RH):
                k.dma("sp", qkt[:, 0, :, :], qT_s[h].rearrange("dc p t -> p dc t"), [hd_b[h]], ld_b)
                k.dma("sp", qkt[:, 1, :, :], kT_s[h].rearrange("dc p t -> p dc t"), [hd_b[h]], ld_b)
                k.dma("sp", vt[:, :, :], v_s[h].rearrange("t p e -> p t e"), [hd_b[h]], ld_b)
                k.ts("dve", msk[:, 0, :], tri[:, 0, :], dec[:, 0, h:h + 1], None, M, None, [dec_b], [msk_b])
                k.ts("dve", msk[:, 1, :], tri[:, 1, :], dec[:, 2, 8 + h:9 + h], None, M, None, [dec_b], [msk_b])
                for tt_ in range(NTL):
                    c0 = tt_ * 128
                    pbuf = psB_b[tt_ % 2]
                    pv = psB[:, (tt_ % 2) * 512:(tt_ % 2) * 512 + 256]
                    for dc in range(2):
                        k.tr(pv[:, dc * 128:(dc + 1) * 128], qkt[:, 1, dc, c0:c0 + 128], identb[:], [ld_b, cst], [pbuf])
                    k.ts("dve", ktf[:, tt_, :], pv, dec[:, 0, h:h + 1], None, M, None, [pbuf, dec_b], [kt_b])
                    k.ts("dve", ktb[:, tt_, :], pv, dec[:, 2, 8 + h:9 + h], None, M, None, [pbuf, dec_b], [kt_b])
                for dr in range(2):
                    order = ([16, 17] + list(range(16))) if dr == 0 else ([17, 16] + list(range(15, -1, -1)))
                    kt = ktf if dr == 0 else ktb
                    qd = dec[:, 1, h:h + 1] if dr == 0 else dec[:, 3, 8 + h:9 + h]
                    cd = dec[:, 4, h:h + 1] if dr == 0 else dec[:, 4, 8 + h:9 + h]
                    first = True
                    for oi, tt_ in enumerate(order):
                        c0 = tt_ * 128
                        pS, pSb = ps_st.next()
                        for dc in range(2):
                            k.mm(pS[:, 0:128], qkt[:, 1, dc, c0:c0 + 128], qkt[:, 0, dc, c0:c0 + 128], dc == 0, dc == 1, [ld_b], [pSb])
                        p_, pb_ = PT_r.next()
                        k.tt("dve", PT[:, p_, :], pS[:, 0:128], msk[:, dr, :], M, [pSb, msk_b], [pb_])
                        po, pob = ps_main.next()
                        k.mm(po[:, :], PT[:, p_, :], vt[:, tt_, :], True, first, [pb_, ld_b], [pob])
                        if not first:
                            for dc in range(2):
                                k.mm(po[:, :], qkt[:, 0, dc, c0:c0 + 128], Sbf[:, dc, :], False, dc == 1, [ld_b, S_b], [pob])
                        if dr == 0:
                            k.ts("dve", oacc[:, tt_, :], po[:, :], qd, None, M, None, [pob, dec_b], [oacc_b[tt_]])
                        else:
                            k.stt(oacc[:, tt_, :], po[:, :], qd, oacc[:, tt_, :], M, A_, [pob, dec_b, oacc_b[tt_]], [oacc_b[tt_]])
                        if oi < NTL - 1:
                            for dc in range(2):
                                pd, pdb = ps_y.next()
                                k.mm(pd[:, :], kt[:, tt_, dc * 128:(dc + 1) * 128], vt[:, tt_, :], True, True, [kt_b, ld_b], [pdb])
                                if first:
                                    k.ts("dve", S32[:, dc, :], pd[:, :], cd, None, M, None, [pdb, dec_b], [S_b])
                                    k.copy("act", Sbf[:, dc, :], S32[:, dc, :], [S_b], [S_b])
                                else:
                                    k.tt("dve", S32[:, dc, :], pd[:, :], S32[:, dc, :], A_, [pdb, S_b], [S_b])
                                    k.act(Sbf[:, dc, :], S32[:, dc, :], AF.Identity, [S_b, dec_b], [S_b], scale=cd)
                                    k.ts("dve", S32[:, dc, :], S32[:, dc, :], cd, None, M, None, [S_b, dec_b], [S_b])
                        first = False
                for tt_ in range(NTL):
                    c0 = tt_ * 128
                    k.op("dve", lambda e: e.bn_stats(out=bst[:, 0:6], in_=oacc[:, tt_, :]), [oacc_b[tt_]], [bst_b])
                    k.op("dve", lambda e: e.bn_aggr(out=bst[:, 6:8], in_=bst[:, 0:6]), [bst_b], [bst_b])
                    k.rsqrt(bst[:, 7:8], bst[:, 7:8], RET_EPS, [bst_b], [bst_b])
                    a_, ab = tmpD_r.next()
                    k.ts("dve", tmpD[:, a_, :], oacc[:, tt_, :], bst[:, 6:7], bst[:, 7:8], SB, M, [oacc_b[tt_], bst_b], [ab])
                    s_, sb2 = sg_r.next()
                    k.dma("sp", sgt[:, s_, :], g_s[h, tt_, :, :], [hd_b[h]], sb2)
                    y_, yb = yt_r.next()
                    k.tt("dve", yt[:, y_, :], tmpD[:, a_, :], sgt[:, s_, :], M, [ab, sb2], [yb])
                    pbuf = psB_b[tt_ % 2]
                    pv = psB[:, (tt_ % 2) * 512:(tt_ % 2 + 1) * 512]
                    for ec in range(4):
                        k.tr(pv[:, ec * 128:(ec + 1) * 128], yt[:, y_, ec * 128:(ec + 1) * 128], identb[:], [yb, cst], [pbuf])
                    z_, zb = yTst_r.next()
                    k.copy("act", yTst[:, z_, :, :], pv.rearrange("p (e c) -> p e c", e=4), [pbuf], [zb])
                    k.dma("sp", yT_s[h * 4:(h + 1) * 4, :, c0:c0 + 128].rearrange("e p t -> p e t"), yTst[:, z_, :, :], [zb], yT_b)
        out_proj(layer, r_wo[ri], yT_s, yT_b, last)

    def deltanet(layer, di, last):
        NTL = 18
        M, A_, SB = ALU.mult, ALU.add, ALU.subtract
        qk_s = k.dram("n_qk%d" % di, [32, 128, NT], BF16)
        v_s = k.dram("n_v%d" % di, [NH, NTL, 128, 128], BF16)
        z_s = k.dram("n_z%d" % di, [NH, NTL, 128, 128], BF16)
        yT_s = k.dram("n_yT%d" % di, [32, 128, NT], BF16)
        qk_b = k.buf("nqk%d" % di)
        vz_b = k.buf("nvz%d" % di)
        yT_b = k.buf("nyT%d" % di)
        LOFF, COFF, RAWW = 2, 2054, 2316
        with k.phase():
            cols = k.sb("cols", [128, NTL, 4, 32], F32)
            cols_b = k.buf("cols")
            gcT = k.sb("gcT", [32, 2, NT], F32)
            gcT_b = k.buf("gcT")
            mneg = k.sb("mneg", [128, 4, 128], F32)
            strict = k.sb("strict", [128, 2, 128], F32)
            nwt = k.sb("nwt", [128, 128], F32)
            ctab_b = k.buf("ctab")
            k.dma("sp", mneg[:], dn_mneg[:, :, :], [], ctab_b)
            k.dma("sp", strict[:], dn_strict[:, :, :], [], ctab_b)
            k.dma("sp", nwt[:], d_nw[di, :, :], [], ctab_b)
            with k.phase():
                hTa = k.sb("hTa", [128, KC, NT], BF16)
                hTa_b = k.buf("hTa")
                for t in range(5):
                    load_x(t)
                    modulate(t, 3, 1 if t == 4 else 0, hTa, hTa_b, col0=TT[t][0])
                def project_fm(wsrc, dst_fn, nrows=128):
                    s_, sb_ = wslot_r.next()
                    wv = wslot[:, s_, 0:KC * 128].rearrange("p (kc c) -> p kc c", kc=KC)
                    k.dma("pool", wv, wsrc, [], sb_)
                    for t in range(5):
                        t0, tw = TT[t]
                        pp, ppb = ps_main.next()
                        for kc in range(KC):
                            k.mm(pp[0:nrows, 0:tw], wv[:, kc, 0:nrows], hTa[:, kc, t0:t0 + tw], kc == 0, kc == KC - 1, [sb_, hTa_b], [ppb])
                        dst_fn(t, pp, ppb)

                with k.phase():
                    rawp = k.sb("rawp", [128, RAWW], F32)
                    raw_b = k.buf("rawp")
                    cv = k.sb("cv", [128, NT], F32)
                    cv_b = k.buf("cv")
                    slb = k.sb("slb", [128, NT], BF16)
                    slb_b = k.buf("slb")
                    cw = k.sb("cw", [128, 5], F32)
                    cw_b = k.buf("cw")
                    stg = k.sb("stg", [128, 2, 512], BF16)
                    stg_r = Rot(k, "stg", 2)
                    k.memset("dve", rawp[:], 0.0, [raw_b])

                    def to_raw(t, pp, ppb):
                        t0, tw = TT[t]
                        off = (LOFF + t0) if t < 4 else COFF
                        k.copy("act", rawp[:, off:off + tw], pp[:, 0:tw], [ppb], [raw_b])

                    def conv_silu(blk, out_ap, out_b):
                        k.dma("sp", cw[:], d_cw[di, :, blk, :], [], cw_b)
                        for (o0, r0, L) in ((0, 0, SEQ), (SEQ, COFF - 2, CTX)):
                            k.ts("dve", cv[:, o0:o0 + L], rawp[:, r0:r0 + L], cw[:, 0:1], None, M, None, [raw_b, cw_b], [cv_b])
                            for tap in range(1, 5):
                                k.stt(cv[:, o0:o0 + L], rawp[:, r0 + tap:r0 + tap + L], cw[:, tap:tap + 1], cv[:, o0:o0 + L], M, A_,
                                      [raw_b, cw_b, cv_b], [cv_b])
                        k.act(out_ap, cv[:], AF.Silu, [cv_b], [out_b])

                    def to_tokmajor(src, src_b, dst, h):
                        for g0 in range(0, NTL, 4):
                            ng = min(4, NTL - g0)
                            for i_ in range(ng):
                                c0 = (g0 + i_) * 128
                                k.tr(psB[:, i_ * 128:(i_ + 1) * 128], src[:, c0:c0 + 128], identb[:], [src_b, cst], [psB_b[0]])
                            g_, gb = stg_r.next()
                            k.copy("dve", stg[:, g_, 0:ng * 128], psB[:, 0:ng * 128], [psB_b[0]], [gb])
                            k.dma("sp", dst[h, g0:g0 + ng, :, :].rearrange("t p e -> p t e"),
                                  stg[:, g_, 0:ng * 128].rearrange("p (t e) -> p t e", t=ng), [gb], vz_b)

                    for blk in range(32):
                        project_fm(d_w[di, blk, :, :, :], to_raw)
                        conv_silu(blk, cv[:], cv_b)
                        for t in range(5):
                            t0, tw = TT[t]
                            a_, ab = tmpA_r.next()
                            k.act(tmpA[:, a_, 0:tw], cv[:, t0:t0 + tw], AF.Square, [cv_b], [ab])
                            pq, pqb = ps_st.next()
                            k.mm(pq[:, 0:tw], ones[:], tmpA[:, a_, 0:tw], True, True, [ab, cst], [pqb])
                            d_, db = tmpD_r.next()
                            k.rsqrt(tmpD[:, d_, 0:tw], pq[:, 0:tw], L2_EPS, [pqb], [db])
                            k.tt("dve", slb[:, t0:t0 + tw], cv[:, t0:t0 + tw], tmpD[:, d_, 0:tw], M, [cv_b, db], [slb_b])
                        k.dma("sp", qk_s[blk, :, :], slb[:], [slb_b], qk_b)
                    for h in range(NH):
                        project_fm(d_w[di, 32 + h, :, :, :], to_raw)
                        conv_silu(32 + h, slb[:], slb_b)
                        to_tokmajor(slb, slb_b, v_s, h)
                    for h in range(NH):
                        def z_evac(t, pp, ppb):
                            t0, tw = TT[t]
                            k.act(slb[:, t0:t0 + tw], pp[:, 0:tw], AF.Silu, [ppb], [slb_b])
                        project_fm(d_w[di, 64 + h, :, :, :], z_evac)
                        to_tokmajor(slb, slb_b, z_s, h)
                with k.phase():
                    gt = k.sb("gt", [32, 2, NT], F32)
                    gt_b = k.buf("gt")
                    cs = k.sb("cs", [32, 128], F32)
                    cs_b = k.buf("cs")
                    one_r = k.sb("one_r", [32, 128], F32)
                    pr = k.sb("pr", [32, 4, 2], F32)
                    pr_b = k.buf("pr")
                    k.memset("dve", one_r[:], 1.0, [pr_b])
                    k.dma("sp", pr[:, 0:2, :], d_gp[di, :, :, :], [], pr_b)
                    k.act(pr[:, 2, :], pr[:, 0, :], AF.Exp, [pr_b], [pr_b])
                    k.ts("dve", pr[:, 2, :], pr[:, 2, :], -1.0, None, M, None, [pr_b], [pr_b])
                    for dr in range(2):
                        for q2 in range(2):
                            def g_evac(t, pp, ppb, q2=q2):
                                t0, tw = TT[t]
                                k.copy("act", gt[:, q2, t0:t0 + tw], pp[0:32, 0:tw], [ppb], [gt_b])
                            project_fm(d_wba[di, 2 * dr + q2, :, :, :], g_evac, nrows=32)
                        bsl = gt[:, 0, :]
                        asl = gt[:, 1, :]
                        k.act(bsl, bsl, AF.Exp, [gt_b], [gt_b], scale=-1.0)
                        k.ts("dve", bsl, bsl, 1.0, None, A_, None, [gt_b], [gt_b])
                        k.op("dve", lambda e: e.reciprocal(out=bsl, in_=bsl), [gt_b], [gt_b])
                        k.act(asl, asl, AF.Exp, [gt_b, pr_b], [gt_b], bias=pr[:, 1, dr:dr + 1])
                        k.act(asl, asl, AF.Ln, [gt_b], [gt_b], bias=1.0)
                        k.ts("dve", asl, asl, pr[:, 2, dr:dr + 1], None, M, None, [gt_b, pr_b], [gt_b])
                        for tt_ in range(NTL):
                            c0 = tt_ * 128
                            k.op("dve", lambda e: e.tensor_tensor_scan(out=cs[:, :], data0=one_r[:], data1=asl[:, c0:c0 + 128],
                                                                       initial=0.0, op0=M, op1=A_), [gt_b, pr_b], [cs_b])
                            if dr == 0:
                                k.copy("dve", gcT[:, 0, c0:c0 + 128], cs[:, :], [cs_b], [gcT_b])
                            else:
                                k.tt("dve", gcT[:, 1, c0:c0 + 128], asl[:, c0:c0 + 128], cs[:, :], SB, [gt_b, cs_b], [gcT_b])
                                k.ts("dve", gcT[:, 1, c0:c0 + 128], gcT[:, 1, c0:c0 + 128], cs[:, 127:128], None, A_, None,
                                     [gcT_b, cs_b], [gcT_b])
                        for tt_ in range(NTL):
                            c0 = tt_ * 128
                            pt_, ptb = ps_st.next()
                            k.tr(pt_[:, 0:32], bsl[:, c0:c0 + 128], ident[0:32, 0:32], [gt_b, cst], [ptb])
                            k.tr(pt_[:, 32:64], gcT[:, dr, c0:c0 + 128], ident[0:32, 0:32], [gcT_b, cst], [ptb])
                            k.copy("dve", cols[:, tt_, 2 * dr:2 * dr + 2, :], pt_[:, 0:64].rearrange("p (a h) -> p a h", a=2), [ptb], [cols_b])
            with k.phase():
                sel = k.sb("sel", [32, 32, 128], F32)
                k.dma("sp", sel[:], dn_sel[:, :, :], [], ctab_b)
                qT = k.sb("qT", [128, NT], BF16)
                kT = k.sb("kT", [128, NT], BF16)
                vt = k.sb("vt", [128, NTL, 128], BF16)
                ld_b = k.buf("nld")
                oacc = k.sb("oacc", [128, NTL, 128], F32)
                oacc_b = [k.buf("noacc") for _ in range(NTL)]
                mats = k.sb("mats", [128, 17, 128], F32)
                mb_ = [k.buf("mat%d" % i) for i in range(17)]
                bd16 = k.sb("bd16", [128, 128], F32)
                lmk = k.sb("lmk", [128, 2, 3, 128], F32)
                k.dma("sp", bd16[:], dn_bd[:, :], [], ctab_b)
                k.dma("sp", lmk[:], dn_lm[:, :, :, :], [], ctab_b)
                matb = k.sb("matb", [128, 8, 128], BF16)
                bb_ = [k.buf("matb%d" % i) for i in range(8)]
                S32 = k.sb("S32", [128, 128], F32)
                Sbf = k.sb("Sbf", [128, 128], BF16)
                S_b = k.buf("S")
                sc4 = k.sb("sc4", [128, 8], F32)
                sc_b = k.buf("sc4")
                zt = k.sb("zt", [128, 2, 128], BF16)
                zt_r = Rot(k, "zt", 2)
                yst = k.sb("yst", [128, 2, 128], BF16)
                yst_r = Rot(k, "yst", 2)
                ytk = k.sb("ytk", [128, 128], BF16)
                ytk_b = k.buf("ytk")
                D_, DT_, DN_, N0, M0, R_, NA, MA, NB, MB, U_, EG, T_, P1_, L0, L1, L2 = range(17)
                VB, KBG, WT, VN, QG, PT_, KO, KTK = range(8)

                def mat(i):
                    return mats[:, i, :]

                for h in range(NH):
                    kh = h // 2
                    if h % 2 == 0:
                        k.dma("sp", qT[:], qk_s[kh, :, :], [qk_b], ld_b)
                        k.dma("sp", kT[:], qk_s[16 + kh, :, :], [qk_b], ld_b)
                    k.dma("sp", vt[:, :, :], v_s[h].rearrange("t p e -> p t e"), [vz_b], ld_b)
                    for dr in range(2):
                        order = ([16, 17] + list(range(16))) if dr == 0 else ([17, 16] + list(range(15, -1, -1)))
                        first = True
                        for oi, tt_ in enumerate(order):
                            c0 = tt_ * 128
                            bcol = cols[:, tt_, 2 * dr, h:h + 1]
                            gcol = cols[:, tt_, 2 * dr + 1, h:h + 1]
                            lastpos = 127 if dr == 0 else 0
                            pbc, pbcb = ps_st.next()
                            k.mm(pbc[:, 0:128], sel[:, h, :], gcT[:, dr, c0:c0 + 128], True, True, [ctab_b, gcT_b], [pbcb])
                            k.ts("dve", mat(D_), pbc[:, 0:128], -1.0, gcol, M, A_, [pbcb, cols_b], [mb_[D_]])
                            k.tt("dve", mat(D_), mat(D_), mneg[:, dr, :], A_, [mb_[D_], ctab_b], [mb_[D_]])
                            k.act(mat(D_), mat(D_), AF.Exp, [mb_[D_]], [mb_[D_]])
                            k.tt("dve", mat(DN_), mat(D_), strict[:, dr, :], M, [mb_[D_], ctab_b], [mb_[DN_]])
                            k.ts("dve", mat(DT_), pbc[:, 0:128], gcol, None, SB, None, [pbcb, cols_b], [mb_[DT_]])
                            k.tt("dve", mat(DT_), mat(DT_), mneg[:, 2 + dr, :], A_, [mb_[DT_], ctab_b], [mb_[DT_]])
                            k.act(mat(DT_), mat(DT_), AF.Exp, [mb_[DT_]], [mb_[DT_]])
                            k.act(mat(EG), pbc[:, 0:128], AF.Exp, [pbcb], [mb_[EG]])
                            k.tt("pool", matb[:, QG, :], qT[:, c0:c0 + 128], mat(EG), M, [ld_b, mb_[EG]], [bb_[QG]])
                            k.copy("dve", sc4[:, 0:1], pbc[:, lastpos:lastpos + 1], [pbcb], [sc_b])
                            k.act(sc4[:, 1:2], sc4[:, 0:1], AF.Exp, [sc_b], [sc_b])
                            k.act(sc4[:, 2:3], gcol, AF.Exp, [cols_b, sc_b], [sc_b], scale=-1.0, bias=sc4[:, 0:1])
                            k.act(sc4[:, 3:4], gcol, AF.Exp, [cols_b], [sc_b])
                            k.tt("dve", sc4[:, 4:5], sc4[:, 3:4], bcol, M, [sc_b, cols_b], [sc_b])
                            pkk, pkkb = ps_st.next()
                            k.mm(pkk[:, 0:128], kT[:, c0:c0 + 128], kT[:, c0:c0 + 128], True, True, [ld_b], [pkkb])
                            k.stt(mat(N0), pkk[:, 0:128], bcol, mat(DN_), M, M, [pkkb, cols_b, mb_[DN_]], [mb_[N0]])
                            ptr, ptrb = ps_st.next()
                            k.tr(ptr[:, 0:128], mat(N0), ident[:], [mb_[N0], cst], [ptrb])
                            k.copy("act", mat(M0), ptr[:, 0:128], [ptrb], [mb_[M0]])
                            k.tt("dve", mat(NA), mat(N0), bd16[:], M, [mb_[N0], ctab_b], [mb_[NA]])
                            k.tt("dve", mat(MA), mat(M0), bd16[:], M, [mb_[M0], ctab_b], [mb_[MA]])
                            k.tt("dve", mat(R_), ident[:], mat(MA), SB, [cst, mb_[MA]], [mb_[R_]])
                            cn, cm = NA, MA
                            for sq in range(3):
                                nn, nm = (NB, MB) if sq % 2 == 0 else (NA, MA)
                                pn, pnb = ps_main.next()
                                k.mm(pn[:, 0:128], mat(cm), mat(cn), True, True, [mb_[cm], mb_[cn]], [pnb])
                                if sq < 2:
                                    pm_, pmb_ = ps_main.next()
                                    k.mm(pm_[:, 0:128], mat(cn), mat(cm), True, True, [mb_[cm], mb_[cn]], [pmb_])
                                k.copy("act", mat(nn), pn[:, 0:128], [pnb], [mb_[nn]])
                                if sq < 2:
                                    k.copy("dve", mat(nm), pm_[:, 0:128], [pmb_], [mb_[nm]])
                                pr_, prb_ = ps_y.next()
                                k.mm(pr_[:, 0:128], mat(nn), mat(R_), True, True, [mb_[nn], mb_[R_]], [prb_])
                                k.tt("dve", mat(R_), mat(R_), pr_[:, 0:128], A_, [mb_[R_], prb_], [mb_[R_]])
                                cn, cm = nn, nm
                            for lv in range(3):
                                pt2, pt2b = ps_st.next()
                                k.tr(pt2[:, 0:128], mat(R_), ident[:], [mb_[R_], cst], [pt2b])
                                k.copy("act", mat(T_), pt2[:, 0:128], [pt2b], [mb_[T_]])
                                p1, p1b = ps_main.next()
                                k.mm(p1[:, 0:128], mat(N0), mat(R_), True, True, [mb_[N0], mb_[R_]], [p1b])
                                k.tt("dve", mat(P1_), p1[:, 0:128], lmk[:, 1 - dr, lv, :], M, [p1b, ctab_b], [mb_[P1_]])
                                p2, p2b = ps_y.next()
                                k.mm(p2[:, 0:128], mat(T_), mat(P1_), True, True, [mb_[T_], mb_[P1_]], [p2b])
                                k.tt("dve", mat(R_), mat(R_), p2[:, 0:128], SB, [mb_[R_], p2b], [mb_[R_]])
                            k.ts("dve", mat(MA), vt[:, tt_, :], bcol, None, M, None, [ld_b, cols_b], [mb_[MA]])
                            pu_, pub_ = ps_main.next()
                            k.mm(pu_[:, 0:128], mat(R_), mat(MA), True, True, [mb_[R_], mb_[MA]], [pub_])
                            k.copy("act", mat(U_), pu_[:, 0:128], [pub_], [mb_[U_]])
                            k.tr(psB[:, 0:128], kT[:, c0:c0 + 128], identb[:], [ld_b, cst], [psB_b[0]])
                            k.ts("dve", mat(MB), psB[:, 0:128], sc4[:, 4:5], None, M, None, [psB_b[0], sc_b], [mb_[MB]])
                            k.ts("dve", matb[:, KO, :], psB[:, 0:128], sc4[:, 2:3], None, M, None, [psB_b[0], sc_b], [bb_[KO]])
                            pw_, pwb_ = ps_main.next()
                            k.mm(pw_[:, 0:128], mat(MB), mat(R_), True, True, [mb_[MB], mb_[R_]], [pwb_])
                            k.copy("act", matb[:, WT, :], pw_[:, 0:128], [pwb_], [bb_[WT]])
                            if first:
                                k.copy("dve", matb[:, VN, :], mat(U_), [mb_[U_]], [bb_[VN]])
                            else:
                                pv_, pvb_ = ps_main.next()
                                k.mm(pv_[:, 0:128], matb[:, WT, :], Sbf[:], True, True, [bb_[WT], S_b], [pvb_])
                                k.tt("dve", matb[:, VN, :], mat(U_), pv_[:, 0:128], SB, [mb_[U_], pvb_], [bb_[VN]])
                            pqk, pqkb = ps_st.next()
                            k.mm(pqk[:, 0:128], kT[:, c0:c0 + 128], qT[:, c0:c0 + 128], True, True, [ld_b], [pqkb])
                            k.tt("dve", matb[:, PT_, :], pqk[:, 0:128], mat(DT_), M, [pqkb, mb_[DT_]], [bb_[PT_]])
                            po, pob = ps_y.next()
                            k.mm(po[:, 0:128], matb[:, PT_, :], matb[:, VN, :], True, first, [bb_[PT_], bb_[VN]], [pob])
                            if not first:
                                k.mm(po[:, 0:128], matb[:, QG, :], Sbf[:], False, True, [bb_[QG], S_b], [pob])
                            if dr == 0:
                                k.copy("act", oacc[:, tt_, :], po[:, 0:128], [pob], [oacc_b[tt_]])
                            else:
                                k.tt("dve", oacc[:, tt_, :], oacc[:, tt_, :], po[:, 0:128], A_, [pob, oacc_b[tt_]], [oacc_b[tt_]])
                            if oi < NTL - 1:
                                pd, pdb = ps_y.next()
                                k.mm(pd[:, 0:128], matb[:, KO, :], matb[:, VN, :], True, True, [bb_[KO], bb_[VN]], [pdb])
                                if first:
                                    k.copy("dve", S32[:], pd[:, 0:128], [pdb], [S_b])
                                else:
                                    k.stt(S32[:], S32[:], sc4[:, 1:2], pd[:, 0:128], M, A_, [pdb, sc_b, S_b], [S_b])
                                k.copy("act", Sbf[:], S32[:], [S_b], [S_b])
                            first = False
                    for tt_ in range(NTL):
                        c0 = tt_ * 128
                        a_, ab = tmpA_r.next()
                        k.act(tmpA[:, a_, 0:128], oacc[:, tt_, :], AF.Square, [oacc_b[tt_]], [ab])
                        k.op("dve", lambda e: e.tensor_reduce(out=sc4[:, 5:6], in_=tmpA[:, a_, 0:128], axis=mybir.AxisListType.X, op=A_), [ab], [sc_b])
                        k.ts("dve", sc4[:, 6:7], sc4[:, 5:6], 1.0 / 128.0, None, M, None, [sc_b], [sc_b])
                        k.rsqrt(sc4[:, 6:7], sc4[:, 6:7], DN_EPS, [sc_b], [sc_b])
                        d_, db = tmpD_r.next()
                        k.stt(tmpD[:, d_, 0:128], oacc[:, tt_, :], sc4[:, 6:7], nwt[:], M, M, [oacc_b[tt_], sc_b, ctab_b], [db])
                        z_, zb = zt_r.next()
                        k.dma("sp", zt[:, z_, :], z_s[h, tt_, :, :], [vz_b], zb)
                        k.tt("dve", ytk[:], tmpD[:, d_, 0:128], zt[:, z_, :], M, [db, zb], [ytk_b])
                        k.tr(psB[:, 0:128], ytk[:], identb[:], [ytk_b, cst], [psB_b[0]])
                        y_, yb = yst_r.next()
                        k.copy("act", yst[:, y_, :], psB[:, 0:128], [psB_b[0]], [yb])
                        k.dma("sp", yT_s[h, :, c0:c0 + 128], yst[:, y_, :], [yb], yT_b)
        out_proj(layer, d_wo[di], yT_s, yT_b, last)

    out_b = k.buf("out")

    ri = 0
    di = 0
    for layer in range(depth):
        last = layer == depth - 1
        modulation(layer)
        ffn_phase(layer, 0, 0, layer * 3 + 0, [0, 1, 2, 3, 4])
        if do_mixer:
            if mixers[layer] == "ret":
                retention(layer, ri, last)
                ri += 1
            else:
                deltanet(layer, di, last)
                di += 1
        ffn_phase(layer, 1, 6, layer * 3 + 2, [0, 1, 2, 3] if last else [0, 1, 2, 3, 4], final=last)
    k.finish("sp", [out_b])


def _fm(v):
    return np.ascontiguousarray(v.reshape(KC, 128).T)


def prep_shared(inputs, depth=DEPTH):
    f = np.float32
    sh = {}
    mw = inputs["mod_w"][:depth]
    sh["mod_w"] = np.ascontiguousarray(mw.reshape(depth, KC, 128, 36, 512).transpose(0, 3, 2, 1, 4))
    sh["mod_b"] = np.ascontiguousarray(inputs["mod_b"][:depth].reshape(depth, 144, 128).transpose(0, 2, 1))
    sh["ln_g"] = np.ascontiguousarray(inputs["ln_g"][:depth].reshape(depth * 3 * KC, 128).T)
    sh["ln_b"] = np.ascontiguousarray(inputs["ln_b"][:depth].reshape(depth * 3 * KC, 128).T)
    w_in = inputs["ffn_w_in"][:depth].reshape(depth * 2, KC, 128, 2, FJ, 128)
    sh["ffn_w_in"] = np.ascontiguousarray(w_in.transpose(0, 4, 2, 3, 1, 5))
    w_out = inputs["ffn_w_out"][:depth].reshape(depth * 2, FJ, 128, KC, 128)
    sh["ffn_w_out"] = np.ascontiguousarray(w_out.transpose(0, 3, 2, 1, 4))
    sh["ident"] = np.eye(128, dtype=f)
    return sh


def prep_ret(inputs, n_ret):
    f = np.float32
    sh = {}
    W = inputs["ret_w_in"][:n_ret]
    qk = W[:, :, 0:4096].reshape(n_ret, KC, 128, 2, RH, 128, 2)
    sh["ret_wqk"] = np.ascontiguousarray(qk.transpose(0, 3, 4, 6, 2, 1, 5))
    vg = W[:, :, 4096:12288].reshape(n_ret, KC, 128, 2, RH, 512)
    sh["ret_wvg"] = np.ascontiguousarray(vg.transpose(0, 3, 4, 2, 1, 5))
    wo_ = inputs["ret_w_out"][:n_ret].reshape(n_ret, 32, 128, KC, 128)
    sh["ret_wo"] = np.ascontiguousarray(wo_.transpose(0, 3, 2, 1, 4))
    lg = inputs["ret_log_decay"][:n_ret].reshape(n_ret, 1, 2 * RH)
    sh["ret_lg"] = np.ascontiguousarray(np.broadcast_to(lg, (n_ret, 128, 2 * RH))).astype(f)
    tok = np.arange(SEQ)
    pos_r = (tok // 64).astype(f)
    pos_c = (tok % 64).astype(f)
    inv = (10000.0 ** (-np.arange(0, 128, 2, dtype=f) / 128.0)).astype(f)
    ang = np.concatenate([pos_r[:, None] * inv, pos_c[:, None] * inv], -1).astype(f)
    sh["cosT"] = np.ascontiguousarray(np.cos(ang).T.astype(f))
    sh["sinT"] = np.ascontiguousarray(np.sin(ang).T.astype(f))
    j = np.arange(128)[:, None]
    i = np.arange(128)[None, :]
    sh["tri"] = np.ascontiguousarray(np.stack([(j <= i), (j >= i)], axis=1).astype(f))
    p = np.arange(128, dtype=f)
    sh["idx"] = np.ascontiguousarray(np.stack([-(p + 1), p + 1, p - 128, 128 - p], axis=1).astype(f))
    return sh


def prep_dn(inputs, n_dn):
    f = np.float32
    sh = {}
    W = inputs["dn_w_in"][:n_dn]
    blk = W[:, :, 0:12288].reshape(n_dn, KC, 128, 96, 128)
    sh["dn_w"] = np.ascontiguousarray(blk.transpose(0, 3, 2, 1, 4))
    ba = W[:, :, 12288:12416].reshape(n_dn, KC, 128, 4, 32)
    bap = np.zeros((n_dn, 4, 128, KC, 128), f)
    bap[:, :, :, :, 0:32] = ba.transpose(0, 3, 2, 1, 4)
    sh["dn_wba"] = bap
    cwv = inputs["dn_conv_w"][:n_dn].reshape(n_dn, 5, 64, 128)
    sh["dn_cw"] = np.ascontiguousarray(cwv.transpose(0, 3, 2, 1))
    gp = np.stack([inputs["dn_a_log"][:n_dn], inputs["dn_dt_bias"][:n_dn]], axis=1)
    sh["dn_gp"] = np.ascontiguousarray(gp.transpose(0, 3, 1, 2)).astype(f)
    nw = inputs["dn_norm_w"][:n_dn].reshape(n_dn, 1, 128)
    sh["dn_nw"] = np.ascontiguousarray(np.broadcast_to(nw, (n_dn, 128, 128))).astype(f)
    wo_ = inputs["dn_w_out"][:n_dn].reshape(n_dn, 32, 128, KC, 128)
    sh["dn_wo"] = np.ascontiguousarray(wo_.transpose(0, 3, 2, 1, 4))
    sel = np.zeros((32, 32, 128), f)
    for h in range(32):
        sel[h, h, :] = 1.0
    sh["dn_sel"] = sel
    i = np.arange(128)[:, None]
    j = np.arange(128)[None, :]
    NEG = -30000.0
    low = np.where(i >= j, 0.0, NEG)
    up = np.where(i <= j, 0.0, NEG)
    sh["dn_mneg"] = np.ascontiguousarray(np.stack([low, up, up, low], axis=1).astype(f))
    sh["dn_strict"] = np.ascontiguousarray(np.stack([(i > j), (i < j)], axis=1).astype(f))
    sh["dn_bd"] = np.ascontiguousarray((i // 16 == j // 16).astype(f))
    lm = np.zeros((128, 2, 3, 128), f)
    for lv, s_ in enumerate((16, 32, 64)):
        same = (i // (2 * s_)) == (j // (2 * s_))
        lo = same & ((i % (2 * s_)) >= s_) & ((j % (2 * s_)) < s_)
        lm[:, 0, lv, :] = lo
        lm[:, 1, lv, :] = lo.T
    sh["dn_lm"] = lm
    return sh


def prep_core(inputs, b):
    xT = np.concatenate([inputs["x"][b].T, inputs["ctx"][b].T], axis=1)
    cc = np.stack([_fm(inputs["c"][b]), _fm(inputs["c_ctx"])], axis=-1)
    return {"xT": np.ascontiguousarray(xT, dtype=np.float32), "cc": np.ascontiguousarray(cc, dtype=np.float32)}


def kernel(**inputs):
    inputs = {k_: np.asarray(v) for k_, v in inputs.items()}
    nc = build_program()
    sh = prep_shared(inputs)
    sh.update(prep_ret(inputs, 2))
    sh.update(prep_dn(inputs, 2))
    B = inputs["x"].shape[0]
    in_maps = []
    for b in range(B):
        m = dict(sh)
        m.update(prep_core(inputs, b))
        in_maps.append(m)
    res = run_bass_kernel_spmd(nc, in_maps, core_ids=list(range(B)))
    out = np.stack([np.ascontiguousarray(res.results[b]["outT"].T) for b in range(B)], axis=0)
    return out.astype(np.float32)
```
